# Optimizing a Trainium2 kernel written in Bass

```python
import jax, jax.numpy as jnp
from jax import lax
import numpy as np

D_MODEL = 4096
BATCH = 2
SEQ = 4096
DEPTH = 1

N_META = 16
D_MIX = D_MODEL
HGRN_WIDTH = D_MIX // 2
LRU_WIDTH = D_MIX - HGRN_WIDTH
HGRN_HEAD_DIM = 128
HGRN_HEADS = HGRN_WIDTH // HGRN_HEAD_DIM
LRU_BLOCK = 256
LRU_HEADS = LRU_WIDTH // LRU_BLOCK
CONV_WIDTH = 4
RG_C = 8.0
CHUNK = 64
D_FF = ((8 * D_MODEL // 3 + 255) // 256) * 256
EPS = 1e-6
IN_COLS = 4 * HGRN_WIDTH + 2 * LRU_WIDTH

kernel_name = "hymba_hgrn2_rglru_macaron_block"


def rms_norm(x, gain, group=None):
    shp = x.shape
    xf = x.astype(jnp.float32)
    if group is not None:
        xf = xf.reshape(*shp[:-1], shp[-1] // group, group)
    y = xf * lax.rsqrt(jnp.mean(xf * xf, axis=-1, keepdims=True) + EPS)
    y = y.reshape(shp) * gain.astype(jnp.float32)
    return y.astype(x.dtype)


def swiglu(u, w_gate, w_up, w_down):
    return (jax.nn.silu(u @ w_gate) * (u @ w_up)) @ w_down


def _hgrn2_chunk(state, inp):
    q, k, v, log_f = inp
    c = q.shape[2]
    cum = jnp.cumsum(log_f, axis=2)
    o_inter = jnp.einsum('bhck,bhkv->bhcv', q * jnp.exp(cum), state)
    causal = jnp.tril(jnp.ones((c, c), dtype=bool))[None, None, :, :, None]
    diff = cum[:, :, :, None, :] - cum[:, :, None, :, :]
    decay = jnp.exp(jnp.where(causal, diff, -jnp.inf))
    scores = jnp.einsum('bhik,bhjk,bhijk->bhij', q, k, decay)
    o_intra = jnp.einsum('bhij,bhjv->bhiv', scores, v)
    last = cum[:, :, -1:, :]
    k_dec = k * jnp.exp(last - cum)
    new_state = jnp.exp(last[:, :, 0, :])[..., None] * state + jnp.einsum('bhck,bhcv->bhkv', k_dec, v)
    return new_state, o_inter + o_intra


def hgrn2_group(q, f_logit, i_in, g_out, lb, out_norm):
    b, t, _ = q.shape
    q = jax.nn.silu(q)
    log_f = jnp.log(lb + (1.0 - lb) * jax.nn.sigmoid(f_logit))
    k = (1.0 - lb) * jax.nn.sigmoid(-f_logit)

    def heads(a):
        return a.reshape(b, t, HGRN_HEADS, HGRN_HEAD_DIM).transpose(0, 2, 1, 3)

    q, k, v, log_f = heads(q), heads(k), heads(i_in), heads(log_f)
    state0 = jnp.zeros((b, HGRN_HEADS, HGRN_HEAD_DIM, HGRN_HEAD_DIM), dtype=v.dtype)
    state, o_meta = _hgrn2_chunk(state0, (q[:, :, :N_META], k[:, :, :N_META],
                                          v[:, :, :N_META], log_f[:, :, :N_META]))
    n_chunks = (t - N_META) // CHUNK

    def to_chunks(a):
        return a[:, :, N_META:].reshape(b, HGRN_HEADS, n_chunks, CHUNK, HGRN_HEAD_DIM).transpose(2, 0, 1, 3, 4)

    _, o_real = lax.scan(_hgrn2_chunk, state, (to_chunks(q), to_chunks(k), to_chunks(v), to_chunks(log_f)))
    o_real = o_real.transpose(1, 2, 0, 3, 4).reshape(b, HGRN_HEADS, n_chunks * CHUNK, HGRN_HEAD_DIM)
    o = jnp.concatenate([o_meta, o_real], axis=2).transpose(0, 2, 1, 3).reshape(b, t, HGRN_WIDTH)
    return rms_norm(o, out_norm, group=HGRN_HEAD_DIM) * jax.nn.silu(g_out)


def _linear_combine(left, right):
    a_l, b_l = left
    a_r, b_r = right
    return a_l * a_r, a_r * b_l + b_r


def rglru_group(x_br, g_br, conv_w, conv_b, w_a, b_a, w_x, b_x, lam, out_norm):
    b, t, _ = x_br.shape
    xc = lax.conv_general_dilated(
        x_br, conv_w[:, None, :].astype(x_br.dtype), window_strides=(1,),
        padding=[(CONV_WIDTH - 1, 0)], dimension_numbers=('NWC', 'WIO', 'NWC'),
        feature_group_count=LRU_WIDTH) + conv_b
    xh = xc.reshape(b, t, LRU_HEADS, LRU_BLOCK)
    r = jax.nn.sigmoid(jnp.einsum('bthi,hij->bthj', xh, w_a).reshape(b, t, LRU_WIDTH) + b_a)
    i = jax.nn.sigmoid(jnp.einsum('bthi,hij->bthj', xh, w_x).reshape(b, t, LRU_WIDTH) + b_x)
    log_a = -RG_C * r * jax.nn.softplus(-lam)
    a = jnp.exp(log_a)
    drive = jnp.sqrt(-jnp.expm1(2.0 * log_a)) * (i * xc)
    _, h = lax.associative_scan(_linear_combine, (a, drive), axis=1)
    return rms_norm(h, out_norm) * jax.nn.gelu(g_br)


def token_mixer(u, w_in, lb, hgrn_out_norm, conv_w, conv_b, w_a, b_a, w_x, b_x, lam, lru_out_norm, w_out):
    proj = u @ w_in
    splits = (HGRN_WIDTH, 2 * HGRN_WIDTH, 3 * HGRN_WIDTH, 4 * HGRN_WIDTH, 4 * HGRN_WIDTH + LRU_WIDTH)
    q, f_logit, i_in, g_out, x_br, g_br = jnp.split(proj, splits, axis=-1)
    y_hgrn = hgrn2_group(q, f_logit, i_in, g_out, lb.astype(u.dtype), hgrn_out_norm)
    y_lru = rglru_group(x_br, g_br, conv_w, conv_b, w_a, b_a, w_x, b_x, lam, lru_out_norm)
    return jnp.concatenate([y_hgrn, y_lru], axis=-1) @ w_out


def setup_inputs(seed: int = 0) -> dict:
    key = jax.random.key(seed)
    ks = jax.random.split(key, 26)
    f32 = jnp.float32

    def nrm(k, shape, scale):
        return jax.random.normal(k, shape, f32) * scale

    def gain(k, shape):
        return 1.0 + 0.02 * jax.random.normal(k, shape, f32)

    u = jax.random.uniform(ks[20], (DEPTH, LRU_WIDTH), f32, 0.9, 0.999)
    a_base = u ** (1.0 / RG_C)
    lam = jnp.log(a_base) - jnp.log1p(-a_base)
    return {
        "x": nrm(ks[0], (BATCH, SEQ, D_MODEL), 1.0),
        "meta_tokens": nrm(ks[1], (N_META, D_MODEL), 1.0),
        "ffn1_pre_norm": gain(ks[2], (DEPTH, D_MODEL)),
        "ffn1_w_gate": nrm(ks[3], (DEPTH, D_MODEL, D_FF), D_MODEL ** -0.5),
        "ffn1_w_up": nrm(ks[4], (DEPTH, D_MODEL, D_FF), D_MODEL ** -0.5),
        "ffn1_w_down": nrm(ks[5], (DEPTH, D_FF, D_MODEL), D_FF ** -0.5),
        "ffn1_post_norm": gain(ks[6], (DEPTH, D_MODEL)),
        "mix_pre_norm": gain(ks[7], (DEPTH, D_MODEL)),
        "w_in": nrm(ks[8], (DEPTH, D_MODEL, IN_COLS), D_MODEL ** -0.5),
        "hgrn_lb_logits": nrm(ks[9], (DEPTH + 1, HGRN_WIDTH), 0.5),
        "hgrn_out_norm": gain(ks[10], (DEPTH, HGRN_WIDTH)),
        "lru_conv_w": nrm(ks[11], (DEPTH, CONV_WIDTH, LRU_WIDTH), CONV_WIDTH ** -0.5),
        "lru_conv_b": nrm(ks[12], (DEPTH, LRU_WIDTH), 0.01),
        "lru_w_a": nrm(ks[13], (DEPTH, LRU_HEADS, LRU_BLOCK, LRU_BLOCK), LRU_BLOCK ** -0.5),
        "lru_b_a": nrm(ks[14], (DEPTH, LRU_WIDTH), 0.01),
        "lru_w_x": nrm(ks[15], (DEPTH, LRU_HEADS, LRU_BLOCK, LRU_BLOCK), LRU_BLOCK ** -0.5),
        "lru_b_x": nrm(ks[16], (DEPTH, LRU_WIDTH), 0.01),
        "lru_lambda": lam,
        "lru_out_norm": gain(ks[17], (DEPTH, LRU_WIDTH)),
        "w_out": nrm(ks[18], (DEPTH, D_MIX, D_MODEL), D_MIX ** -0.5),
        "mix_post_norm": gain(ks[19], (DEPTH, D_MODEL)),
        "ffn2_pre_norm": gain(ks[21], (DEPTH, D_MODEL)),
        "ffn2_w_gate": nrm(ks[22], (DEPTH, D_MODEL, D_FF), D_MODEL ** -0.5),
        "ffn2_w_up": nrm(ks[23], (DEPTH, D_MODEL, D_FF), D_MODEL ** -0.5),
        "ffn2_w_down": nrm(ks[24], (DEPTH, D_FF, D_MODEL), D_FF ** -0.5),
        "ffn2_post_norm": gain(ks[25], (DEPTH, D_MODEL)),
    }


def reference(x, meta_tokens, ffn1_pre_norm, ffn1_w_gate, ffn1_w_up, ffn1_w_down, ffn1_post_norm,
              mix_pre_norm, w_in, hgrn_lb_logits, hgrn_out_norm, lru_conv_w, lru_conv_b,
              lru_w_a, lru_b_a, lru_w_x, lru_b_x, lru_lambda, lru_out_norm, w_out, mix_post_norm,
              ffn2_pre_norm, ffn2_w_gate, ffn2_w_up, ffn2_w_down, ffn2_post_norm):
    b = x.shape[0]
    meta = jnp.broadcast_to(meta_tokens[None].astype(x.dtype), (b, N_META, D_MODEL))
    h = jnp.concatenate([meta, x], axis=1)
    lb_all = jnp.cumsum(jax.nn.softmax(hgrn_lb_logits.astype(jnp.float32), axis=0), axis=0)
    for l in range(DEPTH):
        f1 = swiglu(rms_norm(h, ffn1_pre_norm[l]), ffn1_w_gate[l], ffn1_w_up[l], ffn1_w_down[l])
        h = h + 0.5 * rms_norm(f1, ffn1_post_norm[l])
        m = token_mixer(rms_norm(h, mix_pre_norm[l]), w_in[l], lb_all[l], hgrn_out_norm[l],
                        lru_conv_w[l], lru_conv_b[l], lru_w_a[l], lru_b_a[l], lru_w_x[l], lru_b_x[l],
                        lru_lambda[l], lru_out_norm[l], w_out[l])
        h = h + rms_norm(m, mix_post_norm[l])
        f2 = swiglu(rms_norm(h, ffn2_pre_norm[l]), ffn2_w_gate[l], ffn2_w_up[l], ffn2_w_down[l])
        h = h + 0.5 * rms_norm(f2, ffn2_post_norm[l])
    return h[:, N_META:]
```

```python
import contextlib
import numpy as np
import concourse.bass as bass
import concourse.mybir as mybir
from concourse.bass_utils import run_bass_kernel_spmd

F32 = mybir.dt.float32
BF16 = mybir.dt.bfloat16
AF = mybir.ActivationFunctionType
ALU = mybir.AluOpType
EPS = 1e-6

CFG_FULL = dict(D=4096, DFF=11008, NM=16, SEQ=4096, B=2, NSEG=4, SB=512, DB=512, CH=64, GP=8)


class Eng:
    def __init__(self, k, eng, name):
        self.eng = eng
        self.sem = k.es.enter_context(k.nc.semaphore("s_" + name))
        self.cnt = 0
        self.seen = {}
        self.last = None
        self.inorder = False

    def wait(self, toks):
        for t in toks:
            if t is None:
                continue
            sem, val = t
            if self.seen.get(id(sem), 0) >= val:
                continue
            if self.inorder and sem is self.sem:
                continue
            self.eng.wait_ge(sem, val)
            self.seen[id(sem)] = val

    def done(self, ins):
        self.cnt += 1
        ins.then_inc(self.sem, 1)
        self.last = (self.sem, self.cnt)
        return self.last


class Buf:
    def __init__(self, t):
        self.t = t
        self.w = {}
        self.r = {}
        self.ds = None

    def __getitem__(self, key):
        return self.t[key]


def _merge(d, tok):
    sem, val = tok
    o = d.get(id(sem))
    if o is None or o[1] < val:
        d[id(sem)] = tok


class Phase:
    def __init__(self, k):
        self.k = k
        self.es = contextlib.ExitStack()
        self.bufs = []

    def __enter__(self):
        self.es.__enter__()
        return self

    def sb(self, name, shape, dt):
        self.k.uid += 1
        t = self.es.enter_context(self.k.nc.sbuf_tensor("%s_%d" % (name, self.k.uid), list(shape), dt))
        b = Buf(t)
        self.bufs.append(b)
        return b

    def __exit__(self, *a):
        self.k.barrier()
        for b in self.bufs:
            if b.ds is not None:
                self.k.free_dma.append(b.ds)
                b.ds = None
        return self.es.__exit__(*a)


class K:
    def __init__(self, nc):
        self.nc = nc
        self.es = contextlib.ExitStack()
        self.uid = 0
        self.pe = Eng(self, nc.tensor, "pe")
        self.pe.inorder = True
        self.act = Eng(self, nc.scalar, "act")
        self.dve = Eng(self, nc.vector, "dve")
        self.pool = Eng(self, nc.gpsimd, "pool")
        self.sp = Eng(self, nc.sync, "sp")
        self.engs = [self.pe, self.act, self.dve, self.pool, self.sp]
        self.dma_recs = []
        self.free_dma = []
        self.psb = []
        self.psi = 0

    def phase(self):
        return Phase(self)

    def dma_sem(self):
        if self.free_dma:
            return self.free_dma.pop()
        sem = self.es.enter_context(self.nc.semaphore("d_%d" % len(self.dma_recs)))
        rec = [sem, 0]
        self.dma_recs.append(rec)
        return rec

    def barrier(self):
        toks = [e.last for e in self.engs if e.last is not None]
        toks += [(r[0], r[1]) for r in self.dma_recs if r[1] > 0]
        for e in self.engs:
            e.wait(toks)

    def deps(self, reads, writes):
        d = []
        for b in reads:
            d.extend(b.w.values())
        for b in writes:
            d.extend(b.w.values())
            d.extend(b.r.values())
        return d

    def commit(self, tok, reads, writes):
        for b in reads:
            _merge(b.r, tok)
        for b in writes:
            _merge(b.w, tok)
            b.r = {}

    def op(self, eng, fn, reads=(), writes=()):
        eng.wait(self.deps(reads, writes))
        tok = eng.done(fn())
        self.commit(tok, reads, writes)
        return tok

    def dma(self, q, ob, oap, ib, iap, store=False):
        own = ib if store else ob
        if own.ds is None:
            own.ds = self.dma_sem()
        rec = own.ds
        if store:
            d = list(ib.w.values()) + list(ob.r.values())
        else:
            d = self.deps([ib], [ob])
        q.wait(d if store else [t for t in d if t[0] is not rec[0]])
        rec[1] += 16
        q.eng.dma_start(out=oap, in_=iap).then_inc(rec[0], 16)
        tok = (rec[0], rec[1])
        self.commit(tok, [ib], [ob])
        return tok

    def mmg(self, mats, reads, writes):
        self.pe.wait(self.deps(reads, writes))
        n = len(mats)
        ins = None
        for i, (o, l, r) in enumerate(mats):
            ins = self.nc.tensor.matmul(o, lhsT=l, rhs=r, start=(i == 0), stop=(i == n - 1))
        tok = self.pe.done(ins)
        self.commit(tok, reads, writes)
        return tok

    def ps(self):
        b = self.psb[self.psi % len(self.psb)]
        self.psi += 1
        return b


def build(cfg):
    import os as _os
    _STOP = _os.environ.get('STOP', '')
    MAXDESC = int(_os.environ.get('MAXDESC', '512'))
    D, DFF, NM, SEQ, NSEG, SB, DB, CH, GP = (cfg[x] for x in ("D", "DFF", "NM", "SEQ", "NSEG", "SB", "DB", "CH", "GP"))
    KC = D // 128
    FC = DFF // 128
    HW = D // 2
    LW = D - HW
    NH = HW // 128
    NL = LW // 128
    NLB = LW // 256
    SEG = SEQ // NSEG
    NSB = SEQ // SB
    NOWN = SEG // SB
    NT = NM + SEQ
    NDB = D // DB
    NV = 3 * KC + 3 * NH + 9 * NL
    c_g1, c_gm, c_g2 = 0, KC, 2 * KC
    c_lb0 = 3 * KC
    c_lb1 = c_lb0 + NH
    c_hon = c_lb1 + NH
    c_cw = c_hon + NH
    c_cb = c_cw + 4 * NL
    c_ba = c_cb + NL
    c_bx = c_ba + NL
    c_lam = c_bx + NL
    c_lno = c_lam + NL

    nc = bass.Bass("TRN2", target_bir_lowering=False)

    def din(name, shape):
        return Buf(nc.dram_tensor(name, list(shape), F32, kind="ExternalInput").ap())

    xs = din("xs", [NT, D])
    maskd = din("mask", [128, NT])
    w1g = din("w1g", [D, DFF]); w1u = din("w1u", [D, DFF]); w1d = din("w1d", [DFF, D])
    w2g = din("w2g", [D, DFF]); w2u = din("w2u", [D, DFF]); w2d = din("w2d", [DFF, D])
    win = din("win", [D, 4 * HW + 2 * LW]); wout = din("wout", [D, D])
    wad = din("wa", [NLB, 256, 256]); wxd = din("wx", [NLB, 256, 256])
    vfm = din("vfm", [128, NV]); grow = din("grow", [6, 128, D])
    identd = din("ident", [128, 128]); trid = din("tri", [64, 64])
    outd = Buf(nc.dram_tensor("out", [SEG, D], F32, kind="ExternalOutput").ap())
    h1s = Buf(nc.dram_tensor("h1s", [SB + NM, D], F32, kind="Internal").ap())
    h2s = Buf(nc.dram_tensor("h2s", [SB, D], F32, kind="Internal").ap())
    fs = Buf(nc.dram_tensor("fs", [SB + NM, D], F32, kind="Internal").ap())

    k = K(nc)
    pe, act, dve, pool, sp = k.pe, k.act, k.dve, k.pool, k.sp
    V = nc.vector
    A = nc.scalar
    E = k.es.enter_context

    def gsb(name, shape, dt):
        return Buf(E(nc.sbuf_tensor(name, list(shape), dt)))

    for i in range(5):
        k.psb.append(Buf(E(nc.psum_tensor("psg%d" % i, [128, 512], F32))))
    pacc = Buf(E(nc.psum_tensor("pacc", [128, 512], F32)))
    psT = [Buf(E(nc.psum_tensor("psT%d" % i, [128, 1024], BF16))) for i in range(2)]
    psb5 = list(k.psb)
    psb6 = psb5 + [pacc]
    psTi = [0]

    vf = gsb("vf", [128, NV], F32)
    ident = gsb("ident_bf", [128, 128], BF16)
    ones = gsb("ones_f", [128, 128], F32)
    tri = gsb("tri_f", [64, 64], F32)
    lbt = gsb("lbt", [128, NH], F32)
    omlt = gsb("omlt", [128, NH], F32)
    cch = gsb("cch", [128, NL], F32)
    S = [gsb("S%d" % h, [128, 128], F32) for h in range(NH)]
    Sb = [gsb("Sb%d" % h, [128, 128], BF16) for h in range(NH)]
    hst = gsb("hst", [128, NL], F32)
    tail = gsb("tail", [128, NL, 3], F32)
    tmpc = gsb("tmpc", [128, max(NH, NL)], F32)

    k.dma(sp, vf, vf[:], vfm, vfm[:])
    k.dma(pool, ident, ident[:], identd, identd[:])
    k.dma(sp, tri, tri[:], trid, trid[:])
    k.op(dve, lambda: V.memset(ones[:], 1.0), [], [ones])
    for h in range(NH):
        k.op(dve, lambda h=h: V.memset(S[h][:], 0.0), [], [S[h]])
        k.op(dve, lambda h=h: V.memset(Sb[h][:], 0.0), [], [Sb[h]])
    k.op(dve, lambda: V.memset(hst[:], 0.0), [], [hst])
    k.op(dve, lambda: V.memset(tail[:], 0.0), [], [tail])
    k.op(dve, lambda: V.tensor_tensor(out=tmpc[:, :NH], in0=vf[:, c_lb0:c_lb0 + NH], in1=vf[:, c_lb1:c_lb1 + NH], op=ALU.subtract), [vf], [tmpc])
    k.op(act, lambda: A.activation(out=lbt[:], in_=tmpc[:, :NH], func=AF.Sigmoid), [tmpc], [lbt])
    k.op(act, lambda: A.activation(out=omlt[:], in_=tmpc[:, :NH], func=AF.Sigmoid, scale=-1.0), [tmpc], [omlt])
    k.op(act, lambda: A.activation(out=tmpc[:, :NL], in_=vf[:, c_lam:c_lam + NL], func=AF.Exp, scale=-1.0), [vf, omlt], [tmpc])
    k.op(act, lambda: A.activation(out=tmpc[:, :NL], in_=tmpc[:, :NL], func=AF.Ln, bias=1.0), [], [tmpc])
    k.op(dve, lambda: V.tensor_scalar(out=cch[:], in0=tmpc[:, :NL], scalar1=-8.0, scalar2=None, op0=ALU.mult), [tmpc], [cch])
    k.barrier()

    KSTEP = max(1, MAXDESC // 128)

    class WCache:
        def __init__(self, name, ntiles, elems):
            self.buf = Buf(nc.dram_tensor(name, [ntiles, 128, elems], BF16, kind="Internal").ap())
            self.filled = set()
            self.parity = None

    def wload(wt, wb, c0, ncol=128, cache=None, idx=0):
        if cache is not None and idx in cache.filled:
            k.dma(pool, wt, wt[:, :, :ncol], cache.buf, cache.buf.t[idx][:, :KC * ncol].rearrange("p (k n) -> p k n", n=ncol))
            return
        v = wcols(wb, c0, ncol)
        for kc0 in range(0, KC, KSTEP):
            k.dma(pool, wt, wt[:, kc0:kc0 + KSTEP, :ncol], wb, v[:, kc0:kc0 + KSTEP, :])
        if cache is not None and (cache.parity is None or idx % 2 == cache.parity):
            k.dma(pool, cache.buf, cache.buf.t[idx][:, :KC * ncol].rearrange("p (k n) -> p k n", n=ncol), wt, wt[:, :, :ncol], store=True)
            cache.filled.add(idx)

    def wcols(wb, c0, n):
        return wb.t[:, c0:c0 + n].rearrange("(kc p) n -> p kc n", p=128)

    def rstd_from(ph, ssb, n, scale, inv_n, pre=None):
        if pre is None:
            ms = ph.sb("ms", [128, 1], F32)
            rs = ph.sb("rs", [128, 1], F32)
        else:
            ms, rs = pre
        k.op(dve, lambda: V.tensor_scalar(out=ms[:n], in0=ssb[:n, 0:1], scalar1=inv_n, scalar2=EPS, op0=ALU.mult, op1=ALU.add), [ssb], [ms])
        k.op(act, lambda: A.activation(out=ms[:n], in_=ms[:n], func=AF.Sqrt), [], [ms])
        k.op(dve, lambda: V.reciprocal(out=rs[:n], in_=ms[:n]), [ms], [rs])
        if scale != 1.0:
            k.op(dve, lambda: V.tensor_scalar(out=rs[:n], in0=rs[:n], scalar1=scale, scalar2=None, op0=ALU.mult), [], [rs])
        return rs

    def norm_T(src, r0, W, gcol, uT):
        gi = {c_g1: 3, c_gm: 4, c_g2: 5}[gcol]
        with k.phase() as ph:
            xts = [ph.sb("xt", [128, D], F32) for _ in range(2)]
            hss = [ph.sb("hs", [128, D], BF16) for _ in range(2)]
            gpre = ph.sb("gpre", [128, D], F32)
            k.dma(sp, gpre, gpre[:], grow, grow.t[gi])
            ti = 0
            ei = 0
            for t0 in range(0, W, 128):
                n = min(128, W - t0)
                xt = xts[ti % 2]
                hs = hss[ti % 2]
                ti += 1
                k.dma(sp, xt, xt[:n], src, src.t[r0 + t0:r0 + t0 + n, :])
                ss = ph.sb("ss", [128, 1], F32)
                k.op(act, lambda: A.activation(out=hs[:n], in_=xt[:n], func=AF.Square, accum_out=ss[:n, 0:1]), [xt], [hs, ss])
                rs = rstd_from(ph, ss, n, 1.0, 1.0 / D)
                k.op(dve, lambda: V.scalar_tensor_tensor(out=hs[:n], in0=xt[:n], scalar=rs[:n, 0:1], in1=gpre[:n], op0=ALU.mult, op1=ALU.mult), [xt, rs, gpre], [hs])
                for c0 in range(0, KC, 4):
                    pt = psT[psTi[0] % 2]
                    psTi[0] += 1
                    nn = min(4, KC - c0)
                    pe.wait(k.deps([hs, ident], [pt]))
                    ins = None
                    for j in range(nn):
                        ins = nc.tensor.transpose(pt[:, j * 128:j * 128 + n], hs[:n, (c0 + j) * 128:(c0 + j + 1) * 128], ident[:n, :n])
                    k.commit(pe.done(ins), [hs, ident], [pt])
                    ptv = pt[:, :nn * 128].rearrange("p (c n) -> p c n", n=128)[:, :, :n]
                    if ei % 2 == 0:
                        k.op(act, lambda: A.activation(out=uT[:, c0:c0 + nn, t0:t0 + n], in_=ptv, func=AF.Copy), [pt], [uT])
                    else:
                        k.op(dve, lambda: V.tensor_copy(out=uT[:, c0:c0 + nn, t0:t0 + n], in_=ptv), [pt], [uT])
                    ei += 1

    def proj_feat(wt, uT, W, kch, co=0):
        p = k.ps()
        k.mmg([(p[:, :W], wt[:, kc, co:co + 128], uT[:, kc, :W]) for kc in range(kch)], [wt, uT], [p])
        return p

    def gate_up(ph, uT, W, wg_d, wu_d, aT, cg=None, cu=None):
        rg = [ph.sb("wg", [128, KC, 256], BF16) for _ in range(2)]
        ru = [ph.sb("wu", [128, KC, 256], BF16) for _ in range(2)]
        sgs = [ph.sb("sg", [128, 512], F32) for _ in range(2)]
        sgi = [0]
        for bi_, fb in enumerate(range(0, FC, 2)):
            nfc = min(2, FC - fb)
            wg = rg[bi_ % 2]
            wu = ru[bi_ % 2]
            wload(wg, wg_d, fb * 128, nfc * 128, cg, bi_)
            wload(wu, wu_d, fb * 128, nfc * 128, cu, bi_)
            for j in range(nfc):
                fc = fb + j
                for (ca, cb_) in [(a_, min(W, a_ + 512)) for a_ in range(0, W, 512)]:
                    wc = cb_ - ca
                    pg = k.ps()
                    k.mmg([(pg[:, :wc], wg[:, kc, j * 128:(j + 1) * 128], uT[:, kc, ca:cb_]) for kc in range(KC)], [wg, uT], [pg])
                    pu = k.ps()
                    k.mmg([(pu[:, :wc], wu[:, kc, j * 128:(j + 1) * 128], uT[:, kc, ca:cb_]) for kc in range(KC)], [wu, uT], [pu])
                    sg = sgs[sgi[0] % 2]
                    sgi[0] += 1
                    k.op(act, lambda: A.activation(out=sg[:, :wc], in_=pg[:, :wc], func=AF.Silu), [pg], [sg])
                    k.op(dve, lambda: V.tensor_tensor(out=aT[:, fc, ca:cb_], in0=sg[:, :wc], in1=pu[:, :wc], op=ALU.mult), [sg, pu], [aT])

    def down(outer, aT, kch, W, wd_d, scale, cd=None):
        tts = [(t0, min(128, W - t0)) for t0 in range(0, W, 128)]
        ssq, ssd, mss, rss_ = outer
        with k.phase() as ph:
            ring = [ph.sb("wd", [128, GP, DB], BF16) for _ in range(3)]
            sts = [ph.sb("st", [128, DB], F32) for _ in range(3)]
            junk = ph.sb("junkd", [128, DB], BF16)
            ri = 0
            si = 0
            for db in range(NDB):
                banks = [k.ps() for _ in tts]
                pe.wait(k.deps([aT], []))
                lasttok = None
                for g0 in range(0, kch, GP):
                    g = min(GP, kch - g0)
                    wd = ring[ri % 3]
                    ri += 1
                    cidx = db * ((kch + GP - 1) // GP) + g0 // GP
                    if cd is not None and cidx in cd.filled:
                        k.dma(pool, wd, wd[:, :g, :], cd.buf, cd.buf.t[cidx][:, :g * DB].rearrange("p (g n) -> p g n", n=DB))
                    else:
                        for ga in range(0, g, KSTEP):
                            gb_ = min(g, ga + KSTEP)
                            k.dma(pool, wd, wd[:, ga:gb_, :], wd_d,
                                  wd_d.t[(g0 + ga) * 128:(g0 + gb_) * 128, db * DB:(db + 1) * DB].rearrange("(g p) n -> p g n", p=128))
                        if cd is not None and (cd.parity is None or cidx % 2 == cd.parity):
                            k.dma(pool, cd.buf, cd.buf.t[cidx][:, :g * DB].rearrange("p (g n) -> p g n", n=DB), wd, wd[:, :g, :], store=True)
                            cd.filled.add(cidx)
                    pe.wait(k.deps([wd], []))
                    ins = None
                    for gi in range(g):
                        for bi, (t0, n) in enumerate(tts):
                            if g0 + gi == 0:
                                pe.wait(k.deps([], [banks[bi]]))
                            ins = nc.tensor.matmul(banks[bi][:n, :DB], lhsT=aT[:, g0 + gi, t0:t0 + n], rhs=wd[:, gi, :],
                                                   start=(g0 + gi == 0), stop=(g0 + gi == kch - 1))
                    lasttok = pe.done(ins)
                    k.commit(lasttok, [wd], [])
                k.commit(lasttok, [aT], banks)
                for bi, (t0, n) in enumerate(tts):
                    st = sts[si % 3]
                    si += 1
                    k.op(act, lambda: A.activation(out=st[:n], in_=banks[bi][:n, :DB], func=AF.Copy), [banks[bi]], [st])
                    k.op(act, lambda: A.activation(out=junk[:n], in_=st[:n], func=AF.Square, accum_out=ssq[bi][:n, db:db + 1]), [st], [junk, ssq[bi]])
                    k.dma(sp, fs, fs.t[t0:t0 + n, db * DB:(db + 1) * DB], st, st[:n], store=True)
        res = []
        for bi, (t0, n) in enumerate(tts):
            ss = ssd[bi]
            k.op(dve, lambda: V.reduce_sum(out=ss[:n], in_=ssq[bi][:n, :], axis=mybir.AxisListType.X), [ssq[bi]], [ss])
            res.append((t0, n, rstd_from(None, ss, n, scale, 1.0 / D, pre=(mss[bi], rss_[bi]))))
        return res

    def down_outs(po, W):
        nt = (W + 127) // 128
        return ([po.sb("ssq", [128, NDB], F32) for _ in range(nt)], [po.sb("ssd", [128, 1], F32) for _ in range(nt)],
                [po.sb("msd", [128, 1], F32) for _ in range(nt)], [po.sb("rsd", [128, 1], F32) for _ in range(nt)])

    def residual(src, r0, rss, grow_i, dst, d0):
        with k.phase() as ph:
            gb = ph.sb("gb", [128, D], F32)
            k.dma(sp, gb, gb[:], grow, grow.t[grow_i])
            fts = [ph.sb("ft", [128, D], F32) for _ in range(2)]
            xts = [ph.sb("xr", [128, D], F32) for _ in range(2)]
            for i, (t0, n, rs) in enumerate(rss):
                ft = fts[i % 2]
                xt = xts[i % 2]
                k.dma(sp, ft, ft[:n], fs, fs.t[t0:t0 + n, :])
                k.dma(sp, xt, xt[:n], src, src.t[r0 + t0:r0 + t0 + n, :])
                k.op(dve, lambda: V.scalar_tensor_tensor(out=ft[:n], in0=ft[:n], scalar=rs[:n, 0:1], in1=gb[:n], op0=ALU.mult, op1=ALU.mult), [rs, gb], [ft])
                k.op(dve, lambda: V.tensor_tensor(out=xt[:n], in0=xt[:n], in1=ft[:n], op=ALU.add), [ft], [xt])
                k.dma(sp, dst, dst.t[d0 + t0:d0 + t0 + n, :], xt, xt[:n], store=True)

    def ffn(src, r0, W, gcol, wg_d, wu_d, wd_d, grow_i, dst, d0, caches=(None, None, None)):
        k.psb = psb6
        with k.phase() as po:
            douts = down_outs(po, W)
            with k.phase() as pa:
                aT = pa.sb("aT", [128, FC, W], BF16)
                with k.phase() as pb:
                    uT = pb.sb("uT", [128, KC, W], BF16)
                    norm_T(src, r0, W, gcol, uT)
                    if _STOP != "norm":
                        with k.phase() as pc:
                            gate_up(pc, uT, W, wg_d, wu_d, aT, caches[0], caches[1])
                rss = down(douts, aT, FC, W, wd_d, 0.5, caches[2]) if _STOP not in ("norm", "gate") else None
            if _STOP not in ("norm", "gate", "down"):
                residual(src, r0, rss, grow_i, dst, d0)
        k.psb = psb5

    c_win = WCache("cwin", 2 * NH + NL, KC * 128)

    def mixer(r0, W, own, hrow=0):
        nch = [(c0, min(CH, W - c0)) for c0 in range(0, W, CH)]
        po = k.phase()
        po.__enter__()
        douts = down_outs(po, W) if own else None
        pm = k.phase()
        pm.__enter__()
        yT = pm.sb("yT", [128, KC, W], BF16) if own else None
        with k.phase() as pu_:
            uT = pu_.sb("uTm", [128, KC, W], BF16)
            norm_T(h1s, hrow, W, c_gm, uT)
            with k.phase() as ph:
                nw = 4 if own else 2
                wr = [[ph.sb("wh", [128, KC, 128], BF16) for _ in range(nw)] for _ in range(2)]
                f32t = lambda nm: ph.sb(nm, [128, W], F32)
                bft = lambda nm: ph.sb(nm, [128, W], BF16)
                nC = len(nch)
                cwm = nch[0][1]

                def mkset():
                    d = {x: f32t(x) for x in ("sig", "Fg", "G", "NG", "kk", "Et")}
                    if own:
                        d.update({x: f32t(x) for x in ("qs", "gs")})
                        d.update({x: bft(x) for x in ("qe", "ke")})
                    d.update({x: bft(x) for x in ("kdb", "vTb")})
                    d["dec"] = ph.sb("dec", [128, nC], F32)
                    d["dtmp"] = ph.sb("dtmp", [128, 1], F32)
                    return d
                T = [mkset(), mkset()]
                sq, rst, onesw = f32t("sq"), f32t("rst"), f32t("onesw")
                k.op(dve, lambda: V.memset(onesw[:], 1.0), [], [onesw])
                vt = ph.sb("vt", [64, nC, 128], BF16)
                kdT = ph.sb("kdT", [64, nC, 128], BF16)
                scT = [ph.sb("scT", [64, 64], BF16) for _ in range(2)]

                def A_pe(hd):
                    t = T[hd % 2]
                    ws = wr[hd % 2]
                    wf, wi = ws[0], ws[1]
                    wload(wf, win, HW + hd * 128, 128, c_win, hd)
                    wload(wi, win, 2 * HW + hd * 128, 128, c_win, NH + hd)
                    if own:
                        wq, wgt = ws[2], ws[3]
                        wload(wq, win, hd * 128)
                        wload(wgt, win, 3 * HW + hd * 128)
                    pf = proj_feat(wf, uT, W, KC)
                    k.op(act, lambda: A.activation(out=t["sig"][:], in_=pf[:, :W], func=AF.Sigmoid), [pf], [t["sig"]])
                    pvT = proj_feat(wi, uT, W, KC)
                    k.op(act, lambda: A.activation(out=t["vTb"][:], in_=pvT[:, :W], func=AF.Copy), [pvT], [t["vTb"]])
                    if own:
                        pq = proj_feat(wq, uT, W, KC)
                        k.op(act, lambda: A.activation(out=t["qs"][:], in_=pq[:, :W], func=AF.Silu), [pq], [t["qs"]])
                        pgt = proj_feat(wgt, uT, W, KC)
                        k.op(act, lambda: A.activation(out=t["gs"][:], in_=pgt[:, :W], func=AF.Silu), [pgt], [t["gs"]])

                def A_rest(hd):
                    t = T[hd % 2]
                    sig, Fg, G, NG, kk, Et, kdb, dec, dtmp = (t[x] for x in ("sig", "Fg", "G", "NG", "kk", "Et", "kdb", "dec", "dtmp"))
                    k.op(dve, lambda: V.tensor_scalar(out=Fg[:], in0=sig[:], scalar1=omlt[:, hd:hd + 1], scalar2=lbt[:, hd:hd + 1], op0=ALU.mult, op1=ALU.add), [sig, omlt, lbt], [Fg])
                    k.op(act, lambda: A.activation(out=sig[:], in_=Fg[:], func=AF.Ln), [Fg], [sig])
                    k.op(dve, lambda: V.tensor_tensor_scan(out=G[:], data0=onesw[:], data1=sig[:], initial=0.0, op0=ALU.mult, op1=ALU.add), [onesw, sig], [G])
                    k.op(dve, lambda: V.tensor_scalar(out=NG[:], in0=G[:], scalar1=-1.0, scalar2=None, op0=ALU.mult), [G], [NG])
                    k.op(dve, lambda: V.tensor_scalar(out=kk[:], in0=Fg[:], scalar1=-1.0, scalar2=1.0, op0=ALU.mult, op1=ALU.add), [Fg], [kk])
                    for ci, (c0, cw) in enumerate(nch):
                        gl = G[:, c0 + cw - 1:c0 + cw]
                        g0 = G[:, c0 - 1:c0] if c0 > 0 else 0.0
                        ng0 = NG[:, c0 - 1:c0] if c0 > 0 else 0.0
                        k.op(act, lambda: A.activation(out=Et[:, c0:c0 + cw], in_=G[:, c0:c0 + cw], func=AF.Exp, scale=-1.0, bias=gl), [G], [Et])
                        k.op(dve, lambda: V.tensor_tensor(out=kdb[:, c0:c0 + cw], in0=kk[:, c0:c0 + cw], in1=Et[:, c0:c0 + cw], op=ALU.mult), [kk, Et], [kdb])
                        if own:
                            qs, qe, ke = t["qs"], t["qe"], t["ke"]
                            k.op(act, lambda: A.activation(out=Et[:, c0:c0 + cw], in_=G[:, c0:c0 + cw], func=AF.Exp, scale=1.0, bias=ng0), [G, NG], [Et])
                            k.op(dve, lambda: V.tensor_tensor(out=qe[:, c0:c0 + cw], in0=qs[:, c0:c0 + cw], in1=Et[:, c0:c0 + cw], op=ALU.mult), [qs, Et], [qe])
                            k.op(act, lambda: A.activation(out=Et[:, c0:c0 + cw], in_=G[:, c0:c0 + cw], func=AF.Exp, scale=-1.0, bias=g0), [G], [Et])
                            k.op(dve, lambda: V.tensor_tensor(out=ke[:, c0:c0 + cw], in0=kk[:, c0:c0 + cw], in1=Et[:, c0:c0 + cw], op=ALU.mult), [kk, Et], [ke])
                        if c0 > 0:
                            k.op(dve, lambda: V.tensor_tensor(out=dtmp[:], in0=gl, in1=G[:, c0 - 1:c0], op=ALU.subtract), [G], [dtmp])
                            k.op(act, lambda: A.activation(out=dec[:, ci:ci + 1], in_=dtmp[:], func=AF.Exp), [dtmp], [dec])
                        else:
                            k.op(act, lambda: A.activation(out=dec[:, ci:ci + 1], in_=gl, func=AF.Exp), [G], [dec])

                def B(hd):
                    t = T[hd % 2]
                    vTb, kdb, dec = t["vTb"], t["kdb"], t["dec"]
                    for (srcb, dstb, eng) in ((vTb, vt, "a"), (kdb, kdT, "d")):
                        pt = psT[psTi[0] % 2]
                        psTi[0] += 1
                        pe.wait(k.deps([srcb, ident], [pt]))
                        ins = None
                        for ci, (c0, cw) in enumerate(nch):
                            ins = nc.tensor.transpose(pt[:cw, ci * 128:(ci + 1) * 128], srcb[:, c0:c0 + cw], ident[:, :])
                        k.commit(pe.done(ins), [srcb, ident], [pt])
                        ptv = pt[:cwm, :nC * 128].rearrange("p (c n) -> p c n", n=128)
                        if eng == "a":
                            k.op(act, lambda: A.activation(out=dstb[:cwm, :, :], in_=ptv, func=AF.Copy), [pt], [dstb])
                        else:
                            k.op(dve, lambda: V.tensor_copy(out=dstb[:cwm, :, :], in_=ptv), [pt], [dstb])
                    if own:
                        qe, ke, gs = t["qe"], t["ke"], t["gs"]
                        k.op(act, lambda: A.activation(out=Sb[hd][:], in_=S[hd][:], func=AF.Copy), [S[hd]], [Sb[hd]])
                    for ci, (c0, cw) in enumerate(nch):
                        if own:
                            psc = k.ps()
                            k.mmg([(psc[:cw, :cw], ke[:, c0:c0 + cw], qe[:, c0:c0 + cw])], [ke, qe], [psc])
                            sc = scT[ci % 2]
                            k.op(dve, lambda: V.tensor_tensor(out=sc[:cw, :cw], in0=psc[:cw, :cw], in1=tri[:cw, :cw], op=ALU.mult), [psc, tri], [sc])
                            k.mmg([(pacc[:, c0:c0 + cw], vt[:cw, ci, :], sc[:cw, :cw]),
                                   (pacc[:, c0:c0 + cw], Sb[hd][:, :], qe[:, c0:c0 + cw])], [vt, sc, Sb[hd], qe], [pacc])
                        pS = k.ps()
                        k.mmg([(pS[:, :128], kdT[:cw, ci, :], vt[:cw, ci, :])], [kdT, vt], [pS])
                        k.op(dve, lambda: V.scalar_tensor_tensor(out=S[hd][:], in0=S[hd][:], scalar=dec[:, ci:ci + 1], in1=pS[:, :128], op0=ALU.mult, op1=ALU.add), [dec, pS], [S[hd]])
                        if own and ci + 1 < nC:
                            k.op(act, lambda: A.activation(out=Sb[hd][:], in_=S[hd][:], func=AF.Copy), [S[hd]], [Sb[hd]])
                    if own:
                        k.op(act, lambda: A.activation(out=sq[:], in_=pacc[:, :W], func=AF.Square), [pacc], [sq])
                        pss = k.ps()
                        k.mmg([(pss[:, :W], ones[:, :], sq[:, :])], [ones, sq], [pss])
                        k.op(dve, lambda: V.tensor_scalar(out=rst[:], in0=pss[:, :W], scalar1=1.0 / 128, scalar2=EPS, op0=ALU.mult, op1=ALU.add), [pss], [rst])
                        k.op(act, lambda: A.activation(out=rst[:], in_=rst[:], func=AF.Sqrt), [], [rst])
                        k.op(dve, lambda: V.reciprocal(out=rst[:], in_=rst[:]), [], [rst])
                        k.op(dve, lambda: V.tensor_tensor(out=sq[:], in0=pacc[:, :W], in1=rst[:], op=ALU.mult), [pacc, rst], [sq])
                        k.op(dve, lambda: V.scalar_tensor_tensor(out=yT[:, hd, :W], in0=sq[:], scalar=vf[:, c_hon + hd:c_hon + hd + 1], in1=gs[:], op0=ALU.mult, op1=ALU.mult), [sq, vf, gs], [yT])

                A_pe(0)
                A_rest(0)
                for hd in range(NH):
                    if hd + 1 < NH:
                        A_pe(hd + 1)
                    B(hd)
                    if hd + 1 < NH:
                        A_rest(hd + 1)
            with k.phase() as ph:
                wr = [[ph.sb("wl", [128, KC, 128], BF16) for _ in range(2)] for _ in range(2)]
                wa = ph.sb("wa_bf", [128, NLB, 2, 256], BF16)
                wx = ph.sb("wx_bf", [128, NLB, 2, 256], BF16)
                for bb in range(NLB):
                    k.dma(pool, wa, wa[:, bb], wad, wad.t[bb].rearrange("(ic p) j -> p ic j", p=128))
                    k.dma(pool, wx, wx[:, bb], wxd, wxd.t[bb].rearrange("(ic p) j -> p ic j", p=128))
                xbufs = [ph.sb("xbuf", [128, 3 + W], F32) for _ in range(2)]
                xc = ph.sb("xc", [128, 2, W], F32)
                xcb = ph.sb("xcb", [128, 2, W], BF16)
                hh = ph.sb("hh", [128, NL if own else 2, W], F32)
                ge = ph.sb("ge", [128, 2, W], F32) if own else None
                mk = ph.sb("mk", [128, W], F32)
                k.dma(sp, mk, mk[:], maskd, maskd.t[:, r0:r0 + W])
                f32t = lambda nm: ph.sb(nm, [128, W], F32)
                rgs, igs, aas, t1s, t2s = ([f32t(x) for _ in range(2)] for x in ("rg", "ig", "aa", "t1", "t2"))
                t2 = t2s[0]

                def lockstep(chains):
                    for i in range(max(len(c) for c in chains)):
                        for c in chains:
                            if i < len(c):
                                c[i]()

                def conv_chain(l):
                    j = l % 2
                    xbuf, t1, t2_ = xbufs[j], t1s[j], t2s[j]
                    ws = wr[j]
                    wxb = ws[0]
                    wload(wxb, win, 4 * HW + l * 128, 128, c_win, 2 * NH + l)
                    if own:
                        wgb = ws[1]
                        wload(wgb, win, 4 * HW + LW + l * 128)
                    st_ = {}
                    cwc = lambda q: vf[:, c_cw + q * NL + l:c_cw + q * NL + l + 1]
                    ops = []
                    ops.append(lambda: st_.__setitem__("px", proj_feat(wxb, uT, W, KC)))
                    ops.append(lambda: k.op(dve, lambda: V.tensor_copy(out=xbuf[:, 0:3], in_=tail[:, l, :]), [tail], [xbuf]))
                    ops.append(lambda: k.op(act, lambda: A.activation(out=xbuf[:, 3:3 + W], in_=st_["px"][:, :W], func=AF.Copy), [st_["px"]], [xbuf]))
                    ops.append(lambda: k.op(dve, lambda: V.tensor_scalar(out=xc[:, j, :], in0=xbuf[:, 0:W], scalar1=cwc(0), scalar2=vf[:, c_cb + l:c_cb + l + 1], op0=ALU.mult, op1=ALU.add), [xbuf, vf], [xc]))
                    for q in range(1, 4):
                        ops.append(lambda q=q: k.op(dve, lambda: V.scalar_tensor_tensor(out=xc[:, j, :], in0=xbuf[:, q:q + W], scalar=cwc(q), in1=xc[:, j, :], op0=ALU.mult, op1=ALU.add), [xbuf, vf], [xc]))
                    ops.append(lambda: k.op(dve, lambda: V.tensor_copy(out=tail[:, l, :], in_=xbuf[:, W:W + 3]), [xbuf], [tail]))
                    ops.append(lambda: k.op(act, lambda: A.activation(out=xcb[:, j, :], in_=xc[:, j, :], func=AF.Copy), [xc], [xcb]))
                    if own:
                        ops.append(lambda: st_.__setitem__("pgb", proj_feat(wgb, uT, W, KC)))
                        ops.append(lambda: k.op(act, lambda: A.activation(out=t1[:], in_=st_["pgb"][:, :W], func=AF.Copy), [st_["pgb"]], [t1]))
                        ops.append(lambda: k.op(dve, lambda: V.tensor_tensor(out=t2_[:], in0=t1[:], in1=t1[:], op=ALU.mult), [t1], [t2_]))
                        ops.append(lambda: k.op(dve, lambda: V.tensor_scalar(out=t2_[:], in0=t2_[:], scalar1=0.044715, scalar2=1.0, op0=ALU.mult, op1=ALU.add), [], [t2_]))
                        ops.append(lambda: k.op(dve, lambda: V.tensor_tensor(out=t2_[:], in0=t2_[:], in1=t1[:], op=ALU.mult), [t1], [t2_]))
                        ops.append(lambda: k.op(act, lambda: A.activation(out=t2_[:], in_=t2_[:], func=AF.Sigmoid, scale=1.5957691216057308), [], [t2_]))
                        ops.append(lambda: k.op(dve, lambda: V.tensor_tensor(out=ge[:, j, :], in0=t1[:], in1=t2_[:], op=ALU.mult), [t1, t2_], [ge]))
                    return ops

                def gate_chain(blk, jc):
                    ll = blk * 2 + jc
                    rg, ig, aa, t1 = rgs[jc], igs[jc], aas[jc], t1s[jc]
                    hi = ll if own else jc
                    st_ = {}
                    ops = []
                    ops.append(lambda: st_.__setitem__("pr", k.ps()))
                    ops.append(lambda: k.mmg([(st_["pr"][:, :W], wa[:, blk, ic, jc * 128:(jc + 1) * 128], xcb[:, ic, :]) for ic in range(2)], [wa, xcb], [st_["pr"]]))
                    ops.append(lambda: k.op(act, lambda: A.activation(out=rg[:], in_=st_["pr"][:, :W], func=AF.Sigmoid, bias=vf[:, c_ba + ll:c_ba + ll + 1]), [st_["pr"], vf], [rg]))
                    ops.append(lambda: st_.__setitem__("pi", k.ps()))
                    ops.append(lambda: k.mmg([(st_["pi"][:, :W], wx[:, blk, ic, jc * 128:(jc + 1) * 128], xcb[:, ic, :]) for ic in range(2)], [wx, xcb], [st_["pi"]]))
                    ops.append(lambda: k.op(act, lambda: A.activation(out=ig[:], in_=st_["pi"][:, :W], func=AF.Sigmoid, bias=vf[:, c_bx + ll:c_bx + ll + 1]), [st_["pi"], vf], [ig]))
                    ops.append(lambda: k.op(act, lambda: A.activation(out=aa[:], in_=rg[:], func=AF.Exp, scale=cch[:, ll:ll + 1]), [rg, cch], [aa]))
                    ops.append(lambda: k.op(dve, lambda: V.tensor_tensor(out=rg[:], in0=aa[:], in1=aa[:], op=ALU.mult), [aa], [rg]))
                    ops.append(lambda: k.op(dve, lambda: V.tensor_scalar(out=rg[:], in0=rg[:], scalar1=-1.0, scalar2=1.0, op0=ALU.mult, op1=ALU.add), [], [rg]))
                    ops.append(lambda: k.op(act, lambda: A.activation(out=rg[:], in_=rg[:], func=AF.Sqrt), [], [rg]))
                    ops.append(lambda: k.op(dve, lambda: V.tensor_tensor(out=ig[:], in0=ig[:], in1=xc[:, jc, :], op=ALU.mult), [xc], [ig]))
                    ops.append(lambda: k.op(dve, lambda: V.tensor_tensor(out=ig[:], in0=ig[:], in1=rg[:], op=ALU.mult), [rg], [ig]))
                    ops.append(lambda: k.op(dve, lambda: V.tensor_tensor(out=ig[:], in0=ig[:], in1=mk[:], op=ALU.mult), [mk], [ig]))
                    ops.append(lambda: k.op(dve, lambda: V.tensor_tensor_scan(out=hh[:, hi, :], data0=aa[:], data1=ig[:], initial=hst[:, ll:ll + 1], op0=ALU.mult, op1=ALU.add), [aa, ig, hst], [hh]))
                    ops.append(lambda: k.op(dve, lambda: V.tensor_copy(out=hst[:, ll:ll + 1], in_=hh[:, hi, W - 1:W]), [hh], [hst]))
                    if own:
                        ops.append(lambda: k.op(act, lambda: A.activation(out=t1[:], in_=hh[:, ll, :], func=AF.Square), [hh], [t1]))
                        ops.append(lambda: k.op(dve, lambda: V.scalar_tensor_tensor(out=hh[:, ll, :], in0=hh[:, ll, :], scalar=vf[:, c_lno + ll:c_lno + ll + 1], in1=ge[:, jc, :], op0=ALU.mult, op1=ALU.mult), [vf, ge], [hh]))

                        def accm():
                            pe.wait(k.deps([ones, t1], [pacc] if ll == 0 else []))
                            ins = nc.tensor.matmul(pacc[:, :W], lhsT=ones[:, :], rhs=t1[:, :], start=(ll == 0), stop=(ll == NL - 1))
                            k.commit(pe.done(ins), [ones, t1], [pacc])
                        ops.append(accm)
                    return ops

                for blk in range(NL // 2):
                    lockstep([conv_chain(2 * blk), conv_chain(2 * blk + 1)])
                    lockstep([gate_chain(blk, 0), gate_chain(blk, 1)])
                if own:
                    k.op(dve, lambda: V.tensor_scalar(out=t2[:], in0=pacc[:, :W], scalar1=1.0 / LW, scalar2=EPS, op0=ALU.mult, op1=ALU.add), [pacc], [t2])
                    k.op(act, lambda: A.activation(out=t2[:], in_=t2[:], func=AF.Sqrt), [], [t2])
                    k.op(dve, lambda: V.reciprocal(out=t2[:], in_=t2[:]), [], [t2])
                    for l in range(NL):
                        k.op(dve, lambda: V.tensor_tensor(out=yT[:, NH + l, :W], in0=hh[:, l, :], in1=t2[:], op=ALU.mult), [hh, t2], [yT])
        rss = down(douts, yT, KC, W, wout, 1.0) if own else None
        pm.__exit__(None, None, None)
        if own:
            residual(h1s, hrow, rss, 1, h2s, 0)
        po.__exit__(None, None, None)

    if _STOP == "setup":
        k.barrier()
        k.es.close()
        return nc
    NBLK = (FC + 1) // 2
    c_ffn1 = (WCache("c1g", NBLK, KC * 256), WCache("c1u", NBLK, KC * 256), WCache("c1d", NDB * ((FC + GP - 1) // GP), GP * DB))
    blocks = [(NM + j * SB, SB, j >= NSB - NOWN, (j - (NSB - NOWN)) * SB) for j in range(NSB)]
    _skip = _os.environ.get("SKIP_MIX") == "1"
    for bi0, (r0, W, own, o0) in enumerate(blocks):
        for c_ in c_ffn1:
            c_.parity = bi0 if bi0 < 2 else None
        if bi0 == 0:
            ffn(xs, 0, NM + SB, c_g1, w1g, w1u, w1d, 0, h1s, 0, c_ffn1)
            if not _skip:
                mixer(0, NM, False, 0)
            hrow = NM
        else:
            ffn(xs, r0, W, c_g1, w1g, w1u, w1d, 0, h1s, 0, c_ffn1)
            hrow = 0
        if not _skip:
            mixer(r0, W, own, hrow)
        if own:
            ffn(h1s if _skip else h2s, hrow if _skip else 0, W, c_g2, w2g, w2u, w2d, 2, outd, o0)
    k.barrier()
    k.es.close()
    return nc


def make_inputs(cfg, x, meta_tokens, ffn1_pre_norm, ffn1_w_gate, ffn1_w_up, ffn1_w_down, ffn1_post_norm,
                mix_pre_norm, w_in, hgrn_lb_logits, hgrn_out_norm, lru_conv_w, lru_conv_b,
                lru_w_a, lru_b_a, lru_w_x, lru_b_x, lru_lambda, lru_out_norm, w_out, mix_post_norm,
                ffn2_pre_norm, ffn2_w_gate, ffn2_w_up, ffn2_w_down, ffn2_post_norm):
    D, NM, SEQ, NSEG, B = cfg["D"], cfg["NM"], cfg["SEQ"], cfg["NSEG"], cfg["B"]
    SEG = SEQ // NSEG
    NT = NM + SEQ
    f = lambda a: np.ascontiguousarray(np.asarray(a, dtype=np.float32))
    fm = lambda v: f(v).reshape(-1, 128).T
    cw = f(lru_conv_w)[0]
    vfm = np.concatenate([fm(ffn1_pre_norm[0]), fm(mix_pre_norm[0]), fm(ffn2_pre_norm[0]),
                          fm(hgrn_lb_logits[0]), fm(hgrn_lb_logits[1]), fm(hgrn_out_norm[0]),
                          fm(cw[0]), fm(cw[1]), fm(cw[2]), fm(cw[3]), fm(lru_conv_b[0]), fm(lru_b_a[0]), fm(lru_b_x[0]),
                          fm(lru_lambda[0]), fm(lru_out_norm[0])], axis=1)
    vfm = np.ascontiguousarray(vfm)
    grow = np.ascontiguousarray(np.stack([np.broadcast_to(f(g[0])[None, :], (128, D))
                                          for g in (ffn1_post_norm, mix_post_norm, ffn2_post_norm, ffn1_pre_norm, mix_pre_norm, ffn2_pre_norm)]))
    shared = {"w1g": f(ffn1_w_gate[0]), "w1u": f(ffn1_w_up[0]), "w1d": f(ffn1_w_down[0]),
              "w2g": f(ffn2_w_gate[0]), "w2u": f(ffn2_w_up[0]), "w2d": f(ffn2_w_down[0]),
              "win": f(w_in[0]), "wout": f(w_out[0]), "wa": f(lru_w_a[0]), "wx": f(lru_w_x[0]),
              "vfm": vfm, "grow": grow, "ident": np.eye(128, dtype=np.float32),
              "tri": np.triu(np.ones((64, 64), np.float32))}
    x = f(x)
    meta = f(meta_tokens)
    maps = []
    for c in range(B * NSEG):
        b, s = divmod(c, NSEG)
        npad = (NSEG - 1 - s) * SEG
        xs = np.zeros((NT, D), np.float32)
        xs[npad:npad + NM] = meta
        xs[npad + NM:] = x[b, :(s + 1) * SEG]
        mask = np.zeros((128, NT), np.float32)
        mask[:, npad:] = 1.0
        m = dict(shared)
        m["xs"] = xs
        m["mask"] = mask
        maps.append(m)
    return maps


def run(cfg, inputs):
    nc = build(cfg)
    maps = make_inputs(cfg, **inputs)
    n = cfg["B"] * cfg["NSEG"]
    res = run_bass_kernel_spmd(nc, maps, core_ids=list(range(n)))
    SEG = cfg["SEQ"] // cfg["NSEG"]
    out = np.zeros((cfg["B"], cfg["SEQ"], cfg["D"]), np.float32)
    for c in range(n):
        b, s = divmod(c, cfg["NSEG"])
        out[b, s * SEG:(s + 1) * SEG] = res.results[c]["out"]
    return out


def kernel(**inputs):
    return run(CFG_FULL, inputs)
```

```python
import contextlib
import numpy as np
import concourse.bass as bass
import concourse.mybir as mybir
from concourse.bass_utils import run_bass_kernel_spmd

F32 = mybir.dt.float32
BF16 = mybir.dt.bfloat16
AF = mybir.ActivationFunctionType
ALU = mybir.AluOpType
EPS = 1e-6

CFG_FULL = dict(D=4096, DFF=11008, NM=16, SEQ=4096, B=2, NSEG=4, SB=512, DB=512, CH=64, GP=8)


class Eng:
    def __init__(self, k, eng, name):
        self.eng = eng
        self.sem = k.es.enter_context(k.nc.semaphore("s_" + name))
        self.cnt = 0
        self.seen = {}
        self.last = None
        self.inorder = False

    def wait(self, toks):
        for t in toks:
            if t is None:
                continue
            sem, val = t
            if self.seen.get(id(sem), 0) >= val:
                continue
            if self.inorder and sem is self.sem:
                continue
            self.eng.wait_ge(sem, val)
            self.seen[id(sem)] = val

    def done(self, ins):
        self.cnt += 1
        ins.then_inc(self.sem, 1)
        self.last = (self.sem, self.cnt)
        return self.last


class Buf:
    def __init__(self, t):
        self.t = t
        self.w = {}
        self.r = {}
        self.ds = None

    def __getitem__(self, key):
        return self.t[key]


def _merge(d, tok):
    sem, val = tok
    o = d.get(id(sem))
    if o is None or o[1] < val:
        d[id(sem)] = tok


class Phase:
    def __init__(self, k):
        self.k = k
        self.es = contextlib.ExitStack()
        self.bufs = []

    def __enter__(self):
        self.es.__enter__()
        return self

    def sb(self, name, shape, dt):
        self.k.uid += 1
        t = self.es.enter_context(self.k.nc.sbuf_tensor("%s_%d" % (name, self.k.uid), list(shape), dt))
        b = Buf(t)
        self.bufs.append(b)
        return b

    def __exit__(self, *a):
        self.k.barrier()
        for b in self.bufs:
            if b.ds is not None:
                self.k.free_dma.append(b.ds)
                b.ds = None
        return self.es.__exit__(*a)


class K:
    def __init__(self, nc):
        self.nc = nc
        self.es = contextlib.ExitStack()
        self.uid = 0
        self.pe = Eng(self, nc.tensor, "pe")
        self.pe.inorder = True
        self.act = Eng(self, nc.scalar, "act")
        self.dve = Eng(self, nc.vector, "dve")
        self.pool = Eng(self, nc.gpsimd, "pool")
        self.sp = Eng(self, nc.sync, "sp")
        self.engs = [self.pe, self.act, self.dve, self.pool, self.sp]
        self.dma_recs = []
        self.free_dma = []
        self.psb = []
        self.psi = 0

    def phase(self):
        return Phase(self)

    def dma_sem(self):
        if self.free_dma:
            return self.free_dma.pop()
        sem = self.es.enter_context(self.nc.semaphore("d_%d" % len(self.dma_recs)))
        rec = [sem, 0]
        self.dma_recs.append(rec)
        return rec

    def barrier(self):
        toks = [e.last for e in self.engs if e.last is not None]
        toks += [(r[0], r[1]) for r in self.dma_recs if r[1] > 0]
        for e in self.engs:
            e.wait(toks)

    def deps(self, reads, writes):
        d = []
        for b in reads:
            d.extend(b.w.values())
        for b in writes:
            d.extend(b.w.values())
            d.extend(b.r.values())
        return d

    def commit(self, tok, reads, writes):
        for b in reads:
            _merge(b.r, tok)
        for b in writes:
            _merge(b.w, tok)
            b.r = {}

    def op(self, eng, fn, reads=(), writes=()):
        eng.wait(self.deps(reads, writes))
        tok = eng.done(fn())
        self.commit(tok, reads, writes)
        return tok

    def dma(self, q, ob, oap, ib, iap, store=False):
        own = ib if store else ob
        if own.ds is None:
            own.ds = self.dma_sem()
        rec = own.ds
        if store:
            d = list(ib.w.values()) + list(ob.r.values())
        else:
            d = self.deps([ib], [ob])
        q.wait(d if store else [t for t in d if t[0] is not rec[0]])
        rec[1] += 16
        q.eng.dma_start(out=oap, in_=iap).then_inc(rec[0], 16)
        tok = (rec[0], rec[1])
        self.commit(tok, [ib], [ob])
        return tok

    def mmg(self, mats, reads, writes):
        self.pe.wait(self.deps(reads, writes))
        n = len(mats)
        ins = None
        for i, (o, l, r) in enumerate(mats):
            ins = self.nc.tensor.matmul(o, lhsT=l, rhs=r, start=(i == 0), stop=(i == n - 1))
        tok = self.pe.done(ins)
        self.commit(tok, reads, writes)
        return tok

    def ps(self):
        b = self.psb[self.psi % len(self.psb)]
        self.psi += 1
        return b


def build(cfg):
    import os as _os
    _STOP = _os.environ.get('STOP', '')
    MAXDESC = int(_os.environ.get('MAXDESC', '512'))
    D, DFF, NM, SEQ, NSEG, SB, DB, CH, GP = (cfg[x] for x in ("D", "DFF", "NM", "SEQ", "NSEG", "SB", "DB", "CH", "GP"))
    KC = D // 128
    FC = DFF // 128
    HW = D // 2
    LW = D - HW
    NH = HW // 128
    NL = LW // 128
    NLB = LW // 256
    SEG = SEQ // NSEG
    NSB = SEQ // SB
    NOWN = SEG // SB
    NT = NM + SEQ
    NDB = D // DB
    NV = 3 * KC + 3 * NH + 9 * NL
    c_g1, c_gm, c_g2 = 0, KC, 2 * KC
    c_lb0 = 3 * KC
    c_lb1 = c_lb0 + NH
    c_hon = c_lb1 + NH
    c_cw = c_hon + NH
    c_cb = c_cw + 4 * NL
    c_ba = c_cb + NL
    c_bx = c_ba + NL
    c_lam = c_bx + NL
    c_lno = c_lam + NL

    nc = bass.Bass("TRN2", target_bir_lowering=False)

    def din(name, shape):
        return Buf(nc.dram_tensor(name, list(shape), F32, kind="ExternalInput").ap())

    xs = din("xs", [NT, D])
    maskd = din("mask", [128, NT])
    w1g = din("w1g", [D, DFF]); w1u = din("w1u", [D, DFF]); w1d = din("w1d", [DFF, D])
    w2g = din("w2g", [D, DFF]); w2u = din("w2u", [D, DFF]); w2d = din("w2d", [DFF, D])
    win = din("win", [D, 4 * HW + 2 * LW]); wout = din("wout", [D, D])
    wad = din("wa", [NLB, 256, 256]); wxd = din("wx", [NLB, 256, 256])
    vfm = din("vfm", [128, NV]); grow = din("grow", [6, 128, D])
    identd = din("ident", [128, 128]); trid = din("tri", [64, 64])
    outd = Buf(nc.dram_tensor("out", [SEG, D], F32, kind="ExternalOutput").ap())
    h1s = Buf(nc.dram_tensor("h1s", [SB + NM, D], F32, kind="Internal").ap())
    h2s = Buf(nc.dram_tensor("h2s", [SB, D], F32, kind="Internal").ap())
    fs = Buf(nc.dram_tensor("fs", [SB + NM, D], F32, kind="Internal").ap())

    k = K(nc)
    pe, act, dve, pool, sp = k.pe, k.act, k.dve, k.pool, k.sp
    V = nc.vector
    A = nc.scalar
    E = k.es.enter_context

    def gsb(name, shape, dt):
        return Buf(E(nc.sbuf_tensor(name, list(shape), dt)))

    for i in range(5):
        k.psb.append(Buf(E(nc.psum_tensor("psg%d" % i, [128, 512], F32))))
    pacc = Buf(E(nc.psum_tensor("pacc", [128, 512], F32)))
    psT = [Buf(E(nc.psum_tensor("psT%d" % i, [128, 1024], BF16))) for i in range(2)]
    psb5 = list(k.psb)
    psb6 = psb5 + [pacc]
    psTi = [0]

    vf = gsb("vf", [128, NV], F32)
    ident = gsb("ident_bf", [128, 128], BF16)
    ones = gsb("ones_f", [128, 128], F32)
    tri = gsb("tri_f", [64, 64], F32)
    lbt = gsb("lbt", [128, NH], F32)
    omlt = gsb("omlt", [128, NH], F32)
    cch = gsb("cch", [128, NL], F32)
    S = [gsb("S%d" % h, [128, 128], F32) for h in range(NH)]
    Sb = [gsb("Sb%d" % h, [128, 128], BF16) for h in range(NH)]
    hst = gsb("hst", [128, NL], F32)
    tail = gsb("tail", [128, NL, 3], F32)
    tmpc = gsb("tmpc", [128, max(NH, NL)], F32)

    k.dma(sp, vf, vf[:], vfm, vfm[:])
    k.dma(pool, ident, ident[:], identd, identd[:])
    k.dma(sp, tri, tri[:], trid, trid[:])
    k.op(dve, lambda: V.memset(ones[:], 1.0), [], [ones])
    for h in range(NH):
        k.op(dve, lambda h=h: V.memset(S[h][:], 0.0), [], [S[h]])
        k.op(dve, lambda h=h: V.memset(Sb[h][:], 0.0), [], [Sb[h]])
    k.op(dve, lambda: V.memset(hst[:], 0.0), [], [hst])
    k.op(dve, lambda: V.memset(tail[:], 0.0), [], [tail])
    k.op(dve, lambda: V.tensor_tensor(out=tmpc[:, :NH], in0=vf[:, c_lb0:c_lb0 + NH], in1=vf[:, c_lb1:c_lb1 + NH], op=ALU.subtract), [vf], [tmpc])
    k.op(act, lambda: A.activation(out=lbt[:], in_=tmpc[:, :NH], func=AF.Sigmoid), [tmpc], [lbt])
    k.op(act, lambda: A.activation(out=omlt[:], in_=tmpc[:, :NH], func=AF.Sigmoid, scale=-1.0), [tmpc], [omlt])
    k.op(act, lambda: A.activation(out=tmpc[:, :NL], in_=vf[:, c_lam:c_lam + NL], func=AF.Exp, scale=-1.0), [vf, omlt], [tmpc])
    k.op(act, lambda: A.activation(out=tmpc[:, :NL], in_=tmpc[:, :NL], func=AF.Ln, bias=1.0), [], [tmpc])
    k.op(dve, lambda: V.tensor_scalar(out=cch[:], in0=tmpc[:, :NL], scalar1=-8.0, scalar2=None, op0=ALU.mult), [tmpc], [cch])
    k.barrier()

    KSTEP = max(1, MAXDESC // 128)

    class WCache:
        def __init__(self, name, ntiles, elems):
            self.buf = Buf(nc.dram_tensor(name, [ntiles, 128, elems], BF16, kind="Internal").ap())
            self.filled = set()
            self.parity = None

    def wload(wt, wb, c0, ncol=128, cache=None, idx=0):
        if cache is not None and idx in cache.filled:
            k.dma(pool, wt, wt[:, :, :ncol], cache.buf, cache.buf.t[idx][:, :KC * ncol].rearrange("p (k n) -> p k n", n=ncol))
            return
        v = wcols(wb, c0, ncol)
        for kc0 in range(0, KC, KSTEP):
            k.dma(pool, wt, wt[:, kc0:kc0 + KSTEP, :ncol], wb, v[:, kc0:kc0 + KSTEP, :])
        if cache is not None and (cache.parity is None or idx % 2 == cache.parity):
            k.dma(pool, cache.buf, cache.buf.t[idx][:, :KC * ncol].rearrange("p (k n) -> p k n", n=ncol), wt, wt[:, :, :ncol], store=True)
            cache.filled.add(idx)

    def wcols(wb, c0, n):
        return wb.t[:, c0:c0 + n].rearrange("(kc p) n -> p kc n", p=128)

    def rstd_from(ph, ssb, n, scale, inv_n, pre=None):
        if pre is None:
            ms = ph.sb("ms", [128, 1], F32)
            rs = ph.sb("rs", [128, 1], F32)
        else:
            ms, rs = pre
        k.op(dve, lambda: V.tensor_scalar(out=ms[:n], in0=ssb[:n, 0:1], scalar1=inv_n, scalar2=EPS, op0=ALU.mult, op1=ALU.add), [ssb], [ms])
        k.op(act, lambda: A.activation(out=ms[:n], in_=ms[:n], func=AF.Sqrt), [], [ms])
        k.op(dve, lambda: V.reciprocal(out=rs[:n], in_=ms[:n]), [ms], [rs])
        if scale != 1.0:
            k.op(dve, lambda: V.tensor_scalar(out=rs[:n], in0=rs[:n], scalar1=scale, scalar2=None, op0=ALU.mult), [], [rs])
        return rs

    def norm_T(src, r0, W, gcol, uT):
        gi = {c_g1: 3, c_gm: 4, c_g2: 5}[gcol]
        with k.phase() as ph:
            xts = [ph.sb("xt", [128, D], F32) for _ in range(2)]
            hss = [ph.sb("hs", [128, D], BF16) for _ in range(2)]
            gpre = ph.sb("gpre", [128, D], F32)
            k.dma(sp, gpre, gpre[:], grow, grow.t[gi])
            ti = 0
            ei = 0
            for t0 in range(0, W, 128):
                n = min(128, W - t0)
                xt = xts[ti % 2]
                hs = hss[ti % 2]
                ti += 1
                k.dma(sp, xt, xt[:n], src, src.t[r0 + t0:r0 + t0 + n, :])
                ss = ph.sb("ss", [128, 1], F32)
                k.op(act, lambda: A.activation(out=hs[:n], in_=xt[:n], func=AF.Square, accum_out=ss[:n, 0:1]), [xt], [hs, ss])
                rs = rstd_from(ph, ss, n, 1.0, 1.0 / D)
                k.op(dve, lambda: V.scalar_tensor_tensor(out=hs[:n], in0=xt[:n], scalar=rs[:n, 0:1], in1=gpre[:n], op0=ALU.mult, op1=ALU.mult), [xt, rs, gpre], [hs])
                for c0 in range(0, KC, 4):
                    pt = psT[psTi[0] % 2]
                    psTi[0] += 1
                    nn = min(4, KC - c0)
                    pe.wait(k.deps([hs, ident], [pt]))
                    ins = None
                    for j in range(nn):
                        ins = nc.tensor.transpose(pt[:, j * 128:j * 128 + n], hs[:n, (c0 + j) * 128:(c0 + j + 1) * 128], ident[:n, :n])
                    k.commit(pe.done(ins), [hs, ident], [pt])
                    ptv = pt[:, :nn * 128].rearrange("p (c n) -> p c n", n=128)[:, :, :n]
                    if ei % 2 == 0:
                        k.op(act, lambda: A.activation(out=uT[:, c0:c0 + nn, t0:t0 + n], in_=ptv, func=AF.Copy), [pt], [uT])
                    else:
                        k.op(dve, lambda: V.tensor_copy(out=uT[:, c0:c0 + nn, t0:t0 + n], in_=ptv), [pt], [uT])
                    ei += 1

    def proj_feat(wt, uT, W, kch, co=0):
        p = k.ps()
        k.mmg([(p[:, :W], wt[:, kc, co:co + 128], uT[:, kc, :W]) for kc in range(kch)], [wt, uT], [p])
        return p

    def gate_up(ph, uT, W, wg_d, wu_d, aT, cg=None, cu=None):
        rg = [ph.sb("wg", [128, KC, 256], BF16) for _ in range(2)]
        ru = [ph.sb("wu", [128, KC, 256], BF16) for _ in range(2)]
        sgs = [ph.sb("sg", [128, 512], F32) for _ in range(2)]
        sgi = [0]
        for bi_, fb in enumerate(range(0, FC, 2)):
            nfc = min(2, FC - fb)
            wg = rg[bi_ % 2]
            wu = ru[bi_ % 2]
            wload(wg, wg_d, fb * 128, nfc * 128, cg, bi_)
            wload(wu, wu_d, fb * 128, nfc * 128, cu, bi_)
            for j in range(nfc):
                fc = fb + j
                for (ca, cb_) in [(a_, min(W, a_ + 512)) for a_ in range(0, W, 512)]:
                    wc = cb_ - ca
                    pg = k.ps()
                    k.mmg([(pg[:, :wc], wg[:, kc, j * 128:(j + 1) * 128], uT[:, kc, ca:cb_]) for kc in range(KC)], [wg, uT], [pg])
                    pu = k.ps()
                    k.mmg([(pu[:, :wc], wu[:, kc, j * 128:(j + 1) * 128], uT[:, kc, ca:cb_]) for kc in range(KC)], [wu, uT], [pu])
                    sg = sgs[sgi[0] % 2]
                    sgi[0] += 1
                    k.op(act, lambda: A.activation(out=sg[:, :wc], in_=pg[:, :wc], func=AF.Silu), [pg], [sg])
                    k.op(dve, lambda: V.tensor_tensor(out=aT[:, fc, ca:cb_], in0=sg[:, :wc], in1=pu[:, :wc], op=ALU.mult), [sg, pu], [aT])

    def down(outer, aT, kch, W, wd_d, scale, cd=None, grow_i=0):
        tts = [(t0, min(128, W - t0)) for t0 in range(0, W, 128)]
        ssq, ssd, mss, rss_ = outer
        with k.phase() as ph:
            ring = [ph.sb("wd", [128, GP, DB], BF16) for _ in range(3)]
            sts = [ph.sb("st", [128, DB], F32) for _ in range(3)]
            junk = ph.sb("junkd", [128, DB], BF16)
            gbd = ph.sb("gbd", [128, D], F32)
            k.dma(sp, gbd, gbd[:], grow, grow.t[grow_i])
            ri = 0
            si = 0
            for db in range(NDB):
                banks = [k.ps() for _ in tts]
                pe.wait(k.deps([aT], []))
                lasttok = None
                for g0 in range(0, kch, GP):
                    g = min(GP, kch - g0)
                    wd = ring[ri % 3]
                    ri += 1
                    cidx = db * ((kch + GP - 1) // GP) + g0 // GP
                    if cd is not None and cidx in cd.filled:
                        k.dma(pool, wd, wd[:, :g, :], cd.buf, cd.buf.t[cidx][:, :g * DB].rearrange("p (g n) -> p g n", n=DB))
                    else:
                        for ga in range(0, g, KSTEP):
                            gb_ = min(g, ga + KSTEP)
                            k.dma(pool, wd, wd[:, ga:gb_, :], wd_d,
                                  wd_d.t[(g0 + ga) * 128:(g0 + gb_) * 128, db * DB:(db + 1) * DB].rearrange("(g p) n -> p g n", p=128))
                        if cd is not None and (cd.parity is None or cidx % 2 == cd.parity):
                            k.dma(pool, cd.buf, cd.buf.t[cidx][:, :g * DB].rearrange("p (g n) -> p g n", n=DB), wd, wd[:, :g, :], store=True)
                            cd.filled.add(cidx)
                    pe.wait(k.deps([wd], []))
                    ins = None
                    for gi in range(g):
                        for bi, (t0, n) in enumerate(tts):
                            if g0 + gi == 0:
                                pe.wait(k.deps([], [banks[bi]]))
                            ins = nc.tensor.matmul(banks[bi][:n, :DB], lhsT=aT[:, g0 + gi, t0:t0 + n], rhs=wd[:, gi, :],
                                                   start=(g0 + gi == 0), stop=(g0 + gi == kch - 1))
                    lasttok = pe.done(ins)
                    k.commit(lasttok, [wd], [])
                k.commit(lasttok, [aT], banks)
                for bi, (t0, n) in enumerate(tts):
                    st = sts[si % 3]
                    si += 1
                    k.op(act, lambda: A.activation(out=st[:n], in_=banks[bi][:n, :DB], func=AF.Copy), [banks[bi]], [st])
                    k.op(act, lambda: A.activation(out=junk[:n], in_=st[:n], func=AF.Square, accum_out=ssq[bi][:n, db:db + 1]), [st], [junk, ssq[bi]])
                    k.op(dve, lambda: V.tensor_tensor(out=st[:n], in0=st[:n], in1=gbd[:n, db * DB:(db + 1) * DB], op=ALU.mult), [gbd], [st])
                    k.dma(sp, fs, fs.t[t0:t0 + n, db * DB:(db + 1) * DB], st, st[:n], store=True)
        res = []
        for bi, (t0, n) in enumerate(tts):
            ss = ssd[bi]
            k.op(dve, lambda: V.reduce_sum(out=ss[:n], in_=ssq[bi][:n, :], axis=mybir.AxisListType.X), [ssq[bi]], [ss])
            res.append((t0, n, rstd_from(None, ss, n, scale, 1.0 / D, pre=(mss[bi], rss_[bi]))))
        return res

    def down_outs(po, W):
        nt = (W + 127) // 128
        return ([po.sb("ssq", [128, NDB], F32) for _ in range(nt)], [po.sb("ssd", [128, 1], F32) for _ in range(nt)],
                [po.sb("msd", [128, 1], F32) for _ in range(nt)], [po.sb("rsd", [128, 1], F32) for _ in range(nt)])

    def residual(src, r0, rss, grow_i, dst, d0):
        with k.phase() as ph:
            fts = [ph.sb("ft", [128, D], F32) for _ in range(2)]
            xts = [ph.sb("xr", [128, D], F32) for _ in range(2)]
            for i, (t0, n, rs) in enumerate(rss):
                ft = fts[i % 2]
                xt = xts[i % 2]
                k.dma(sp, ft, ft[:n], fs, fs.t[t0:t0 + n, :])
                k.dma(sp, xt, xt[:n], src, src.t[r0 + t0:r0 + t0 + n, :])
                k.op(dve, lambda: V.scalar_tensor_tensor(out=xt[:n], in0=ft[:n], scalar=rs[:n, 0:1], in1=xt[:n], op0=ALU.mult, op1=ALU.add), [rs, ft], [xt])
                k.dma(sp, dst, dst.t[d0 + t0:d0 + t0 + n, :], xt, xt[:n], store=True)

    def ffn(src, r0, W, gcol, wg_d, wu_d, wd_d, grow_i, dst, d0, caches=(None, None, None)):
        k.psb = psb6
        with k.phase() as po:
            douts = down_outs(po, W)
            with k.phase() as pa:
                aT = pa.sb("aT", [128, FC, W], BF16)
                with k.phase() as pb:
                    uT = pb.sb("uT", [128, KC, W], BF16)
                    norm_T(src, r0, W, gcol, uT)
                    if _STOP != "norm":
                        with k.phase() as pc:
                            gate_up(pc, uT, W, wg_d, wu_d, aT, caches[0], caches[1])
                rss = down(douts, aT, FC, W, wd_d, 0.5, caches[2], grow_i) if _STOP not in ("norm", "gate") else None
            if _STOP not in ("norm", "gate", "down"):
                residual(src, r0, rss, grow_i, dst, d0)
        k.psb = psb5

    c_win = WCache("cwin", 2 * NH + NL, KC * 128)

    def mixer(r0, W, own, hrow=0):
        nch = [(c0, min(CH, W - c0)) for c0 in range(0, W, CH)]
        po = k.phase()
        po.__enter__()
        douts = down_outs(po, W) if own else None
        pm = k.phase()
        pm.__enter__()
        yT = pm.sb("yT", [128, KC, W], BF16) if own else None
        with k.phase() as pu_:
            uT = pu_.sb("uTm", [128, KC, W], BF16)
            norm_T(h1s, hrow, W, c_gm, uT)
            with k.phase() as ph:
                nw = 4 if own else 2
                wr = [[ph.sb("wh", [128, KC, 128], BF16) for _ in range(nw)] for _ in range(2)]
                f32t = lambda nm: ph.sb(nm, [128, W], F32)
                bft = lambda nm: ph.sb(nm, [128, W], BF16)
                nC = len(nch)
                cwm = nch[0][1]

                def mkset():
                    d = {x: f32t(x) for x in ("sig", "Fg", "G", "NG", "kk", "Et")}
                    if own:
                        d.update({x: f32t(x) for x in ("qs", "gs")})
                        d.update({x: bft(x) for x in ("qe", "ke")})
                    d.update({x: bft(x) for x in ("kdb", "vTb")})
                    d["dec"] = ph.sb("dec", [128, nC], F32)
                    d["dtmp"] = ph.sb("dtmp", [128, 1], F32)
                    d["d8"] = ph.sb("d8", [128, nC], F32)
                    return d
                T = [mkset(), mkset()]
                sq, rst, onesw = f32t("sq"), f32t("rst"), f32t("onesw")
                k.op(dve, lambda: V.memset(onesw[:], 1.0), [], [onesw])
                vt = ph.sb("vt", [64, nC, 128], BF16)
                kdT = ph.sb("kdT", [64, nC, 128], BF16)
                scT = [ph.sb("scT", [64, 64], BF16) for _ in range(2)]

                def A_pe(hd):
                    t = T[hd % 2]
                    ws = wr[hd % 2]
                    wf, wi = ws[0], ws[1]
                    wload(wf, win, HW + hd * 128, 128, c_win, hd)
                    wload(wi, win, 2 * HW + hd * 128, 128, c_win, NH + hd)
                    if own:
                        wq, wgt = ws[2], ws[3]
                        wload(wq, win, hd * 128)
                        wload(wgt, win, 3 * HW + hd * 128)
                    pf = proj_feat(wf, uT, W, KC)
                    k.op(act, lambda: A.activation(out=t["sig"][:], in_=pf[:, :W], func=AF.Sigmoid), [pf], [t["sig"]])
                    pvT = proj_feat(wi, uT, W, KC)
                    k.op(act, lambda: A.activation(out=t["vTb"][:], in_=pvT[:, :W], func=AF.Copy), [pvT], [t["vTb"]])
                    if own:
                        pq = proj_feat(wq, uT, W, KC)
                        k.op(act, lambda: A.activation(out=t["qs"][:], in_=pq[:, :W], func=AF.Silu), [pq], [t["qs"]])
                        pgt = proj_feat(wgt, uT, W, KC)
                        k.op(act, lambda: A.activation(out=t["gs"][:], in_=pgt[:, :W], func=AF.Silu), [pgt], [t["gs"]])

                def A_rest(hd):
                    t = T[hd % 2]
                    sig, Fg, G, NG, kk, Et, kdb, dec, dtmp = (t[x] for x in ("sig", "Fg", "G", "NG", "kk", "Et", "kdb", "dec", "dtmp"))
                    k.op(dve, lambda: V.tensor_scalar(out=Fg[:], in0=sig[:], scalar1=omlt[:, hd:hd + 1], scalar2=lbt[:, hd:hd + 1], op0=ALU.mult, op1=ALU.add), [sig, omlt, lbt], [Fg])
                    k.op(act, lambda: A.activation(out=sig[:], in_=Fg[:], func=AF.Ln), [Fg], [sig])
                    k.op(dve, lambda: V.tensor_tensor_scan(out=G[:], data0=onesw[:], data1=sig[:], initial=0.0, op0=ALU.mult, op1=ALU.add), [onesw, sig], [G])
                    if own:
                        k.op(dve, lambda: V.tensor_scalar(out=NG[:], in0=G[:], scalar1=-1.0, scalar2=None, op0=ALU.mult), [G], [NG])
                    d8 = t["d8"]
                    k.op(dve, lambda: V.tensor_copy(out=d8[:, 0:1], in_=G[:, cwm - 1:cwm]), [G], [d8])
                    if nC > 1:
                        k.op(dve, lambda: V.tensor_tensor(out=d8[:, 1:nC], in0=G[:, 2 * cwm - 1:W:cwm], in1=G[:, cwm - 1:W - cwm:cwm], op=ALU.subtract), [G], [d8])
                    k.op(act, lambda: A.activation(out=dec[:, :], in_=d8[:, :], func=AF.Exp), [d8], [dec])
                    k.op(dve, lambda: V.tensor_scalar(out=kk[:], in0=Fg[:], scalar1=-1.0, scalar2=1.0, op0=ALU.mult, op1=ALU.add), [Fg], [kk])
                    for ci, (c0, cw) in enumerate(nch):
                        gl = G[:, c0 + cw - 1:c0 + cw]
                        g0 = G[:, c0 - 1:c0] if c0 > 0 else 0.0
                        ng0 = NG[:, c0 - 1:c0] if c0 > 0 else 0.0
                        k.op(act, lambda: A.activation(out=Et[:, c0:c0 + cw], in_=G[:, c0:c0 + cw], func=AF.Exp, scale=-1.0, bias=gl), [G], [Et])
                        k.op(dve, lambda: V.tensor_tensor(out=kdb[:, c0:c0 + cw], in0=kk[:, c0:c0 + cw], in1=Et[:, c0:c0 + cw], op=ALU.mult), [kk, Et], [kdb])
                        if own:
                            qs, qe, ke = t["qs"], t["qe"], t["ke"]
                            k.op(act, lambda: A.activation(out=Et[:, c0:c0 + cw], in_=G[:, c0:c0 + cw], func=AF.Exp, scale=1.0, bias=ng0), [G, NG], [Et])
                            k.op(dve, lambda: V.tensor_tensor(out=qe[:, c0:c0 + cw], in0=qs[:, c0:c0 + cw], in1=Et[:, c0:c0 + cw], op=ALU.mult), [qs, Et], [qe])
                            k.op(act, lambda: A.activation(out=Et[:, c0:c0 + cw], in_=G[:, c0:c0 + cw], func=AF.Exp, scale=-1.0, bias=g0), [G], [Et])
                            k.op(dve, lambda: V.tensor_tensor(out=ke[:, c0:c0 + cw], in0=kk[:, c0:c0 + cw], in1=Et[:, c0:c0 + cw], op=ALU.mult), [kk, Et], [ke])

                def B(hd):
                    t = T[hd % 2]
                    vTb, kdb, dec = t["vTb"], t["kdb"], t["dec"]
                    for (srcb, dstb, eng) in ((vTb, vt, "a"), (kdb, kdT, "d")):
                        pt = psT[psTi[0] % 2]
                        psTi[0] += 1
                        pe.wait(k.deps([srcb, ident], [pt]))
                        ins = None
                        for ci, (c0, cw) in enumerate(nch):
                            ins = nc.tensor.transpose(pt[:cw, ci * 128:(ci + 1) * 128], srcb[:, c0:c0 + cw], ident[:, :])
                        k.commit(pe.done(ins), [srcb, ident], [pt])
                        ptv = pt[:cwm, :nC * 128].rearrange("p (c n) -> p c n", n=128)
                        if eng == "a":
                            k.op(act, lambda: A.activation(out=dstb[:cwm, :, :], in_=ptv, func=AF.Copy), [pt], [dstb])
                        else:
                            k.op(dve, lambda: V.tensor_copy(out=dstb[:cwm, :, :], in_=ptv), [pt], [dstb])
                    if own:
                        qe, ke, gs = t["qe"], t["ke"], t["gs"]
                        k.op(act, lambda: A.activation(out=Sb[hd][:], in_=S[hd][:], func=AF.Copy), [S[hd]], [Sb[hd]])
                    for ci, (c0, cw) in enumerate(nch):
                        if own:
                            psc = k.ps()
                            k.mmg([(psc[:cw, :cw], ke[:, c0:c0 + cw], qe[:, c0:c0 + cw])], [ke, qe], [psc])
                            sc = scT[ci % 2]
                            k.op(dve, lambda: V.tensor_tensor(out=sc[:cw, :cw], in0=psc[:cw, :cw], in1=tri[:cw, :cw], op=ALU.mult), [psc, tri], [sc])
                            k.mmg([(pacc[:, c0:c0 + cw], vt[:cw, ci, :], sc[:cw, :cw]),
                                   (pacc[:, c0:c0 + cw], Sb[hd][:, :], qe[:, c0:c0 + cw])], [vt, sc, Sb[hd], qe], [pacc])
                        pS = k.ps()
                        k.mmg([(pS[:, :128], kdT[:cw, ci, :], vt[:cw, ci, :])], [kdT, vt], [pS])
                        k.op(dve, lambda: V.scalar_tensor_tensor(out=S[hd][:], in0=S[hd][:], scalar=dec[:, ci:ci + 1], in1=pS[:, :128], op0=ALU.mult, op1=ALU.add), [dec, pS], [S[hd]])
                        if own and ci + 1 < nC:
                            k.op(act, lambda: A.activation(out=Sb[hd][:], in_=S[hd][:], func=AF.Copy), [S[hd]], [Sb[hd]])
                    if own:
                        k.op(act, lambda: A.activation(out=sq[:], in_=pacc[:, :W], func=AF.Square), [pacc], [sq])
                        pss = k.ps()
                        k.mmg([(pss[:, :W], ones[:, :], sq[:, :])], [ones, sq], [pss])
                        k.op(dve, lambda: V.tensor_scalar(out=rst[:], in0=pss[:, :W], scalar1=1.0 / 128, scalar2=EPS, op0=ALU.mult, op1=ALU.add), [pss], [rst])
                        k.op(act, lambda: A.activation(out=rst[:], in_=rst[:], func=AF.Sqrt), [], [rst])
                        k.op(dve, lambda: V.reciprocal(out=rst[:], in_=rst[:]), [], [rst])
                        k.op(dve, lambda: V.tensor_tensor(out=sq[:], in0=pacc[:, :W], in1=rst[:], op=ALU.mult), [pacc, rst], [sq])
                        k.op(dve, lambda: V.scalar_tensor_tensor(out=yT[:, hd, :W], in0=sq[:], scalar=vf[:, c_hon + hd:c_hon + hd + 1], in1=gs[:], op0=ALU.mult, op1=ALU.mult), [sq, vf, gs], [yT])

                A_pe(0)
                A_rest(0)
                for hd in range(NH):
                    if hd + 1 < NH:
                        A_pe(hd + 1)
                    B(hd)
                    if hd + 1 < NH:
                        A_rest(hd + 1)
            with k.phase() as ph:
                wr = [[ph.sb("wl", [128, KC, 128], BF16) for _ in range(2)] for _ in range(2)]
                wa = ph.sb("wa_bf", [128, NLB, 2, 256], BF16)
                wx = ph.sb("wx_bf", [128, NLB, 2, 256], BF16)
                for bb in range(NLB):
                    k.dma(pool, wa, wa[:, bb], wad, wad.t[bb].rearrange("(ic p) j -> p ic j", p=128))
                    k.dma(pool, wx, wx[:, bb], wxd, wxd.t[bb].rearrange("(ic p) j -> p ic j", p=128))
                xbufs = [ph.sb("xbuf", [128, 3 + W], F32) for _ in range(2)]
                xc = ph.sb("xc", [128, 2, W], F32)
                xcb = ph.sb("xcb", [128, 2, W], BF16)
                hh = ph.sb("hh", [128, NL if own else 2, W], F32)
                ge = ph.sb("ge", [128, 2, W], F32) if own else None
                mk = ph.sb("mk", [128, W], F32)
                k.dma(sp, mk, mk[:], maskd, maskd.t[:, r0:r0 + W])
                f32t = lambda nm: ph.sb(nm, [128, W], F32)
                rgs, igs, aas, t1s, t2s = ([f32t(x) for _ in range(2)] for x in ("rg", "ig", "aa", "t1", "t2"))
                t2 = t2s[0]

                def lockstep(chains):
                    for i in range(max(len(c) for c in chains)):
                        for c in chains:
                            if i < len(c):
                                c[i]()

                def conv_chain(l):
                    j = l % 2
                    xbuf, t1, t2_ = xbufs[j], t1s[j], t2s[j]
                    ws = wr[j]
                    wxb = ws[0]
                    wload(wxb, win, 4 * HW + l * 128, 128, c_win, 2 * NH + l)
                    if own:
                        wgb = ws[1]
                        wload(wgb, win, 4 * HW + LW + l * 128)
                    st_ = {}
                    cwc = lambda q: vf[:, c_cw + q * NL + l:c_cw + q * NL + l + 1]
                    ops = []
                    ops.append(lambda: st_.__setitem__("px", proj_feat(wxb, uT, W, KC)))
                    ops.append(lambda: k.op(dve, lambda: V.tensor_copy(out=xbuf[:, 0:3], in_=tail[:, l, :]), [tail], [xbuf]))
                    ops.append(lambda: k.op(act, lambda: A.activation(out=xbuf[:, 3:3 + W], in_=st_["px"][:, :W], func=AF.Copy), [st_["px"]], [xbuf]))
                    ops.append(lambda: k.op(dve, lambda: V.tensor_scalar(out=xc[:, j, :], in0=xbuf[:, 0:W], scalar1=cwc(0), scalar2=vf[:, c_cb + l:c_cb + l + 1], op0=ALU.mult, op1=ALU.add), [xbuf, vf], [xc]))
                    for q in range(1, 4):
                        ops.append(lambda q=q: k.op(dve, lambda: V.scalar_tensor_tensor(out=xc[:, j, :], in0=xbuf[:, q:q + W], scalar=cwc(q), in1=xc[:, j, :], op0=ALU.mult, op1=ALU.add), [xbuf, vf], [xc]))
                    ops.append(lambda: k.op(dve, lambda: V.tensor_copy(out=tail[:, l, :], in_=xbuf[:, W:W + 3]), [xbuf], [tail]))
                    ops.append(lambda: k.op(act, lambda: A.activation(out=xcb[:, j, :], in_=xc[:, j, :], func=AF.Copy), [xc], [xcb]))
                    if own:
                        ops.append(lambda: st_.__setitem__("pgb", proj_feat(wgb, uT, W, KC)))
                        ops.append(lambda: k.op(act, lambda: A.activation(out=t1[:], in_=st_["pgb"][:, :W], func=AF.Copy), [st_["pgb"]], [t1]))
                        ops.append(lambda: k.op(dve, lambda: V.tensor_tensor(out=t2_[:], in0=t1[:], in1=t1[:], op=ALU.mult), [t1], [t2_]))
                        ops.append(lambda: k.op(dve, lambda: V.tensor_scalar(out=t2_[:], in0=t2_[:], scalar1=0.044715, scalar2=1.0, op0=ALU.mult, op1=ALU.add), [], [t2_]))
                        ops.append(lambda: k.op(dve, lambda: V.tensor_tensor(out=t2_[:], in0=t2_[:], in1=t1[:], op=ALU.mult), [t1], [t2_]))
                        ops.append(lambda: k.op(act, lambda: A.activation(out=t2_[:], in_=t2_[:], func=AF.Sigmoid, scale=1.5957691216057308), [], [t2_]))
                        ops.append(lambda: k.op(dve, lambda: V.tensor_tensor(out=ge[:, j, :], in0=t1[:], in1=t2_[:], op=ALU.mult), [t1, t2_], [ge]))
                    return ops

                def gate_chain(blk, jc):
                    ll = blk * 2 + jc
                    rg, ig, aa, t1 = rgs[jc], igs[jc], aas[jc], t1s[jc]
                    hi = ll if own else jc
                    st_ = {}
                    ops = []
                    ops.append(lambda: st_.__setitem__("pr", k.ps()))
                    ops.append(lambda: k.mmg([(st_["pr"][:, :W], wa[:, blk, ic, jc * 128:(jc + 1) * 128], xcb[:, ic, :]) for ic in range(2)], [wa, xcb], [st_["pr"]]))
                    ops.append(lambda: k.op(act, lambda: A.activation(out=rg[:], in_=st_["pr"][:, :W], func=AF.Sigmoid, bias=vf[:, c_ba + ll:c_ba + ll + 1]), [st_["pr"], vf], [rg]))
                    ops.append(lambda: st_.__setitem__("pi", k.ps()))
                    ops.append(lambda: k.mmg([(st_["pi"][:, :W], wx[:, blk, ic, jc * 128:(jc + 1) * 128], xcb[:, ic, :]) for ic in range(2)], [wx, xcb], [st_["pi"]]))
                    ops.append(lambda: k.op(act, lambda: A.activation(out=ig[:], in_=st_["pi"][:, :W], func=AF.Sigmoid, bias=vf[:, c_bx + ll:c_bx + ll + 1]), [st_["pi"], vf], [ig]))
                    ops.append(lambda: k.op(act, lambda: A.activation(out=aa[:], in_=rg[:], func=AF.Exp, scale=cch[:, ll:ll + 1]), [rg, cch], [aa]))
                    ops.append(lambda: k.op(dve, lambda: V.tensor_tensor(out=rg[:], in0=aa[:], in1=aa[:], op=ALU.mult), [aa], [rg]))
                    ops.append(lambda: k.op(dve, lambda: V.tensor_scalar(out=rg[:], in0=rg[:], scalar1=-1.0, scalar2=1.0, op0=ALU.mult, op1=ALU.add), [], [rg]))
                    ops.append(lambda: k.op(act, lambda: A.activation(out=rg[:], in_=rg[:], func=AF.Sqrt), [], [rg]))
                    ops.append(lambda: k.op(dve, lambda: V.tensor_tensor(out=ig[:], in0=ig[:], in1=xc[:, jc, :], op=ALU.mult), [xc], [ig]))
                    ops.append(lambda: k.op(dve, lambda: V.tensor_tensor(out=ig[:], in0=ig[:], in1=rg[:], op=ALU.mult), [rg], [ig]))
                    ops.append(lambda: k.op(dve, lambda: V.tensor_tensor(out=ig[:], in0=ig[:], in1=mk[:], op=ALU.mult), [mk], [ig]))
                    ops.append(lambda: k.op(dve, lambda: V.tensor_tensor_scan(out=hh[:, hi, :], data0=aa[:], data1=ig[:], initial=hst[:, ll:ll + 1], op0=ALU.mult, op1=ALU.add), [aa, ig, hst], [hh]))
                    ops.append(lambda: k.op(dve, lambda: V.tensor_copy(out=hst[:, ll:ll + 1], in_=hh[:, hi, W - 1:W]), [hh], [hst]))
                    if own:
                        ops.append(lambda: k.op(act, lambda: A.activation(out=t1[:], in_=hh[:, ll, :], func=AF.Square), [hh], [t1]))
                        ops.append(lambda: k.op(dve, lambda: V.scalar_tensor_tensor(out=hh[:, ll, :], in0=hh[:, ll, :], scalar=vf[:, c_lno + ll:c_lno + ll + 1], in1=ge[:, jc, :], op0=ALU.mult, op1=ALU.mult), [vf, ge], [hh]))

                        def accm():
                            pe.wait(k.deps([ones, t1], [pacc] if ll == 0 else []))
                            ins = nc.tensor.matmul(pacc[:, :W], lhsT=ones[:, :], rhs=t1[:, :], start=(ll == 0), stop=(ll == NL - 1))
                            k.commit(pe.done(ins), [ones, t1], [pacc])
                        ops.append(accm)
                    return ops

                for blk in range(NL // 2):
                    lockstep([conv_chain(2 * blk), conv_chain(2 * blk + 1)])
                    lockstep([gate_chain(blk, 0), gate_chain(blk, 1)])
                if own:
                    k.op(dve, lambda: V.tensor_scalar(out=t2[:], in0=pacc[:, :W], scalar1=1.0 / LW, scalar2=EPS, op0=ALU.mult, op1=ALU.add), [pacc], [t2])
                    k.op(act, lambda: A.activation(out=t2[:], in_=t2[:], func=AF.Sqrt), [], [t2])
                    k.op(dve, lambda: V.reciprocal(out=t2[:], in_=t2[:]), [], [t2])
                    for l in range(NL):
                        k.op(dve, lambda: V.tensor_tensor(out=yT[:, NH + l, :W], in0=hh[:, l, :], in1=t2[:], op=ALU.mult), [hh, t2], [yT])
        rss = down(douts, yT, KC, W, wout, 1.0, None, 1) if own else None
        pm.__exit__(None, None, None)
        if own:
            residual(h1s, hrow, rss, 1, h2s, 0)
        po.__exit__(None, None, None)

    if _STOP == "setup":
        k.barrier()
        k.es.close()
        return nc
    NBLK = (FC + 1) // 2
    c_ffn1 = (WCache("c1g", NBLK, KC * 256), WCache("c1u", NBLK, KC * 256), WCache("c1d", NDB * ((FC + GP - 1) // GP), GP * DB))
    blocks = [(NM + j * SB, SB, j >= NSB - NOWN, (j - (NSB - NOWN)) * SB) for j in range(NSB)]
    _skip = _os.environ.get("SKIP_MIX") == "1"
    for bi0, (r0, W, own, o0) in enumerate(blocks):
        for c_ in c_ffn1:
            c_.parity = bi0 if bi0 < 2 else None
        if bi0 == 0:
            ffn(xs, 0, NM + SB, c_g1, w1g, w1u, w1d, 0, h1s, 0, c_ffn1)
            if not _skip:
                c_win.parity = 2
                mixer(0, NM, False, 0)
                c_win.parity = None
            hrow = NM
        else:
            ffn(xs, r0, W, c_g1, w1g, w1u, w1d, 0, h1s, 0, c_ffn1)
            hrow = 0
        if not _skip:
            mixer(r0, W, own, hrow)
        if own:
            ffn(h1s if _skip else h2s, hrow if _skip else 0, W, c_g2, w2g, w2u, w2d, 2, outd, o0)
    k.barrier()
    k.es.close()
    return nc


def make_inputs(cfg, x, meta_tokens, ffn1_pre_norm, ffn1_w_gate, ffn1_w_up, ffn1_w_down, ffn1_post_norm,
                mix_pre_norm, w_in, hgrn_lb_logits, hgrn_out_norm, lru_conv_w, lru_conv_b,
                lru_w_a, lru_b_a, lru_w_x, lru_b_x, lru_lambda, lru_out_norm, w_out, mix_post_norm,
                ffn2_pre_norm, ffn2_w_gate, ffn2_w_up, ffn2_w_down, ffn2_post_norm):
    D, NM, SEQ, NSEG, B = cfg["D"], cfg["NM"], cfg["SEQ"], cfg["NSEG"], cfg["B"]
    SEG = SEQ // NSEG
    NT = NM + SEQ
    f = lambda a: np.ascontiguousarray(np.asarray(a, dtype=np.float32))
    fm = lambda v: f(v).reshape(-1, 128).T
    cw = f(lru_conv_w)[0]
    vfm = np.concatenate([fm(ffn1_pre_norm[0]), fm(mix_pre_norm[0]), fm(ffn2_pre_norm[0]),
                          fm(hgrn_lb_logits[0]), fm(hgrn_lb_logits[1]), fm(hgrn_out_norm[0]),
                          fm(cw[0]), fm(cw[1]), fm(cw[2]), fm(cw[3]), fm(lru_conv_b[0]), fm(lru_b_a[0]), fm(lru_b_x[0]),
                          fm(lru_lambda[0]), fm(lru_out_norm[0])], axis=1)
    vfm = np.ascontiguousarray(vfm)
    grow = np.ascontiguousarray(np.stack([np.broadcast_to(f(g[0])[None, :], (128, D))
                                          for g in (ffn1_post_norm, mix_post_norm, ffn2_post_norm, ffn1_pre_norm, mix_pre_norm, ffn2_pre_norm)]))
    shared = {"w1g": f(ffn1_w_gate[0]), "w1u": f(ffn1_w_up[0]), "w1d": f(ffn1_w_down[0]),
              "w2g": f(ffn2_w_gate[0]), "w2u": f(ffn2_w_up[0]), "w2d": f(ffn2_w_down[0]),
              "win": f(w_in[0]), "wout": f(w_out[0]), "wa": f(lru_w_a[0]), "wx": f(lru_w_x[0]),
              "vfm": vfm, "grow": grow, "ident": np.eye(128, dtype=np.float32),
              "tri": np.triu(np.ones((64, 64), np.float32))}
    x = f(x)
    meta = f(meta_tokens)
    maps = []
    for c in range(B * NSEG):
        b, s = divmod(c, NSEG)
        npad = (NSEG - 1 - s) * SEG
        xs = np.zeros((NT, D), np.float32)
        xs[npad:npad + NM] = meta
        xs[npad + NM:] = x[b, :(s + 1) * SEG]
        mask = np.zeros((128, NT), np.float32)
        mask[:, npad:] = 1.0
        m = dict(shared)
        m["xs"] = xs
        m["mask"] = mask
        maps.append(m)
    return maps


def run(cfg, inputs):
    nc = build(cfg)
    maps = make_inputs(cfg, **inputs)
    n = cfg["B"] * cfg["NSEG"]
    res = run_bass_kernel_spmd(nc, maps, core_ids=list(range(n)))
    SEG = cfg["SEQ"] // cfg["NSEG"]
    out = np.zeros((cfg["B"], cfg["SEQ"], cfg["D"]), np.float32)
    for c in range(n):
        b, s = divmod(c, cfg["NSEG"])
        out[b, s * SEG:(s + 1) * SEG] = res.results[c]["out"]
    return out


def kernel(**inputs):
    return run(CFG_FULL, inputs)
```

```python
import contextlib
import numpy as np
import concourse.bass as bass
import concourse.mybir as mybir
from concourse.bass_utils import run_bass_kernel_spmd

F32 = mybir.dt.float32
BF16 = mybir.dt.bfloat16
AF = mybir.ActivationFunctionType
ALU = mybir.AluOpType
EPS = 1e-6

CFG_FULL = dict(D=4096, DFF=11008, NM=16, SEQ=4096, B=2, NSEG=4, SB=512, DB=512, CH=64, GP=8)


class Eng:
    def __init__(self, k, eng, name):
        self.eng = eng
        self.sem = k.es.enter_context(k.nc.semaphore("s_" + name))
        self.cnt = 0
        self.seen = {}
        self.last = None
        self.inorder = False

    def wait(self, toks):
        for t in toks:
            if t is None:
                continue
            sem, val = t
            if self.seen.get(id(sem), 0) >= val:
                continue
            if self.inorder and sem is self.sem:
                continue
            self.eng.wait_ge(sem, val)
            self.seen[id(sem)] = val

    def done(self, ins):
        self.cnt += 1
        ins.then_inc(self.sem, 1)
        self.last = (self.sem, self.cnt)
        return self.last


class Buf:
    def __init__(self, t):
        self.t = t
        self.w = {}
        self.r = {}
        self.ds = None

    def __getitem__(self, key):
        return self.t[key]


def _merge(d, tok):
    sem, val = tok
    o = d.get(id(sem))
    if o is None or o[1] < val:
        d[id(sem)] = tok


class Phase:
    def __init__(self, k):
        self.k = k
        self.es = contextlib.ExitStack()
        self.bufs = []

    def __enter__(self):
        self.es.__enter__()
        return self

    def sb(self, name, shape, dt):
        self.k.uid += 1
        t = self.es.enter_context(self.k.nc.sbuf_tensor("%s_%d" % (name, self.k.uid), list(shape), dt))
        b = Buf(t)
        self.bufs.append(b)
        return b

    def __exit__(self, *a):
        self.k.barrier()
        for b in self.bufs:
            if b.ds is not None:
                self.k.free_dma.append(b.ds)
                b.ds = None
        return self.es.__exit__(*a)


class K:
    def __init__(self, nc):
        self.nc = nc
        self.es = contextlib.ExitStack()
        self.uid = 0
        self.pe = Eng(self, nc.tensor, "pe")
        self.pe.inorder = True
        self.act = Eng(self, nc.scalar, "act")
        self.dve = Eng(self, nc.vector, "dve")
        self.pool = Eng(self, nc.gpsimd, "pool")
        self.sp = Eng(self, nc.sync, "sp")
        self.engs = [self.pe, self.act, self.dve, self.pool, self.sp]
        self.dma_recs = []
        self.free_dma = []
        self.psb = []
        self.psi = 0

    def phase(self):
        return Phase(self)

    def dma_sem(self):
        if self.free_dma:
            return self.free_dma.pop()
        sem = self.es.enter_context(self.nc.semaphore("d_%d" % len(self.dma_recs)))
        rec = [sem, 0]
        self.dma_recs.append(rec)
        return rec

    def barrier(self):
        toks = [e.last for e in self.engs if e.last is not None]
        toks += [(r[0], r[1]) for r in self.dma_recs if r[1] > 0]
        for e in self.engs:
            e.wait(toks)

    def deps(self, reads, writes):
        d = []
        for b in reads:
            d.extend(b.w.values())
        for b in writes:
            d.extend(b.w.values())
            d.extend(b.r.values())
        return d

    def commit(self, tok, reads, writes):
        for b in reads:
            _merge(b.r, tok)
        for b in writes:
            _merge(b.w, tok)
            b.r = {}

    def op(self, eng, fn, reads=(), writes=()):
        eng.wait(self.deps(reads, writes))
        tok = eng.done(fn())
        self.commit(tok, reads, writes)
        return tok

    def dma(self, q, ob, oap, ib, iap, store=False):
        own = ib if store else ob
        if own.ds is None:
            own.ds = self.dma_sem()
        rec = own.ds
        if store:
            d = list(ib.w.values()) + list(ob.r.values())
        else:
            d = self.deps([ib], [ob])
        q.wait(d if store else [t for t in d if t[0] is not rec[0]])
        rec[1] += 16
        q.eng.dma_start(out=oap, in_=iap).then_inc(rec[0], 16)
        tok = (rec[0], rec[1])
        self.commit(tok, [ib], [ob])
        return tok

    def mmg(self, mats, reads, writes):
        self.pe.wait(self.deps(reads, writes))
        n = len(mats)
        ins = None
        for i, (o, l, r) in enumerate(mats):
            ins = self.nc.tensor.matmul(o, lhsT=l, rhs=r, start=(i == 0), stop=(i == n - 1))
        tok = self.pe.done(ins)
        self.commit(tok, reads, writes)
        return tok

    def ps(self):
        b = self.psb[self.psi % len(self.psb)]
        self.psi += 1
        return b


def build(cfg):
    import os as _os
    _STOP = _os.environ.get('STOP', '')
    MAXDESC = int(_os.environ.get('MAXDESC', '512'))
    D, DFF, NM, SEQ, NSEG, SB, DB, CH, GP = (cfg[x] for x in ("D", "DFF", "NM", "SEQ", "NSEG", "SB", "DB", "CH", "GP"))
    KC = D // 128
    FC = DFF // 128
    HW = D // 2
    LW = D - HW
    NH = HW // 128
    NL = LW // 128
    NLB = LW // 256
    SEG = SEQ // NSEG
    NSB = SEQ // SB
    NOWN = SEG // SB
    NT = NM + SEQ
    NDB = D // DB
    NV = 3 * KC + 3 * NH + 9 * NL
    c_g1, c_gm, c_g2 = 0, KC, 2 * KC
    c_lb0 = 3 * KC
    c_lb1 = c_lb0 + NH
    c_hon = c_lb1 + NH
    c_cw = c_hon + NH
    c_cb = c_cw + 4 * NL
    c_ba = c_cb + NL
    c_bx = c_ba + NL
    c_lam = c_bx + NL
    c_lno = c_lam + NL

    nc = bass.Bass("TRN2", target_bir_lowering=False)

    def din(name, shape):
        return Buf(nc.dram_tensor(name, list(shape), F32, kind="ExternalInput").ap())

    xs = din("xs", [NT, D])
    maskd = din("mask", [128, NT])
    w1g = din("w1g", [D, DFF]); w1u = din("w1u", [D, DFF]); w1d = din("w1d", [DFF, D])
    w2g = din("w2g", [D, DFF]); w2u = din("w2u", [D, DFF]); w2d = din("w2d", [DFF, D])
    win = din("win", [D, 4 * HW + 2 * LW]); wout = din("wout", [D, D])
    wad = din("wa", [NLB, 256, 256]); wxd = din("wx", [NLB, 256, 256])
    vfm = din("vfm", [128, NV]); grow = din("grow", [6, 128, D])
    identd = din("ident", [128, 128]); trid = din("tri", [64, 64])
    outd = Buf(nc.dram_tensor("out", [SEG, D], F32, kind="ExternalOutput").ap())
    h1s = Buf(nc.dram_tensor("h1s", [SB + NM, D], F32, kind="Internal").ap())
    h2s = Buf(nc.dram_tensor("h2s", [SB, D], F32, kind="Internal").ap())
    fs = Buf(nc.dram_tensor("fs", [SB + NM, D], F32, kind="Internal").ap())

    k = K(nc)
    pe, act, dve, pool, sp = k.pe, k.act, k.dve, k.pool, k.sp
    V = nc.vector
    A = nc.scalar
    E = k.es.enter_context

    def gsb(name, shape, dt):
        return Buf(E(nc.sbuf_tensor(name, list(shape), dt)))

    for i in range(5):
        k.psb.append(Buf(E(nc.psum_tensor("psg%d" % i, [128, 512], F32))))
    pacc = Buf(E(nc.psum_tensor("pacc", [128, 512], F32)))
    psT = [Buf(E(nc.psum_tensor("psT%d" % i, [128, 1024], BF16))) for i in range(2)]
    psb5 = list(k.psb)
    psb6 = psb5 + [pacc]
    psTi = [0]

    vf = gsb("vf", [128, NV], F32)
    ident = gsb("ident_bf", [128, 128], BF16)
    ones = gsb("ones_f", [128, 128], F32)
    tri = gsb("tri_f", [64, 64], F32)
    lbt = gsb("lbt", [128, NH], F32)
    omlt = gsb("omlt", [128, NH], F32)
    cch = gsb("cch", [128, NL], F32)
    S = [gsb("S%d" % h, [128, 128], F32) for h in range(NH)]
    Sb = [gsb("Sb%d" % h, [128, 128], BF16) for h in range(NH)]
    hst = gsb("hst", [128, NL], F32)
    tail = gsb("tail", [128, NL, 3], F32)
    tmpc = gsb("tmpc", [128, max(NH, NL)], F32)

    k.dma(sp, vf, vf[:], vfm, vfm[:])
    k.dma(pool, ident, ident[:], identd, identd[:])
    k.dma(sp, tri, tri[:], trid, trid[:])
    k.op(dve, lambda: V.memset(ones[:], 1.0), [], [ones])
    for h in range(NH):
        k.op(dve, lambda h=h: V.memset(S[h][:], 0.0), [], [S[h]])
        k.op(dve, lambda h=h: V.memset(Sb[h][:], 0.0), [], [Sb[h]])
    k.op(dve, lambda: V.memset(hst[:], 0.0), [], [hst])
    k.op(dve, lambda: V.memset(tail[:], 0.0), [], [tail])
    k.op(dve, lambda: V.tensor_tensor(out=tmpc[:, :NH], in0=vf[:, c_lb0:c_lb0 + NH], in1=vf[:, c_lb1:c_lb1 + NH], op=ALU.subtract), [vf], [tmpc])
    k.op(act, lambda: A.activation(out=lbt[:], in_=tmpc[:, :NH], func=AF.Sigmoid), [tmpc], [lbt])
    k.op(act, lambda: A.activation(out=omlt[:], in_=tmpc[:, :NH], func=AF.Sigmoid, scale=-1.0), [tmpc], [omlt])
    k.op(act, lambda: A.activation(out=tmpc[:, :NL], in_=vf[:, c_lam:c_lam + NL], func=AF.Exp, scale=-1.0), [vf, omlt], [tmpc])
    k.op(act, lambda: A.activation(out=tmpc[:, :NL], in_=tmpc[:, :NL], func=AF.Ln, bias=1.0), [], [tmpc])
    k.op(dve, lambda: V.tensor_scalar(out=cch[:], in0=tmpc[:, :NL], scalar1=-8.0, scalar2=None, op0=ALU.mult), [tmpc], [cch])
    k.barrier()

    KSTEP = max(1, MAXDESC // 128)

    class WCache:
        def __init__(self, name, ntiles, elems):
            self.buf = Buf(nc.dram_tensor(name, [ntiles, 128, elems], BF16, kind="Internal").ap())
            self.filled = set()
            self.parity = None

    def wload(wt, wb, c0, ncol=128, cache=None, idx=0):
        if cache is not None and idx in cache.filled:
            k.dma(pool, wt, wt[:, :, :ncol], cache.buf, cache.buf.t[idx][:, :KC * ncol].rearrange("p (k n) -> p k n", n=ncol))
            return
        v = wcols(wb, c0, ncol)
        for kc0 in range(0, KC, KSTEP):
            k.dma(pool, wt, wt[:, kc0:kc0 + KSTEP, :ncol], wb, v[:, kc0:kc0 + KSTEP, :])
        if cache is not None and (cache.parity is None or idx % 2 == cache.parity):
            k.dma(pool, cache.buf, cache.buf.t[idx][:, :KC * ncol].rearrange("p (k n) -> p k n", n=ncol), wt, wt[:, :, :ncol], store=True)
            cache.filled.add(idx)

    def wcols(wb, c0, n):
        return wb.t[:, c0:c0 + n].rearrange("(kc p) n -> p kc n", p=128)

    def rstd_from(ph, ssb, n, scale, inv_n, pre=None):
        if pre is None:
            ms = ph.sb("ms", [128, 1], F32)
            rs = ph.sb("rs", [128, 1], F32)
        else:
            ms, rs = pre
        k.op(dve, lambda: V.tensor_scalar(out=ms[:n], in0=ssb[:n, 0:1], scalar1=inv_n, scalar2=EPS, op0=ALU.mult, op1=ALU.add), [ssb], [ms])
        k.op(act, lambda: A.activation(out=ms[:n], in_=ms[:n], func=AF.Sqrt), [], [ms])
        k.op(dve, lambda: V.reciprocal(out=rs[:n], in_=ms[:n]), [ms], [rs])
        if scale != 1.0:
            k.op(dve, lambda: V.tensor_scalar(out=rs[:n], in0=rs[:n], scalar1=scale, scalar2=None, op0=ALU.mult), [], [rs])
        return rs

    def norm_T(src, r0, W, gcol, uT):
        gi = {c_g1: 3, c_gm: 4, c_g2: 5}[gcol]
        with k.phase() as ph:
            xts = [ph.sb("xt", [128, D], F32) for _ in range(2)]
            hss = [ph.sb("hs", [128, D], BF16) for _ in range(2)]
            gpre = ph.sb("gpre", [128, D], F32)
            k.dma(sp, gpre, gpre[:], grow, grow.t[gi])
            ti = 0
            ei = 0
            for t0 in range(0, W, 128):
                n = min(128, W - t0)
                xt = xts[ti % 2]
                hs = hss[ti % 2]
                ti += 1
                k.dma(sp, xt, xt[:n], src, src.t[r0 + t0:r0 + t0 + n, :])
                ss = ph.sb("ss", [128, 1], F32)
                k.op(act, lambda: A.activation(out=hs[:n], in_=xt[:n], func=AF.Square, accum_out=ss[:n, 0:1]), [xt], [hs, ss])
                rs = rstd_from(ph, ss, n, 1.0, 1.0 / D)
                k.op(dve, lambda: V.scalar_tensor_tensor(out=hs[:n], in0=xt[:n], scalar=rs[:n, 0:1], in1=gpre[:n], op0=ALU.mult, op1=ALU.mult), [xt, rs, gpre], [hs])
                for c0 in range(0, KC, 4):
                    pt = psT[psTi[0] % 2]
                    psTi[0] += 1
                    nn = min(4, KC - c0)
                    pe.wait(k.deps([hs, ident], [pt]))
                    ins = None
                    for j in range(nn):
                        ins = nc.tensor.transpose(pt[:, j * 128:j * 128 + n], hs[:n, (c0 + j) * 128:(c0 + j + 1) * 128], ident[:n, :n])
                    k.commit(pe.done(ins), [hs, ident], [pt])
                    ptv = pt[:, :nn * 128].rearrange("p (c n) -> p c n", n=128)[:, :, :n]
                    if ei % 2 == 0:
                        k.op(act, lambda: A.activation(out=uT[:, c0:c0 + nn, t0:t0 + n], in_=ptv, func=AF.Copy), [pt], [uT])
                    else:
                        k.op(dve, lambda: V.tensor_copy(out=uT[:, c0:c0 + nn, t0:t0 + n], in_=ptv), [pt], [uT])
                    ei += 1

    def proj_feat(wt, uT, W, kch, co=0):
        p = k.ps()
        k.mmg([(p[:, :W], wt[:, kc, co:co + 128], uT[:, kc, :W]) for kc in range(kch)], [wt, uT], [p])
        return p

    def gate_up(ph, uT, W, wg_d, wu_d, aT, cg=None, cu=None):
        rg = [ph.sb("wg", [128, KC, 256], BF16) for _ in range(2)]
        ru = [ph.sb("wu", [128, KC, 256], BF16) for _ in range(2)]
        sgs = [ph.sb("sg", [128, 512], F32) for _ in range(2)]
        sgi = [0]
        for bi_, fb in enumerate(range(0, FC, 2)):
            nfc = min(2, FC - fb)
            wg = rg[bi_ % 2]
            wu = ru[bi_ % 2]
            wload(wg, wg_d, fb * 128, nfc * 128, cg, bi_)
            wload(wu, wu_d, fb * 128, nfc * 128, cu, bi_)
            for j in range(nfc):
                fc = fb + j
                for (ca, cb_) in [(a_, min(W, a_ + 512)) for a_ in range(0, W, 512)]:
                    wc = cb_ - ca
                    pg = k.ps()
                    k.mmg([(pg[:, :wc], wg[:, kc, j * 128:(j + 1) * 128], uT[:, kc, ca:cb_]) for kc in range(KC)], [wg, uT], [pg])
                    pu = k.ps()
                    k.mmg([(pu[:, :wc], wu[:, kc, j * 128:(j + 1) * 128], uT[:, kc, ca:cb_]) for kc in range(KC)], [wu, uT], [pu])
                    sg = sgs[sgi[0] % 2]
                    sgi[0] += 1
                    k.op(act, lambda: A.activation(out=sg[:, :wc], in_=pg[:, :wc], func=AF.Silu), [pg], [sg])
                    k.op(dve, lambda: V.tensor_tensor(out=aT[:, fc, ca:cb_], in0=sg[:, :wc], in1=pu[:, :wc], op=ALU.mult), [sg, pu], [aT])

    def down(outer, aT, kch, W, wd_d, scale, cd=None, grow_i=0):
        tts = [(t0, min(128, W - t0)) for t0 in range(0, W, 128)]
        ssq, ssd, mss, rss_ = outer
        with k.phase() as ph:
            NRING = 4
            ring = [ph.sb("wd", [128, GP, DB], BF16) for _ in range(NRING)]
            sts = [ph.sb("st", [128, DB], F32) for _ in range(NRING)]
            junk = ph.sb("junkd", [128, DB], BF16)
            gbd = ph.sb("gbd", [128, D], F32)
            k.dma(sp, gbd, gbd[:], grow, grow.t[grow_i])
            ri = 0
            si = 0
            for db in range(NDB):
                banks = [k.ps() for _ in tts]
                pe.wait(k.deps([aT], []))
                lasttok = None
                for g0 in range(0, kch, GP):
                    g = min(GP, kch - g0)
                    wd = ring[ri % NRING]
                    ri += 1
                    cidx = db * ((kch + GP - 1) // GP) + g0 // GP
                    if cd is not None and cidx in cd.filled:
                        k.dma(pool, wd, wd[:, :g, :], cd.buf, cd.buf.t[cidx][:, :g * DB].rearrange("p (g n) -> p g n", n=DB))
                    else:
                        for ga in range(0, g, KSTEP):
                            gb_ = min(g, ga + KSTEP)
                            k.dma(pool, wd, wd[:, ga:gb_, :], wd_d,
                                  wd_d.t[(g0 + ga) * 128:(g0 + gb_) * 128, db * DB:(db + 1) * DB].rearrange("(g p) n -> p g n", p=128))
                        if cd is not None and (cd.parity is None or cidx % 2 == cd.parity):
                            k.dma(pool, cd.buf, cd.buf.t[cidx][:, :g * DB].rearrange("p (g n) -> p g n", n=DB), wd, wd[:, :g, :], store=True)
                            cd.filled.add(cidx)
                    pe.wait(k.deps([wd], []))
                    ins = None
                    for gi in range(g):
                        for bi, (t0, n) in enumerate(tts):
                            if g0 + gi == 0:
                                pe.wait(k.deps([], [banks[bi]]))
                            ins = nc.tensor.matmul(banks[bi][:n, :DB], lhsT=aT[:, g0 + gi, t0:t0 + n], rhs=wd[:, gi, :],
                                                   start=(g0 + gi == 0), stop=(g0 + gi == kch - 1))
                    lasttok = pe.done(ins)
                    k.commit(lasttok, [wd], [])
                k.commit(lasttok, [aT], banks)
                for bi, (t0, n) in enumerate(tts):
                    st = sts[si % NRING]
                    si += 1
                    k.op(act, lambda: A.activation(out=st[:n], in_=banks[bi][:n, :DB], func=AF.Copy), [banks[bi]], [st])
                    k.op(act, lambda: A.activation(out=junk[:n], in_=st[:n], func=AF.Square, accum_out=ssq[bi][:n, db:db + 1]), [st], [junk, ssq[bi]])
                    k.op(dve, lambda: V.tensor_tensor(out=st[:n], in0=st[:n], in1=gbd[:n, db * DB:(db + 1) * DB], op=ALU.mult), [gbd], [st])
                    k.dma(sp, fs, fs.t[t0:t0 + n, db * DB:(db + 1) * DB], st, st[:n], store=True)
        res = []
        for bi, (t0, n) in enumerate(tts):
            ss = ssd[bi]
            k.op(dve, lambda: V.reduce_sum(out=ss[:n], in_=ssq[bi][:n, :], axis=mybir.AxisListType.X), [ssq[bi]], [ss])
            res.append((t0, n, rstd_from(None, ss, n, scale, 1.0 / D, pre=(mss[bi], rss_[bi]))))
        return res

    def down_outs(po, W):
        nt = (W + 127) // 128
        return ([po.sb("ssq", [128, NDB], F32) for _ in range(nt)], [po.sb("ssd", [128, 1], F32) for _ in range(nt)],
                [po.sb("msd", [128, 1], F32) for _ in range(nt)], [po.sb("rsd", [128, 1], F32) for _ in range(nt)])

    def residual(src, r0, rss, grow_i, dst, d0):
        with k.phase() as ph:
            fts = [ph.sb("ft", [128, D], F32) for _ in range(2)]
            xts = [ph.sb("xr", [128, D], F32) for _ in range(2)]
            for i, (t0, n, rs) in enumerate(rss):
                ft = fts[i % 2]
                xt = xts[i % 2]
                k.dma(sp, ft, ft[:n], fs, fs.t[t0:t0 + n, :])
                k.dma(sp, xt, xt[:n], src, src.t[r0 + t0:r0 + t0 + n, :])
                k.op(dve, lambda: V.scalar_tensor_tensor(out=xt[:n], in0=ft[:n], scalar=rs[:n, 0:1], in1=xt[:n], op0=ALU.mult, op1=ALU.add), [rs, ft], [xt])
                k.dma(sp, dst, dst.t[d0 + t0:d0 + t0 + n, :], xt, xt[:n], store=True)

    def ffn(src, r0, W, gcol, wg_d, wu_d, wd_d, grow_i, dst, d0, caches=(None, None, None)):
        k.psb = psb6
        with k.phase() as po:
            douts = down_outs(po, W)
            with k.phase() as pa:
                aT = pa.sb("aT", [128, FC, W], BF16)
                with k.phase() as pb:
                    uT = pb.sb("uT", [128, KC, W], BF16)
                    norm_T(src, r0, W, gcol, uT)
                    if _STOP != "norm":
                        with k.phase() as pc:
                            gate_up(pc, uT, W, wg_d, wu_d, aT, caches[0], caches[1])
                rss = down(douts, aT, FC, W, wd_d, 0.5, caches[2], grow_i) if _STOP not in ("norm", "gate") else None
            if _STOP not in ("norm", "gate", "down"):
                residual(src, r0, rss, grow_i, dst, d0)
        k.psb = psb5

    c_win = WCache("cwin", 2 * NH + NL, KC * 128)

    def mixer(r0, W, own, hrow=0):
        nch = [(c0, min(CH, W - c0)) for c0 in range(0, W, CH)]
        po = k.phase()
        po.__enter__()
        douts = down_outs(po, W) if own else None
        pm = k.phase()
        pm.__enter__()
        yT = pm.sb("yT", [128, KC, W], BF16) if own else None
        with k.phase() as pu_:
            uT = pu_.sb("uTm", [128, KC, W], BF16)
            norm_T(h1s, hrow, W, c_gm, uT)
            with k.phase() as ph:
                nw = 4 if own else 2
                wr = [[ph.sb("wh", [128, KC, 128], BF16) for _ in range(nw)] for _ in range(2)]
                f32t = lambda nm: ph.sb(nm, [128, W], F32)
                bft = lambda nm: ph.sb(nm, [128, W], BF16)
                nC = len(nch)
                cwm = nch[0][1]

                def mkset():
                    d = {x: f32t(x) for x in ("sig", "Fg", "G", "NG", "kk", "Et")}
                    if own:
                        d.update({x: f32t(x) for x in ("qs", "gs")})
                        d.update({x: bft(x) for x in ("qe", "ke")})
                    d.update({x: bft(x) for x in ("kdb", "vTb")})
                    d["dec"] = ph.sb("dec", [128, nC], F32)
                    d["dtmp"] = ph.sb("dtmp", [128, 1], F32)
                    d["d8"] = ph.sb("d8", [128, nC], F32)
                    return d
                T = [mkset(), mkset()]
                sq, rst, onesw = f32t("sq"), f32t("rst"), f32t("onesw")
                k.op(dve, lambda: V.memset(onesw[:], 1.0), [], [onesw])
                vt = ph.sb("vt", [64, nC, 128], BF16)
                kdT = ph.sb("kdT", [64, nC, 128], BF16)
                scT = [ph.sb("scT", [64, 64], BF16) for _ in range(2)]

                def A_pe(hd):
                    t = T[hd % 2]
                    ws = wr[hd % 2]
                    wf, wi = ws[0], ws[1]
                    wload(wf, win, HW + hd * 128, 128, c_win, hd)
                    wload(wi, win, 2 * HW + hd * 128, 128, c_win, NH + hd)
                    if own:
                        wq, wgt = ws[2], ws[3]
                        wload(wq, win, hd * 128)
                        wload(wgt, win, 3 * HW + hd * 128)
                    pf = proj_feat(wf, uT, W, KC)
                    k.op(act, lambda: A.activation(out=t["sig"][:], in_=pf[:, :W], func=AF.Sigmoid), [pf], [t["sig"]])
                    pvT = proj_feat(wi, uT, W, KC)
                    k.op(act, lambda: A.activation(out=t["vTb"][:], in_=pvT[:, :W], func=AF.Copy), [pvT], [t["vTb"]])
                    if own:
                        pq = proj_feat(wq, uT, W, KC)
                        k.op(act, lambda: A.activation(out=t["qs"][:], in_=pq[:, :W], func=AF.Silu), [pq], [t["qs"]])
                        pgt = proj_feat(wgt, uT, W, KC)
                        k.op(act, lambda: A.activation(out=t["gs"][:], in_=pgt[:, :W], func=AF.Silu), [pgt], [t["gs"]])

                def A_rest(hd):
                    t = T[hd % 2]
                    sig, Fg, G, NG, kk, Et, kdb, dec, dtmp = (t[x] for x in ("sig", "Fg", "G", "NG", "kk", "Et", "kdb", "dec", "dtmp"))
                    k.op(dve, lambda: V.tensor_scalar(out=Fg[:], in0=sig[:], scalar1=omlt[:, hd:hd + 1], scalar2=lbt[:, hd:hd + 1], op0=ALU.mult, op1=ALU.add), [sig, omlt, lbt], [Fg])
                    k.op(act, lambda: A.activation(out=sig[:], in_=Fg[:], func=AF.Ln), [Fg], [sig])
                    k.op(dve, lambda: V.tensor_tensor_scan(out=G[:], data0=onesw[:], data1=sig[:], initial=0.0, op0=ALU.mult, op1=ALU.add), [onesw, sig], [G])
                    if own:
                        k.op(dve, lambda: V.tensor_scalar(out=NG[:], in0=G[:], scalar1=-1.0, scalar2=None, op0=ALU.mult), [G], [NG])
                    d8 = t["d8"]
                    k.op(dve, lambda: V.tensor_copy(out=d8[:, 0:1], in_=G[:, cwm - 1:cwm]), [G], [d8])
                    if nC > 1:
                        k.op(dve, lambda: V.tensor_tensor(out=d8[:, 1:nC], in0=G[:, 2 * cwm - 1:W:cwm], in1=G[:, cwm - 1:W - cwm:cwm], op=ALU.subtract), [G], [d8])
                    k.op(act, lambda: A.activation(out=dec[:, :], in_=d8[:, :], func=AF.Exp), [d8], [dec])
                    k.op(dve, lambda: V.tensor_scalar(out=kk[:], in0=Fg[:], scalar1=-1.0, scalar2=1.0, op0=ALU.mult, op1=ALU.add), [Fg], [kk])
                    for ci, (c0, cw) in enumerate(nch):
                        gl = G[:, c0 + cw - 1:c0 + cw]
                        g0 = G[:, c0 - 1:c0] if c0 > 0 else 0.0
                        ng0 = NG[:, c0 - 1:c0] if c0 > 0 else 0.0
                        k.op(act, lambda: A.activation(out=Et[:, c0:c0 + cw], in_=G[:, c0:c0 + cw], func=AF.Exp, scale=-1.0, bias=gl), [G], [Et])
                        k.op(dve, lambda: V.tensor_tensor(out=kdb[:, c0:c0 + cw], in0=kk[:, c0:c0 + cw], in1=Et[:, c0:c0 + cw], op=ALU.mult), [kk, Et], [kdb])
                        if own:
                            qs, qe, ke = t["qs"], t["qe"], t["ke"]
                            k.op(act, lambda: A.activation(out=Et[:, c0:c0 + cw], in_=G[:, c0:c0 + cw], func=AF.Exp, scale=1.0, bias=ng0), [G, NG], [Et])
                            k.op(dve, lambda: V.tensor_tensor(out=qe[:, c0:c0 + cw], in0=qs[:, c0:c0 + cw], in1=Et[:, c0:c0 + cw], op=ALU.mult), [qs, Et], [qe])
                            k.op(act, lambda: A.activation(out=Et[:, c0:c0 + cw], in_=G[:, c0:c0 + cw], func=AF.Exp, scale=-1.0, bias=g0), [G], [Et])
                            k.op(dve, lambda: V.tensor_tensor(out=ke[:, c0:c0 + cw], in0=kk[:, c0:c0 + cw], in1=Et[:, c0:c0 + cw], op=ALU.mult), [kk, Et], [ke])

                def B(hd):
                    t = T[hd % 2]
                    vTb, kdb, dec = t["vTb"], t["kdb"], t["dec"]
                    for (srcb, dstb, eng) in ((vTb, vt, "a"), (kdb, kdT, "d")):
                        pt = psT[psTi[0] % 2]
                        psTi[0] += 1
                        pe.wait(k.deps([srcb, ident], [pt]))
                        ins = None
                        for ci, (c0, cw) in enumerate(nch):
                            ins = nc.tensor.transpose(pt[:cw, ci * 128:(ci + 1) * 128], srcb[:, c0:c0 + cw], ident[:, :])
                        k.commit(pe.done(ins), [srcb, ident], [pt])
                        ptv = pt[:cwm, :nC * 128].rearrange("p (c n) -> p c n", n=128)
                        if eng == "a":
                            k.op(act, lambda: A.activation(out=dstb[:cwm, :, :], in_=ptv, func=AF.Copy), [pt], [dstb])
                        else:
                            k.op(dve, lambda: V.tensor_copy(out=dstb[:cwm, :, :], in_=ptv), [pt], [dstb])
                    if own:
                        qe, ke, gs = t["qe"], t["ke"], t["gs"]
                        k.op(act, lambda: A.activation(out=Sb[hd][:], in_=S[hd][:], func=AF.Copy), [S[hd]], [Sb[hd]])
                    for ci, (c0, cw) in enumerate(nch):
                        if own:
                            psc = k.ps()
                            k.mmg([(psc[:cw, :cw], ke[:, c0:c0 + cw], qe[:, c0:c0 + cw])], [ke, qe], [psc])
                            sc = scT[ci % 2]
                            k.op(dve, lambda: V.tensor_tensor(out=sc[:cw, :cw], in0=psc[:cw, :cw], in1=tri[:cw, :cw], op=ALU.mult), [psc, tri], [sc])
                            k.mmg([(pacc[:, c0:c0 + cw], vt[:cw, ci, :], sc[:cw, :cw]),
                                   (pacc[:, c0:c0 + cw], Sb[hd][:, :], qe[:, c0:c0 + cw])], [vt, sc, Sb[hd], qe], [pacc])
                        pS = k.ps()
                        k.mmg([(pS[:, :128], kdT[:cw, ci, :], vt[:cw, ci, :])], [kdT, vt], [pS])
                        k.op(dve, lambda: V.scalar_tensor_tensor(out=S[hd][:], in0=S[hd][:], scalar=dec[:, ci:ci + 1], in1=pS[:, :128], op0=ALU.mult, op1=ALU.add), [dec, pS], [S[hd]])
                        if own and ci + 1 < nC:
                            k.op(act, lambda: A.activation(out=Sb[hd][:], in_=S[hd][:], func=AF.Copy), [S[hd]], [Sb[hd]])
                    if own:
                        k.op(act, lambda: A.activation(out=sq[:], in_=pacc[:, :W], func=AF.Square), [pacc], [sq])
                        pss = k.ps()
                        k.mmg([(pss[:, :W], ones[:, :], sq[:, :])], [ones, sq], [pss])
                        k.op(dve, lambda: V.tensor_scalar(out=rst[:], in0=pss[:, :W], scalar1=1.0 / 128, scalar2=EPS, op0=ALU.mult, op1=ALU.add), [pss], [rst])
                        k.op(act, lambda: A.activation(out=rst[:], in_=rst[:], func=AF.Sqrt), [], [rst])
                        k.op(dve, lambda: V.reciprocal(out=rst[:], in_=rst[:]), [], [rst])
                        k.op(dve, lambda: V.tensor_tensor(out=sq[:], in0=pacc[:, :W], in1=rst[:], op=ALU.mult), [pacc, rst], [sq])
                        k.op(dve, lambda: V.scalar_tensor_tensor(out=yT[:, hd, :W], in0=sq[:], scalar=vf[:, c_hon + hd:c_hon + hd + 1], in1=gs[:], op0=ALU.mult, op1=ALU.mult), [sq, vf, gs], [yT])

                A_pe(0)
                A_rest(0)
                for hd in range(NH):
                    if hd + 1 < NH:
                        A_pe(hd + 1)
                    B(hd)
                    if hd + 1 < NH:
                        A_rest(hd + 1)
            with k.phase() as ph:
                wr = [[ph.sb("wl", [128, KC, 128], BF16) for _ in range(2)] for _ in range(2)]
                wa = ph.sb("wa_bf", [128, NLB, 2, 256], BF16)
                wx = ph.sb("wx_bf", [128, NLB, 2, 256], BF16)
                for bb in range(NLB):
                    k.dma(pool, wa, wa[:, bb], wad, wad.t[bb].rearrange("(ic p) j -> p ic j", p=128))
                    k.dma(pool, wx, wx[:, bb], wxd, wxd.t[bb].rearrange("(ic p) j -> p ic j", p=128))
                xbufs = [ph.sb("xbuf", [128, 3 + W], F32) for _ in range(2)]
                xc = ph.sb("xc", [128, 2, W], F32)
                xcb = ph.sb("xcb", [128, 2, W], BF16)
                hh = ph.sb("hh", [128, NL if own else 2, W], F32)
                ge = ph.sb("ge", [128, 2, W], F32) if own else None
                mk = ph.sb("mk", [128, W], F32)
                k.dma(sp, mk, mk[:], maskd, maskd.t[:, r0:r0 + W])
                f32t = lambda nm: ph.sb(nm, [128, W], F32)
                rgs, igs, aas, t1s, t2s = ([f32t(x) for _ in range(2)] for x in ("rg", "ig", "aa", "t1", "t2"))
                t2 = t2s[0]

                def lockstep(chains):
                    for i in range(max(len(c) for c in chains)):
                        for c in chains:
                            if i < len(c):
                                c[i]()

                def conv_chain(l):
                    j = l % 2
                    xbuf, t1, t2_ = xbufs[j], t1s[j], t2s[j]
                    ws = wr[j]
                    wxb = ws[0]
                    wload(wxb, win, 4 * HW + l * 128, 128, c_win, 2 * NH + l)
                    if own:
                        wgb = ws[1]
                        wload(wgb, win, 4 * HW + LW + l * 128)
                    st_ = {}
                    cwc = lambda q: vf[:, c_cw + q * NL + l:c_cw + q * NL + l + 1]
                    ops = []
                    ops.append(lambda: st_.__setitem__("px", proj_feat(wxb, uT, W, KC)))
                    ops.append(lambda: k.op(dve, lambda: V.tensor_copy(out=xbuf[:, 0:3], in_=tail[:, l, :]), [tail], [xbuf]))
                    ops.append(lambda: k.op(act, lambda: A.activation(out=xbuf[:, 3:3 + W], in_=st_["px"][:, :W], func=AF.Copy), [st_["px"]], [xbuf]))
                    ops.append(lambda: k.op(dve, lambda: V.tensor_scalar(out=xc[:, j, :], in0=xbuf[:, 0:W], scalar1=cwc(0), scalar2=vf[:, c_cb + l:c_cb + l + 1], op0=ALU.mult, op1=ALU.add), [xbuf, vf], [xc]))
                    for q in range(1, 4):
                        ops.append(lambda q=q: k.op(dve, lambda: V.scalar_tensor_tensor(out=xc[:, j, :], in0=xbuf[:, q:q + W], scalar=cwc(q), in1=xc[:, j, :], op0=ALU.mult, op1=ALU.add), [xbuf, vf], [xc]))
                    ops.append(lambda: k.op(dve, lambda: V.tensor_copy(out=tail[:, l, :], in_=xbuf[:, W:W + 3]), [xbuf], [tail]))
                    ops.append(lambda: k.op(act, lambda: A.activation(out=xcb[:, j, :], in_=xc[:, j, :], func=AF.Copy), [xc], [xcb]))
                    if own:
                        ops.append(lambda: st_.__setitem__("pgb", proj_feat(wgb, uT, W, KC)))
                        ops.append(lambda: k.op(act, lambda: A.activation(out=t1[:], in_=st_["pgb"][:, :W], func=AF.Copy), [st_["pgb"]], [t1]))
                        ops.append(lambda: k.op(dve, lambda: V.tensor_tensor(out=t2_[:], in0=t1[:], in1=t1[:], op=ALU.mult), [t1], [t2_]))
                        ops.append(lambda: k.op(dve, lambda: V.tensor_scalar(out=t2_[:], in0=t2_[:], scalar1=0.044715, scalar2=1.0, op0=ALU.mult, op1=ALU.add), [], [t2_]))
                        ops.append(lambda: k.op(dve, lambda: V.tensor_tensor(out=t2_[:], in0=t2_[:], in1=t1[:], op=ALU.mult), [t1], [t2_]))
                        ops.append(lambda: k.op(act, lambda: A.activation(out=t2_[:], in_=t2_[:], func=AF.Sigmoid, scale=1.5957691216057308), [], [t2_]))
                        ops.append(lambda: k.op(dve, lambda: V.tensor_tensor(out=ge[:, j, :], in0=t1[:], in1=t2_[:], op=ALU.mult), [t1, t2_], [ge]))
                    return ops

                def gate_chain(blk, jc):
                    ll = blk * 2 + jc
                    rg, ig, aa, t1 = rgs[jc], igs[jc], aas[jc], t1s[jc]
                    hi = ll if own else jc
                    st_ = {}
                    ops = []
                    ops.append(lambda: st_.__setitem__("pr", k.ps()))
                    ops.append(lambda: k.mmg([(st_["pr"][:, :W], wa[:, blk, ic, jc * 128:(jc + 1) * 128], xcb[:, ic, :]) for ic in range(2)], [wa, xcb], [st_["pr"]]))
                    ops.append(lambda: k.op(act, lambda: A.activation(out=rg[:], in_=st_["pr"][:, :W], func=AF.Sigmoid, bias=vf[:, c_ba + ll:c_ba + ll + 1]), [st_["pr"], vf], [rg]))
                    ops.append(lambda: st_.__setitem__("pi", k.ps()))
                    ops.append(lambda: k.mmg([(st_["pi"][:, :W], wx[:, blk, ic, jc * 128:(jc + 1) * 128], xcb[:, ic, :]) for ic in range(2)], [wx, xcb], [st_["pi"]]))
                    ops.append(lambda: k.op(act, lambda: A.activation(out=ig[:], in_=st_["pi"][:, :W], func=AF.Sigmoid, bias=vf[:, c_bx + ll:c_bx + ll + 1]), [st_["pi"], vf], [ig]))
                    ops.append(lambda: k.op(act, lambda: A.activation(out=aa[:], in_=rg[:], func=AF.Exp, scale=cch[:, ll:ll + 1]), [rg, cch], [aa]))
                    ops.append(lambda: k.op(dve, lambda: V.tensor_tensor(out=rg[:], in0=aa[:], in1=aa[:], op=ALU.mult), [aa], [rg]))
                    ops.append(lambda: k.op(dve, lambda: V.tensor_scalar(out=rg[:], in0=rg[:], scalar1=-1.0, scalar2=1.0, op0=ALU.mult, op1=ALU.add), [], [rg]))
                    ops.append(lambda: k.op(act, lambda: A.activation(out=rg[:], in_=rg[:], func=AF.Sqrt), [], [rg]))
                    ops.append(lambda: k.op(dve, lambda: V.tensor_tensor(out=ig[:], in0=ig[:], in1=xc[:, jc, :], op=ALU.mult), [xc], [ig]))
                    ops.append(lambda: k.op(dve, lambda: V.tensor_tensor(out=ig[:], in0=ig[:], in1=rg[:], op=ALU.mult), [rg], [ig]))
                    ops.append(lambda: k.op(dve, lambda: V.tensor_tensor(out=ig[:], in0=ig[:], in1=mk[:], op=ALU.mult), [mk], [ig]))
                    ops.append(lambda: k.op(dve, lambda: V.tensor_tensor_scan(out=hh[:, hi, :], data0=aa[:], data1=ig[:], initial=hst[:, ll:ll + 1], op0=ALU.mult, op1=ALU.add), [aa, ig, hst], [hh]))
                    ops.append(lambda: k.op(dve, lambda: V.tensor_copy(out=hst[:, ll:ll + 1], in_=hh[:, hi, W - 1:W]), [hh], [hst]))
                    if own:
                        ops.append(lambda: k.op(act, lambda: A.activation(out=t1[:], in_=hh[:, ll, :], func=AF.Square), [hh], [t1]))
                        ops.append(lambda: k.op(dve, lambda: V.scalar_tensor_tensor(out=hh[:, ll, :], in0=hh[:, ll, :], scalar=vf[:, c_lno + ll:c_lno + ll + 1], in1=ge[:, jc, :], op0=ALU.mult, op1=ALU.mult), [vf, ge], [hh]))

                        def accm():
                            pe.wait(k.deps([ones, t1], [pacc] if ll == 0 else []))
                            ins = nc.tensor.matmul(pacc[:, :W], lhsT=ones[:, :], rhs=t1[:, :], start=(ll == 0), stop=(ll == NL - 1))
                            k.commit(pe.done(ins), [ones, t1], [pacc])
                        ops.append(accm)
                    return ops

                for blk in range(NL // 2):
                    lockstep([conv_chain(2 * blk), conv_chain(2 * blk + 1)])
                    lockstep([gate_chain(blk, 0), gate_chain(blk, 1)])
                if own:
                    k.op(dve, lambda: V.tensor_scalar(out=t2[:], in0=pacc[:, :W], scalar1=1.0 / LW, scalar2=EPS, op0=ALU.mult, op1=ALU.add), [pacc], [t2])
                    k.op(act, lambda: A.activation(out=t2[:], in_=t2[:], func=AF.Sqrt), [], [t2])
                    k.op(dve, lambda: V.reciprocal(out=t2[:], in_=t2[:]), [], [t2])
                    for l in range(NL):
                        k.op(dve, lambda: V.tensor_tensor(out=yT[:, NH + l, :W], in0=hh[:, l, :], in1=t2[:], op=ALU.mult), [hh, t2], [yT])
        rss = down(douts, yT, KC, W, wout, 1.0, None, 1) if own else None
        pm.__exit__(None, None, None)
        if own:
            residual(h1s, hrow, rss, 1, h2s, 0)
        po.__exit__(None, None, None)

    if _STOP == "setup":
        k.barrier()
        k.es.close()
        return nc
    NBLK = (FC + 1) // 2
    c_ffn1 = (WCache("c1g", NBLK, KC * 256), WCache("c1u", NBLK, KC * 256), WCache("c1d", NDB * ((FC + GP - 1) // GP), GP * DB))
    blocks = [(NM + j * SB, SB, j >= NSB - NOWN, (j - (NSB - NOWN)) * SB) for j in range(NSB)]
    _skip = _os.environ.get("SKIP_MIX") == "1"
    for bi0, (r0, W, own, o0) in enumerate(blocks):
        for c_ in c_ffn1:
            c_.parity = bi0 if bi0 < 2 else None
        if bi0 == 0:
            ffn(xs, 0, NM + SB, c_g1, w1g, w1u, w1d, 0, h1s, 0, c_ffn1)
            if not _skip:
                c_win.parity = 2
                mixer(0, NM, False, 0)
                c_win.parity = None
            hrow = NM
        else:
            ffn(xs, r0, W, c_g1, w1g, w1u, w1d, 0, h1s, 0, c_ffn1)
            hrow = 0
        if not _skip:
            mixer(r0, W, own, hrow)
        if own:
            ffn(h1s if _skip else h2s, hrow if _skip else 0, W, c_g2, w2g, w2u, w2d, 2, outd, o0)
    k.barrier()
    k.es.close()
    return nc


def make_inputs(cfg, x, meta_tokens, ffn1_pre_norm, ffn1_w_gate, ffn1_w_up, ffn1_w_down, ffn1_post_norm,
                mix_pre_norm, w_in, hgrn_lb_logits, hgrn_out_norm, lru_conv_w, lru_conv_b,
                lru_w_a, lru_b_a, lru_w_x, lru_b_x, lru_lambda, lru_out_norm, w_out, mix_post_norm,
                ffn2_pre_norm, ffn2_w_gate, ffn2_w_up, ffn2_w_down, ffn2_post_norm):
    D, NM, SEQ, NSEG, B = cfg["D"], cfg["NM"], cfg["SEQ"], cfg["NSEG"], cfg["B"]
    SEG = SEQ // NSEG
    NT = NM + SEQ
    f = lambda a: np.ascontiguousarray(np.asarray(a, dtype=np.float32))
    fm = lambda v: f(v).reshape(-1, 128).T
    cw = f(lru_conv_w)[0]
    vfm = np.concatenate([fm(ffn1_pre_norm[0]), fm(mix_pre_norm[0]), fm(ffn2_pre_norm[0]),
                          fm(hgrn_lb_logits[0]), fm(hgrn_lb_logits[1]), fm(hgrn_out_norm[0]),
                          fm(cw[0]), fm(cw[1]), fm(cw[2]), fm(cw[3]), fm(lru_conv_b[0]), fm(lru_b_a[0]), fm(lru_b_x[0]),
                          fm(lru_lambda[0]), fm(lru_out_norm[0])], axis=1)
    vfm = np.ascontiguousarray(vfm)
    grow = np.ascontiguousarray(np.stack([np.broadcast_to(f(g[0])[None, :], (128, D))
                                          for g in (ffn1_post_norm, mix_post_norm, ffn2_post_norm, ffn1_pre_norm, mix_pre_norm, ffn2_pre_norm)]))
    shared = {"w1g": f(ffn1_w_gate[0]), "w1u": f(ffn1_w_up[0]), "w1d": f(ffn1_w_down[0]),
              "w2g": f(ffn2_w_gate[0]), "w2u": f(ffn2_w_up[0]), "w2d": f(ffn2_w_down[0]),
              "win": f(w_in[0]), "wout": f(w_out[0]), "wa": f(lru_w_a[0]), "wx": f(lru_w_x[0]),
              "vfm": vfm, "grow": grow, "ident": np.eye(128, dtype=np.float32),
              "tri": np.triu(np.ones((64, 64), np.float32))}
    x = f(x)
    meta = f(meta_tokens)
    maps = []
    for c in range(B * NSEG):
        b, s = divmod(c, NSEG)
        npad = (NSEG - 1 - s) * SEG
        xs = np.zeros((NT, D), np.float32)
        xs[npad:npad + NM] = meta
        xs[npad + NM:] = x[b, :(s + 1) * SEG]
        mask = np.zeros((128, NT), np.float32)
        mask[:, npad:] = 1.0
        m = dict(shared)
        m["xs"] = xs
        m["mask"] = mask
        maps.append(m)
    return maps


def run(cfg, inputs):
    nc = build(cfg)
    maps = make_inputs(cfg, **inputs)
    n = cfg["B"] * cfg["NSEG"]
    res = run_bass_kernel_spmd(nc, maps, core_ids=list(range(n)))
    SEG = cfg["SEQ"] // cfg["NSEG"]
    out = np.zeros((cfg["B"], cfg["SEQ"], cfg["D"]), np.float32)
    for c in range(n):
        b, s = divmod(c, cfg["NSEG"])
        out[b, s * SEG:(s + 1) * SEG] = res.results[c]["out"]
    return out


def kernel(**inputs):
    return run(CFG_FULL, inputs)
```

```python
import contextlib
import numpy as np
import concourse.bass as bass
import concourse.mybir as mybir
from concourse.bass_utils import run_bass_kernel_spmd

F32 = mybir.dt.float32
BF16 = mybir.dt.bfloat16
AF = mybir.ActivationFunctionType
ALU = mybir.AluOpType
EPS = 1e-6

CFG_FULL = dict(D=4096, DFF=11008, NM=16, SEQ=4096, B=2, NSEG=4, SB=512, DB=512, CH=64, GP=8)


class Eng:
    def __init__(self, k, eng, name):
        self.eng = eng
        self.sem = k.es.enter_context(k.nc.semaphore("s_" + name))
        self.cnt = 0
        self.seen = {}
        self.last = None
        self.inorder = False

    def wait(self, toks):
        for t in toks:
            if t is None:
                continue
            sem, val = t
            if self.seen.get(id(sem), 0) >= val:
                continue
            if self.inorder and sem is self.sem:
                continue
            self.eng.wait_ge(sem, val)
            self.seen[id(sem)] = val

    def done(self, ins):
        self.cnt += 1
        ins.then_inc(self.sem, 1)
        self.last = (self.sem, self.cnt)
        return self.last


class Buf:
    def __init__(self, t):
        self.t = t
        self.w = {}
        self.r = {}
        self.ds = None

    def __getitem__(self, key):
        return self.t[key]


def _merge(d, tok):
    sem, val = tok
    o = d.get(id(sem))
    if o is None or o[1] < val:
        d[id(sem)] = tok


class Phase:
    def __init__(self, k):
        self.k = k
        self.es = contextlib.ExitStack()
        self.bufs = []

    def __enter__(self):
        self.es.__enter__()
        return self

    def sb(self, name, shape, dt):
        self.k.uid += 1
        t = self.es.enter_context(self.k.nc.sbuf_tensor("%s_%d" % (name, self.k.uid), list(shape), dt))
        b = Buf(t)
        self.bufs.append(b)
        return b

    def __exit__(self, *a):
        self.k.barrier()
        for b in self.bufs:
            if b.ds is not None:
                self.k.free_dma.append(b.ds)
                b.ds = None
        return self.es.__exit__(*a)


class K:
    def __init__(self, nc):
        self.nc = nc
        self.es = contextlib.ExitStack()
        self.uid = 0
        self.pe = Eng(self, nc.tensor, "pe")
        self.pe.inorder = True
        self.act = Eng(self, nc.scalar, "act")
        self.dve = Eng(self, nc.vector, "dve")
        self.pool = Eng(self, nc.gpsimd, "pool")
        self.sp = Eng(self, nc.sync, "sp")
        self.engs = [self.pe, self.act, self.dve, self.pool, self.sp]
        self.dma_recs = []
        self.free_dma = []
        self.psb = []
        self.psi = 0

    def phase(self):
        return Phase(self)

    def dma_sem(self):
        if self.free_dma:
            return self.free_dma.pop()
        sem = self.es.enter_context(self.nc.semaphore("d_%d" % len(self.dma_recs)))
        rec = [sem, 0]
        self.dma_recs.append(rec)
        return rec

    def barrier(self):
        toks = [e.last for e in self.engs if e.last is not None]
        toks += [(r[0], r[1]) for r in self.dma_recs if r[1] > 0]
        for e in self.engs:
            e.wait(toks)

    def deps(self, reads, writes):
        d = []
        for b in reads:
            d.extend(b.w.values())
        for b in writes:
            d.extend(b.w.values())
            d.extend(b.r.values())
        return d

    def commit(self, tok, reads, writes):
        for b in reads:
            _merge(b.r, tok)
        for b in writes:
            _merge(b.w, tok)
            b.r = {}

    def op(self, eng, fn, reads=(), writes=()):
        eng.wait(self.deps(reads, writes))
        tok = eng.done(fn())
        self.commit(tok, reads, writes)
        return tok

    def dma(self, q, ob, oap, ib, iap, store=False):
        own = ib if store else ob
        if own.ds is None:
            own.ds = self.dma_sem()
        rec = own.ds
        if store:
            d = list(ib.w.values()) + list(ob.r.values())
        else:
            d = self.deps([ib], [ob])
        q.wait(d if store else [t for t in d if t[0] is not rec[0]])
        rec[1] += 16
        q.eng.dma_start(out=oap, in_=iap).then_inc(rec[0], 16)
        tok = (rec[0], rec[1])
        self.commit(tok, [ib], [ob])
        return tok

    def mmg(self, mats, reads, writes):
        self.pe.wait(self.deps(reads, writes))
        n = len(mats)
        ins = None
        for i, (o, l, r) in enumerate(mats):
            ins = self.nc.tensor.matmul(o, lhsT=l, rhs=r, start=(i == 0), stop=(i == n - 1))
        tok = self.pe.done(ins)
        self.commit(tok, reads, writes)
        return tok

    def ps(self):
        b = self.psb[self.psi % len(self.psb)]
        self.psi += 1
        return b


def build(cfg):
    import os as _os
    _STOP = _os.environ.get('STOP', '')
    MAXDESC = int(_os.environ.get('MAXDESC', '512'))
    D, DFF, NM, SEQ, NSEG, SB, DB, CH, GP = (cfg[x] for x in ("D", "DFF", "NM", "SEQ", "NSEG", "SB", "DB", "CH", "GP"))
    KC = D // 128
    FC = DFF // 128
    HW = D // 2
    LW = D - HW
    NH = HW // 128
    NL = LW // 128
    NLB = LW // 256
    SEG = SEQ // NSEG
    NSB = SEQ // SB
    NOWN = SEG // SB
    NT = NM + SEQ
    NDB = D // DB
    NV = 3 * KC + 3 * NH + 9 * NL
    c_g1, c_gm, c_g2 = 0, KC, 2 * KC
    c_lb0 = 3 * KC
    c_lb1 = c_lb0 + NH
    c_hon = c_lb1 + NH
    c_cw = c_hon + NH
    c_cb = c_cw + 4 * NL
    c_ba = c_cb + NL
    c_bx = c_ba + NL
    c_lam = c_bx + NL
    c_lno = c_lam + NL

    nc = bass.Bass("TRN2", target_bir_lowering=False)

    def din(name, shape):
        return Buf(nc.dram_tensor(name, list(shape), F32, kind="ExternalInput").ap())

    xs = din("xs", [NT, D])
    maskd = din("mask", [128, NT])
    w1g = din("w1g", [D, DFF]); w1u = din("w1u", [D, DFF]); w1d = din("w1d", [DFF, D])
    w2g = din("w2g", [D, DFF]); w2u = din("w2u", [D, DFF]); w2d = din("w2d", [DFF, D])
    win = din("win", [D, 4 * HW + 2 * LW]); wout = din("wout", [D, D])
    wad = din("wa", [NLB, 256, 256]); wxd = din("wx", [NLB, 256, 256])
    vfm = din("vfm", [128, NV]); grow = din("grow", [6, 128, D])
    identd = din("ident", [128, 128]); trid = din("tri", [64, 64])
    outd = Buf(nc.dram_tensor("out", [SEG, D], F32, kind="ExternalOutput").ap())
    h1s = Buf(nc.dram_tensor("h1s", [SB + NM, D], F32, kind="Internal").ap())
    h2s = Buf(nc.dram_tensor("h2s", [SB, D], F32, kind="Internal").ap())
    fs = Buf(nc.dram_tensor("fs", [SB + NM, D], F32, kind="Internal").ap())

    k = K(nc)
    pe, act, dve, pool, sp = k.pe, k.act, k.dve, k.pool, k.sp
    V = nc.vector
    A = nc.scalar
    E = k.es.enter_context

    def gsb(name, shape, dt):
        return Buf(E(nc.sbuf_tensor(name, list(shape), dt)))

    for i in range(5):
        k.psb.append(Buf(E(nc.psum_tensor("psg%d" % i, [128, 512], F32))))
    pacc = Buf(E(nc.psum_tensor("pacc", [128, 512], F32)))
    psT = [Buf(E(nc.psum_tensor("psT%d" % i, [128, 1024], BF16))) for i in range(2)]
    psb5 = list(k.psb)
    psb6 = psb5 + [pacc]
    psTi = [0]

    vf = gsb("vf", [128, NV], F32)
    ident = gsb("ident_bf", [128, 128], BF16)
    ones = gsb("ones_f", [128, 128], F32)
    tri = gsb("tri_f", [64, 64], F32)
    lbt = gsb("lbt", [128, NH], F32)
    omlt = gsb("omlt", [128, NH], F32)
    cch = gsb("cch", [128, NL], F32)
    S = [gsb("S%d" % h, [128, 128], F32) for h in range(NH)]
    Sb = [gsb("Sb%d" % h, [128, 128], BF16) for h in range(NH)]
    hst = gsb("hst", [128, NL], F32)
    tail = gsb("tail", [128, NL, 3], F32)
    tmpc = gsb("tmpc", [128, max(NH, NL)], F32)

    k.dma(sp, vf, vf[:], vfm, vfm[:])
    k.dma(pool, ident, ident[:], identd, identd[:])
    k.dma(sp, tri, tri[:], trid, trid[:])
    k.op(dve, lambda: V.memset(ones[:], 1.0), [], [ones])
    for h in range(NH):
        k.op(dve, lambda h=h: V.memset(S[h][:], 0.0), [], [S[h]])
        k.op(dve, lambda h=h: V.memset(Sb[h][:], 0.0), [], [Sb[h]])
    k.op(dve, lambda: V.memset(hst[:], 0.0), [], [hst])
    k.op(dve, lambda: V.memset(tail[:], 0.0), [], [tail])
    k.op(dve, lambda: V.tensor_tensor(out=tmpc[:, :NH], in0=vf[:, c_lb0:c_lb0 + NH], in1=vf[:, c_lb1:c_lb1 + NH], op=ALU.subtract), [vf], [tmpc])
    k.op(act, lambda: A.activation(out=lbt[:], in_=tmpc[:, :NH], func=AF.Sigmoid), [tmpc], [lbt])
    k.op(act, lambda: A.activation(out=omlt[:], in_=tmpc[:, :NH], func=AF.Sigmoid, scale=-1.0), [tmpc], [omlt])
    k.op(act, lambda: A.activation(out=tmpc[:, :NL], in_=vf[:, c_lam:c_lam + NL], func=AF.Exp, scale=-1.0), [vf, omlt], [tmpc])
    k.op(act, lambda: A.activation(out=tmpc[:, :NL], in_=tmpc[:, :NL], func=AF.Ln, bias=1.0), [], [tmpc])
    k.op(dve, lambda: V.tensor_scalar(out=cch[:], in0=tmpc[:, :NL], scalar1=-8.0, scalar2=None, op0=ALU.mult), [tmpc], [cch])
    k.barrier()

    KSTEP = max(1, MAXDESC // 128)

    class WCache:
        def __init__(self, name, ntiles, elems):
            self.buf = Buf(nc.dram_tensor(name, [ntiles, 128, elems], BF16, kind="Internal").ap())
            self.filled = set()
            self.parity = None

    def wload(wt, wb, c0, ncol=128, cache=None, idx=0):
        if cache is not None and idx in cache.filled:
            k.dma(pool, wt, wt[:, :, :ncol], cache.buf, cache.buf.t[idx][:, :KC * ncol].rearrange("p (k n) -> p k n", n=ncol))
            return
        v = wcols(wb, c0, ncol)
        for kc0 in range(0, KC, KSTEP):
            k.dma(pool, wt, wt[:, kc0:kc0 + KSTEP, :ncol], wb, v[:, kc0:kc0 + KSTEP, :])
        if cache is not None and (cache.parity is None or idx % 2 == cache.parity):
            k.dma(pool, cache.buf, cache.buf.t[idx][:, :KC * ncol].rearrange("p (k n) -> p k n", n=ncol), wt, wt[:, :, :ncol], store=True)
            cache.filled.add(idx)

    def wcols(wb, c0, n):
        return wb.t[:, c0:c0 + n].rearrange("(kc p) n -> p kc n", p=128)

    def rstd_from(ph, ssb, n, scale, inv_n, pre=None):
        if pre is None:
            ms = ph.sb("ms", [128, 1], F32)
            rs = ph.sb("rs", [128, 1], F32)
        else:
            ms, rs = pre
        k.op(dve, lambda: V.tensor_scalar(out=ms[:n], in0=ssb[:n, 0:1], scalar1=inv_n, scalar2=EPS, op0=ALU.mult, op1=ALU.add), [ssb], [ms])
        k.op(act, lambda: A.activation(out=ms[:n], in_=ms[:n], func=AF.Sqrt), [], [ms])
        k.op(dve, lambda: V.reciprocal(out=rs[:n], in_=ms[:n]), [ms], [rs])
        if scale != 1.0:
            k.op(dve, lambda: V.tensor_scalar(out=rs[:n], in0=rs[:n], scalar1=scale, scalar2=None, op0=ALU.mult), [], [rs])
        return rs

    def norm_T(src, r0, W, gcol, uT):
        gi = {c_g1: 3, c_gm: 4, c_g2: 5}[gcol]
        with k.phase() as ph:
            xts = [ph.sb("xt", [128, D], F32) for _ in range(2)]
            hss = [ph.sb("hs", [128, D], BF16) for _ in range(2)]
            gpre = ph.sb("gpre", [128, D], F32)
            k.dma(sp, gpre, gpre[:], grow, grow.t[gi])
            ti = 0
            ei = 0
            for t0 in range(0, W, 128):
                n = min(128, W - t0)
                xt = xts[ti % 2]
                hs = hss[ti % 2]
                ti += 1
                k.dma(sp, xt, xt[:n], src, src.t[r0 + t0:r0 + t0 + n, :])
                ss = ph.sb("ss", [128, 1], F32)
                k.op(act, lambda: A.activation(out=hs[:n], in_=xt[:n], func=AF.Square, accum_out=ss[:n, 0:1]), [xt], [hs, ss])
                rs = rstd_from(ph, ss, n, 1.0, 1.0 / D)
                k.op(dve, lambda: V.scalar_tensor_tensor(out=hs[:n], in0=xt[:n], scalar=rs[:n, 0:1], in1=gpre[:n], op0=ALU.mult, op1=ALU.mult), [xt, rs, gpre], [hs])
                for c0 in range(0, KC, 4):
                    pt = psT[psTi[0] % 2]
                    psTi[0] += 1
                    nn = min(4, KC - c0)
                    pe.wait(k.deps([hs, ident], [pt]))
                    ins = None
                    for j in range(nn):
                        ins = nc.tensor.transpose(pt[:, j * 128:j * 128 + n], hs[:n, (c0 + j) * 128:(c0 + j + 1) * 128], ident[:n, :n])
                    k.commit(pe.done(ins), [hs, ident], [pt])
                    ptv = pt[:, :nn * 128].rearrange("p (c n) -> p c n", n=128)[:, :, :n]
                    if ei % 2 == 0:
                        k.op(act, lambda: A.activation(out=uT[:, c0:c0 + nn, t0:t0 + n], in_=ptv, func=AF.Copy), [pt], [uT])
                    else:
                        k.op(dve, lambda: V.tensor_copy(out=uT[:, c0:c0 + nn, t0:t0 + n], in_=ptv), [pt], [uT])
                    ei += 1

    def proj_feat(wt, uT, W, kch, co=0):
        p = k.ps()
        k.mmg([(p[:, :W], wt[:, kc, co:co + 128], uT[:, kc, :W]) for kc in range(kch)], [wt, uT], [p])
        return p

    def gate_up(ph, uT, W, wg_d, wu_d, aT, cg=None, cu=None):
        rg = [ph.sb("wg", [128, KC, 256], BF16) for _ in range(2)]
        ru = [ph.sb("wu", [128, KC, 256], BF16) for _ in range(2)]
        sgs = [ph.sb("sg", [128, 512], F32) for _ in range(2)]
        sgi = [0]
        for bi_, fb in enumerate(range(0, FC, 2)):
            nfc = min(2, FC - fb)
            wg = rg[bi_ % 2]
            wu = ru[bi_ % 2]
            wload(wg, wg_d, fb * 128, nfc * 128, cg, bi_)
            wload(wu, wu_d, fb * 128, nfc * 128, cu, bi_)
            for j in range(nfc):
                fc = fb + j
                for (ca, cb_) in [(a_, min(W, a_ + 512)) for a_ in range(0, W, 512)]:
                    wc = cb_ - ca
                    pg = k.ps()
                    k.mmg([(pg[:, :wc], wg[:, kc, j * 128:(j + 1) * 128], uT[:, kc, ca:cb_]) for kc in range(KC)], [wg, uT], [pg])
                    pu = k.ps()
                    k.mmg([(pu[:, :wc], wu[:, kc, j * 128:(j + 1) * 128], uT[:, kc, ca:cb_]) for kc in range(KC)], [wu, uT], [pu])
                    sg = sgs[sgi[0] % 2]
                    sgi[0] += 1
                    k.op(act, lambda: A.activation(out=sg[:, :wc], in_=pg[:, :wc], func=AF.Silu), [pg], [sg])
                    k.op(dve, lambda: V.tensor_tensor(out=aT[:, fc, ca:cb_], in0=sg[:, :wc], in1=pu[:, :wc], op=ALU.mult), [sg, pu], [aT])

    def down(outer, aT, kch, W, wd_d, scale, cd=None, grow_i=0):
        tts = [(t0, min(128, W - t0)) for t0 in range(0, W, 128)]
        ssq, ssd, mss, rss_ = outer
        with k.phase() as ph:
            NRING = 6
            ring = [ph.sb("wd", [128, GP, DB], BF16) for _ in range(NRING)]
            sts = [ph.sb("st", [128, DB], F32) for _ in range(NRING)]
            junk = ph.sb("junkd", [128, DB], BF16)
            gbd = ph.sb("gbd", [128, D], F32)
            k.dma(sp, gbd, gbd[:], grow, grow.t[grow_i])
            ri = 0
            si = 0
            for db in range(NDB):
                banks = [k.ps() for _ in tts]
                pe.wait(k.deps([aT], []))
                lasttok = None
                for g0 in range(0, kch, GP):
                    g = min(GP, kch - g0)
                    wd = ring[ri % NRING]
                    ri += 1
                    cidx = db * ((kch + GP - 1) // GP) + g0 // GP
                    if cd is not None and cidx in cd.filled:
                        k.dma(pool, wd, wd[:, :g, :], cd.buf, cd.buf.t[cidx][:, :g * DB].rearrange("p (g n) -> p g n", n=DB))
                    else:
                        for ga in range(0, g, KSTEP):
                            gb_ = min(g, ga + KSTEP)
                            k.dma(pool, wd, wd[:, ga:gb_, :], wd_d,
                                  wd_d.t[(g0 + ga) * 128:(g0 + gb_) * 128, db * DB:(db + 1) * DB].rearrange("(g p) n -> p g n", p=128))
                        if cd is not None and (cd.parity is None or cidx % 2 == cd.parity):
                            k.dma(pool, cd.buf, cd.buf.t[cidx][:, :g * DB].rearrange("p (g n) -> p g n", n=DB), wd, wd[:, :g, :], store=True)
                            cd.filled.add(cidx)
                    pe.wait(k.deps([wd], []))
                    ins = None
                    for gi in range(g):
                        for bi, (t0, n) in enumerate(tts):
                            if g0 + gi == 0:
                                pe.wait(k.deps([], [banks[bi]]))
                            ins = nc.tensor.matmul(banks[bi][:n, :DB], lhsT=aT[:, g0 + gi, t0:t0 + n], rhs=wd[:, gi, :],
                                                   start=(g0 + gi == 0), stop=(g0 + gi == kch - 1))
                    lasttok = pe.done(ins)
                    k.commit(lasttok, [wd], [])
                k.commit(lasttok, [aT], banks)
                for bi, (t0, n) in enumerate(tts):
                    st = sts[si % NRING]
                    si += 1
                    k.op(act, lambda: A.activation(out=st[:n], in_=banks[bi][:n, :DB], func=AF.Copy), [banks[bi]], [st])
                    k.op(act, lambda: A.activation(out=junk[:n], in_=st[:n], func=AF.Square, accum_out=ssq[bi][:n, db:db + 1]), [st], [junk, ssq[bi]])
                    k.op(dve, lambda: V.tensor_tensor(out=st[:n], in0=st[:n], in1=gbd[:n, db * DB:(db + 1) * DB], op=ALU.mult), [gbd], [st])
                    k.dma(sp, fs, fs.t[t0:t0 + n, db * DB:(db + 1) * DB], st, st[:n], store=True)
        res = []
        for bi, (t0, n) in enumerate(tts):
            ss = ssd[bi]
            k.op(dve, lambda: V.reduce_sum(out=ss[:n], in_=ssq[bi][:n, :], axis=mybir.AxisListType.X), [ssq[bi]], [ss])
            res.append((t0, n, rstd_from(None, ss, n, scale, 1.0 / D, pre=(mss[bi], rss_[bi]))))
        return res

    def down_outs(po, W):
        nt = (W + 127) // 128
        return ([po.sb("ssq", [128, NDB], F32) for _ in range(nt)], [po.sb("ssd", [128, 1], F32) for _ in range(nt)],
                [po.sb("msd", [128, 1], F32) for _ in range(nt)], [po.sb("rsd", [128, 1], F32) for _ in range(nt)])

    def residual(src, r0, rss, grow_i, dst, d0):
        with k.phase() as ph:
            fts = [ph.sb("ft", [128, D], F32) for _ in range(2)]
            xts = [ph.sb("xr", [128, D], F32) for _ in range(2)]
            for i, (t0, n, rs) in enumerate(rss):
                ft = fts[i % 2]
                xt = xts[i % 2]
                k.dma(sp, ft, ft[:n], fs, fs.t[t0:t0 + n, :])
                k.dma(sp, xt, xt[:n], src, src.t[r0 + t0:r0 + t0 + n, :])
                k.op(dve, lambda: V.scalar_tensor_tensor(out=xt[:n], in0=ft[:n], scalar=rs[:n, 0:1], in1=xt[:n], op0=ALU.mult, op1=ALU.add), [rs, ft], [xt])
                k.dma(sp, dst, dst.t[d0 + t0:d0 + t0 + n, :], xt, xt[:n], store=True)

    def ffn(src, r0, W, gcol, wg_d, wu_d, wd_d, grow_i, dst, d0, caches=(None, None, None)):
        k.psb = psb6
        with k.phase() as po:
            douts = down_outs(po, W)
            with k.phase() as pa:
                aT = pa.sb("aT", [128, FC, W], BF16)
                with k.phase() as pb:
                    uT = pb.sb("uT", [128, KC, W], BF16)
                    norm_T(src, r0, W, gcol, uT)
                    if _STOP != "norm":
                        with k.phase() as pc:
                            gate_up(pc, uT, W, wg_d, wu_d, aT, caches[0], caches[1])
                rss = down(douts, aT, FC, W, wd_d, 0.5, caches[2], grow_i) if _STOP not in ("norm", "gate") else None
            if _STOP not in ("norm", "gate", "down"):
                residual(src, r0, rss, grow_i, dst, d0)
        k.psb = psb5

    c_win = WCache("cwin", 2 * NH + NL, KC * 128)

    def mixer(r0, W, own, hrow=0):
        nch = [(c0, min(CH, W - c0)) for c0 in range(0, W, CH)]
        po = k.phase()
        po.__enter__()
        douts = down_outs(po, W) if own else None
        pm = k.phase()
        pm.__enter__()
        yT = pm.sb("yT", [128, KC, W], BF16) if own else None
        with k.phase() as pu_:
            uT = pu_.sb("uTm", [128, KC, W], BF16)
            norm_T(h1s, hrow, W, c_gm, uT)
            with k.phase() as ph:
                nw = 4 if own else 2
                wr = [[ph.sb("wh", [128, KC, 128], BF16) for _ in range(nw)] for _ in range(2)]
                f32t = lambda nm: ph.sb(nm, [128, W], F32)
                bft = lambda nm: ph.sb(nm, [128, W], BF16)
                nC = len(nch)
                cwm = nch[0][1]

                def mkset():
                    d = {x: f32t(x) for x in ("sig", "Fg", "G", "NG", "kk", "Et")}
                    if own:
                        d.update({x: f32t(x) for x in ("qs", "gs")})
                        d.update({x: bft(x) for x in ("qe", "ke")})
                    d.update({x: bft(x) for x in ("kdb", "vTb")})
                    d["dec"] = ph.sb("dec", [128, nC], F32)
                    d["dtmp"] = ph.sb("dtmp", [128, 1], F32)
                    d["d8"] = ph.sb("d8", [128, nC], F32)
                    return d
                T = [mkset(), mkset()]
                sq, rst, onesw = f32t("sq"), f32t("rst"), f32t("onesw")
                k.op(dve, lambda: V.memset(onesw[:], 1.0), [], [onesw])
                vt = ph.sb("vt", [64, nC, 128], BF16)
                kdT = ph.sb("kdT", [64, nC, 128], BF16)
                scT = [ph.sb("scT", [64, 64], BF16) for _ in range(2)]

                def A_pe(hd):
                    t = T[hd % 2]
                    ws = wr[hd % 2]
                    wf, wi = ws[0], ws[1]
                    wload(wf, win, HW + hd * 128, 128, c_win, hd)
                    wload(wi, win, 2 * HW + hd * 128, 128, c_win, NH + hd)
                    if own:
                        wq, wgt = ws[2], ws[3]
                        wload(wq, win, hd * 128)
                        wload(wgt, win, 3 * HW + hd * 128)
                    pf = proj_feat(wf, uT, W, KC)
                    k.op(act, lambda: A.activation(out=t["sig"][:], in_=pf[:, :W], func=AF.Sigmoid), [pf], [t["sig"]])
                    pvT = proj_feat(wi, uT, W, KC)
                    k.op(act, lambda: A.activation(out=t["vTb"][:], in_=pvT[:, :W], func=AF.Copy), [pvT], [t["vTb"]])
                    if own:
                        pq = proj_feat(wq, uT, W, KC)
                        k.op(act, lambda: A.activation(out=t["qs"][:], in_=pq[:, :W], func=AF.Silu), [pq], [t["qs"]])
                        pgt = proj_feat(wgt, uT, W, KC)
                        k.op(act, lambda: A.activation(out=t["gs"][:], in_=pgt[:, :W], func=AF.Silu), [pgt], [t["gs"]])

                def A_rest(hd):
                    t = T[hd % 2]
                    sig, Fg, G, NG, kk, Et, kdb, dec, dtmp = (t[x] for x in ("sig", "Fg", "G", "NG", "kk", "Et", "kdb", "dec", "dtmp"))
                    k.op(dve, lambda: V.tensor_scalar(out=Fg[:], in0=sig[:], scalar1=omlt[:, hd:hd + 1], scalar2=lbt[:, hd:hd + 1], op0=ALU.mult, op1=ALU.add), [sig, omlt, lbt], [Fg])
                    k.op(act, lambda: A.activation(out=sig[:], in_=Fg[:], func=AF.Ln), [Fg], [sig])
                    k.op(dve, lambda: V.tensor_tensor_scan(out=G[:], data0=onesw[:], data1=sig[:], initial=0.0, op0=ALU.mult, op1=ALU.add), [onesw, sig], [G])
                    if own:
                        k.op(dve, lambda: V.tensor_scalar(out=NG[:], in0=G[:], scalar1=-1.0, scalar2=None, op0=ALU.mult), [G], [NG])
                    d8 = t["d8"]
                    k.op(dve, lambda: V.tensor_copy(out=d8[:, 0:1], in_=G[:, cwm - 1:cwm]), [G], [d8])
                    if nC > 1:
                        k.op(dve, lambda: V.tensor_tensor(out=d8[:, 1:nC], in0=G[:, 2 * cwm - 1:W:cwm], in1=G[:, cwm - 1:W - cwm:cwm], op=ALU.subtract), [G], [d8])
                    k.op(act, lambda: A.activation(out=dec[:, :], in_=d8[:, :], func=AF.Exp), [d8], [dec])
                    k.op(dve, lambda: V.tensor_scalar(out=kk[:], in0=Fg[:], scalar1=-1.0, scalar2=1.0, op0=ALU.mult, op1=ALU.add), [Fg], [kk])
                    for ci, (c0, cw) in enumerate(nch):
                        gl = G[:, c0 + cw - 1:c0 + cw]
                        g0 = G[:, c0 - 1:c0] if c0 > 0 else 0.0
                        ng0 = NG[:, c0 - 1:c0] if c0 > 0 else 0.0
                        k.op(act, lambda: A.activation(out=Et[:, c0:c0 + cw], in_=G[:, c0:c0 + cw], func=AF.Exp, scale=-1.0, bias=gl), [G], [Et])
                        k.op(dve, lambda: V.tensor_tensor(out=kdb[:, c0:c0 + cw], in0=kk[:, c0:c0 + cw], in1=Et[:, c0:c0 + cw], op=ALU.mult), [kk, Et], [kdb])
                        if own:
                            qs, qe, ke = t["qs"], t["qe"], t["ke"]
                            k.op(act, lambda: A.activation(out=Et[:, c0:c0 + cw], in_=G[:, c0:c0 + cw], func=AF.Exp, scale=1.0, bias=ng0), [G, NG], [Et])
                            k.op(dve, lambda: V.tensor_tensor(out=qe[:, c0:c0 + cw], in0=qs[:, c0:c0 + cw], in1=Et[:, c0:c0 + cw], op=ALU.mult), [qs, Et], [qe])
                            k.op(act, lambda: A.activation(out=Et[:, c0:c0 + cw], in_=G[:, c0:c0 + cw], func=AF.Exp, scale=-1.0, bias=g0), [G], [Et])
                            k.op(dve, lambda: V.tensor_tensor(out=ke[:, c0:c0 + cw], in0=kk[:, c0:c0 + cw], in1=Et[:, c0:c0 + cw], op=ALU.mult), [kk, Et], [ke])

                def B(hd):
                    t = T[hd % 2]
                    vTb, kdb, dec = t["vTb"], t["kdb"], t["dec"]
                    for (srcb, dstb, eng) in ((vTb, vt, "a"), (kdb, kdT, "d")):
                        pt = psT[psTi[0] % 2]
                        psTi[0] += 1
                        pe.wait(k.deps([srcb, ident], [pt]))
                        ins = None
                        for ci, (c0, cw) in enumerate(nch):
                            ins = nc.tensor.transpose(pt[:cw, ci * 128:(ci + 1) * 128], srcb[:, c0:c0 + cw], ident[:, :])
                        k.commit(pe.done(ins), [srcb, ident], [pt])
                        ptv = pt[:cwm, :nC * 128].rearrange("p (c n) -> p c n", n=128)
                        if eng == "a":
                            k.op(act, lambda: A.activation(out=dstb[:cwm, :, :], in_=ptv, func=AF.Copy), [pt], [dstb])
                        else:
                            k.op(dve, lambda: V.tensor_copy(out=dstb[:cwm, :, :], in_=ptv), [pt], [dstb])
                    if own:
                        qe, ke, gs = t["qe"], t["ke"], t["gs"]
                        k.op(act, lambda: A.activation(out=Sb[hd][:], in_=S[hd][:], func=AF.Copy), [S[hd]], [Sb[hd]])
                    for ci, (c0, cw) in enumerate(nch):
                        if own:
                            psc = k.ps()
                            k.mmg([(psc[:cw, :cw], ke[:, c0:c0 + cw], qe[:, c0:c0 + cw])], [ke, qe], [psc])
                            sc = scT[ci % 2]
                            k.op(dve, lambda: V.tensor_tensor(out=sc[:cw, :cw], in0=psc[:cw, :cw], in1=tri[:cw, :cw], op=ALU.mult), [psc, tri], [sc])
                            k.mmg([(pacc[:, c0:c0 + cw], vt[:cw, ci, :], sc[:cw, :cw]),
                                   (pacc[:, c0:c0 + cw], Sb[hd][:, :], qe[:, c0:c0 + cw])], [vt, sc, Sb[hd], qe], [pacc])
                        pS = k.ps()
                        k.mmg([(pS[:, :128], kdT[:cw, ci, :], vt[:cw, ci, :])], [kdT, vt], [pS])
                        k.op(dve, lambda: V.scalar_tensor_tensor(out=S[hd][:], in0=S[hd][:], scalar=dec[:, ci:ci + 1], in1=pS[:, :128], op0=ALU.mult, op1=ALU.add), [dec, pS], [S[hd]])
                        if own and ci + 1 < nC:
                            k.op(act, lambda: A.activation(out=Sb[hd][:], in_=S[hd][:], func=AF.Copy), [S[hd]], [Sb[hd]])
                    if own:
                        k.op(act, lambda: A.activation(out=sq[:], in_=pacc[:, :W], func=AF.Square), [pacc], [sq])
                        pss = k.ps()
                        k.mmg([(pss[:, :W], ones[:, :], sq[:, :])], [ones, sq], [pss])
                        k.op(dve, lambda: V.tensor_scalar(out=rst[:], in0=pss[:, :W], scalar1=1.0 / 128, scalar2=EPS, op0=ALU.mult, op1=ALU.add), [pss], [rst])
                        k.op(act, lambda: A.activation(out=rst[:], in_=rst[:], func=AF.Sqrt), [], [rst])
                        k.op(dve, lambda: V.reciprocal(out=rst[:], in_=rst[:]), [], [rst])
                        k.op(dve, lambda: V.tensor_tensor(out=sq[:], in0=pacc[:, :W], in1=rst[:], op=ALU.mult), [pacc, rst], [sq])
                        k.op(dve, lambda: V.scalar_tensor_tensor(out=yT[:, hd, :W], in0=sq[:], scalar=vf[:, c_hon + hd:c_hon + hd + 1], in1=gs[:], op0=ALU.mult, op1=ALU.mult), [sq, vf, gs], [yT])

                A_pe(0)
                A_rest(0)
                for hd in range(NH):
                    if hd + 1 < NH:
                        A_pe(hd + 1)
                    B(hd)
                    if hd + 1 < NH:
                        A_rest(hd + 1)
            with k.phase() as ph:
                wr = [[ph.sb("wl", [128, KC, 128], BF16) for _ in range(2)] for _ in range(2)]
                wa = ph.sb("wa_bf", [128, NLB, 2, 256], BF16)
                wx = ph.sb("wx_bf", [128, NLB, 2, 256], BF16)
                for bb in range(NLB):
                    k.dma(pool, wa, wa[:, bb], wad, wad.t[bb].rearrange("(ic p) j -> p ic j", p=128))
                    k.dma(pool, wx, wx[:, bb], wxd, wxd.t[bb].rearrange("(ic p) j -> p ic j", p=128))
                xbufs = [ph.sb("xbuf", [128, 3 + W], F32) for _ in range(2)]
                xc = ph.sb("xc", [128, 2, W], F32)
                xcb = ph.sb("xcb", [128, 2, W], BF16)
                hh = ph.sb("hh", [128, NL if own else 2, W], F32)
                ge = ph.sb("ge", [128, 2, W], F32) if own else None
                mk = ph.sb("mk", [128, W], F32)
                k.dma(sp, mk, mk[:], maskd, maskd.t[:, r0:r0 + W])
                f32t = lambda nm: ph.sb(nm, [128, W], F32)
                rgs, igs, aas, t1s, t2s = ([f32t(x) for _ in range(2)] for x in ("rg", "ig", "aa", "t1", "t2"))
                t2 = t2s[0]

                def lockstep(chains):
                    for i in range(max(len(c) for c in chains)):
                        for c in chains:
                            if i < len(c):
                                c[i]()

                def conv_chain(l):
                    j = l % 2
                    xbuf, t1, t2_ = xbufs[j], t1s[j], t2s[j]
                    ws = wr[j]
                    wxb = ws[0]
                    wload(wxb, win, 4 * HW + l * 128, 128, c_win, 2 * NH + l)
                    if own:
                        wgb = ws[1]
                        wload(wgb, win, 4 * HW + LW + l * 128)
                    st_ = {}
                    cwc = lambda q: vf[:, c_cw + q * NL + l:c_cw + q * NL + l + 1]
                    ops = []
                    ops.append(lambda: st_.__setitem__("px", proj_feat(wxb, uT, W, KC)))
                    ops.append(lambda: k.op(dve, lambda: V.tensor_copy(out=xbuf[:, 0:3], in_=tail[:, l, :]), [tail], [xbuf]))
                    ops.append(lambda: k.op(act, lambda: A.activation(out=xbuf[:, 3:3 + W], in_=st_["px"][:, :W], func=AF.Copy), [st_["px"]], [xbuf]))
                    ops.append(lambda: k.op(dve, lambda: V.tensor_scalar(out=xc[:, j, :], in0=xbuf[:, 0:W], scalar1=cwc(0), scalar2=vf[:, c_cb + l:c_cb + l + 1], op0=ALU.mult, op1=ALU.add), [xbuf, vf], [xc]))
                    for q in range(1, 4):
                        ops.append(lambda q=q: k.op(dve, lambda: V.scalar_tensor_tensor(out=xc[:, j, :], in0=xbuf[:, q:q + W], scalar=cwc(q), in1=xc[:, j, :], op0=ALU.mult, op1=ALU.add), [xbuf, vf], [xc]))
                    ops.append(lambda: k.op(dve, lambda: V.tensor_copy(out=tail[:, l, :], in_=xbuf[:, W:W + 3]), [xbuf], [tail]))
                    ops.append(lambda: k.op(act, lambda: A.activation(out=xcb[:, j, :], in_=xc[:, j, :], func=AF.Copy), [xc], [xcb]))
                    if own:
                        ops.append(lambda: st_.__setitem__("pgb", proj_feat(wgb, uT, W, KC)))
                        ops.append(lambda: k.op(act, lambda: A.activation(out=t1[:], in_=st_["pgb"][:, :W], func=AF.Copy), [st_["pgb"]], [t1]))
                        ops.append(lambda: k.op(dve, lambda: V.tensor_tensor(out=t2_[:], in0=t1[:], in1=t1[:], op=ALU.mult), [t1], [t2_]))
                        ops.append(lambda: k.op(dve, lambda: V.tensor_scalar(out=t2_[:], in0=t2_[:], scalar1=0.044715, scalar2=1.0, op0=ALU.mult, op1=ALU.add), [], [t2_]))
                        ops.append(lambda: k.op(dve, lambda: V.tensor_tensor(out=t2_[:], in0=t2_[:], in1=t1[:], op=ALU.mult), [t1], [t2_]))
                        ops.append(lambda: k.op(act, lambda: A.activation(out=t2_[:], in_=t2_[:], func=AF.Sigmoid, scale=1.5957691216057308), [], [t2_]))
                        ops.append(lambda: k.op(dve, lambda: V.tensor_tensor(out=ge[:, j, :], in0=t1[:], in1=t2_[:], op=ALU.mult), [t1, t2_], [ge]))
                    return ops

                def gate_chain(blk, jc):
                    ll = blk * 2 + jc
                    rg, ig, aa, t1 = rgs[jc], igs[jc], aas[jc], t1s[jc]
                    hi = ll if own else jc
                    st_ = {}
                    ops = []
                    ops.append(lambda: st_.__setitem__("pr", k.ps()))
                    ops.append(lambda: k.mmg([(st_["pr"][:, :W], wa[:, blk, ic, jc * 128:(jc + 1) * 128], xcb[:, ic, :]) for ic in range(2)], [wa, xcb], [st_["pr"]]))
                    ops.append(lambda: k.op(act, lambda: A.activation(out=rg[:], in_=st_["pr"][:, :W], func=AF.Sigmoid, bias=vf[:, c_ba + ll:c_ba + ll + 1]), [st_["pr"], vf], [rg]))
                    ops.append(lambda: st_.__setitem__("pi", k.ps()))
                    ops.append(lambda: k.mmg([(st_["pi"][:, :W], wx[:, blk, ic, jc * 128:(jc + 1) * 128], xcb[:, ic, :]) for ic in range(2)], [wx, xcb], [st_["pi"]]))
                    ops.append(lambda: k.op(act, lambda: A.activation(out=ig[:], in_=st_["pi"][:, :W], func=AF.Sigmoid, bias=vf[:, c_bx + ll:c_bx + ll + 1]), [st_["pi"], vf], [ig]))
                    ops.append(lambda: k.op(act, lambda: A.activation(out=aa[:], in_=rg[:], func=AF.Exp, scale=cch[:, ll:ll + 1]), [rg, cch], [aa]))
                    ops.append(lambda: k.op(dve, lambda: V.tensor_tensor(out=rg[:], in0=aa[:], in1=aa[:], op=ALU.mult), [aa], [rg]))
                    ops.append(lambda: k.op(dve, lambda: V.tensor_scalar(out=rg[:], in0=rg[:], scalar1=-1.0, scalar2=1.0, op0=ALU.mult, op1=ALU.add), [], [rg]))
                    ops.append(lambda: k.op(act, lambda: A.activation(out=rg[:], in_=rg[:], func=AF.Sqrt), [], [rg]))
                    ops.append(lambda: k.op(dve, lambda: V.tensor_tensor(out=ig[:], in0=ig[:], in1=xc[:, jc, :], op=ALU.mult), [xc], [ig]))
                    ops.append(lambda: k.op(dve, lambda: V.tensor_tensor(out=ig[:], in0=ig[:], in1=rg[:], op=ALU.mult), [rg], [ig]))
                    ops.append(lambda: k.op(dve, lambda: V.tensor_tensor(out=ig[:], in0=ig[:], in1=mk[:], op=ALU.mult), [mk], [ig]))
                    ops.append(lambda: k.op(dve, lambda: V.tensor_tensor_scan(out=hh[:, hi, :], data0=aa[:], data1=ig[:], initial=hst[:, ll:ll + 1], op0=ALU.mult, op1=ALU.add), [aa, ig, hst], [hh]))
                    ops.append(lambda: k.op(dve, lambda: V.tensor_copy(out=hst[:, ll:ll + 1], in_=hh[:, hi, W - 1:W]), [hh], [hst]))
                    if own:
                        ops.append(lambda: k.op(act, lambda: A.activation(out=t1[:], in_=hh[:, ll, :], func=AF.Square), [hh], [t1]))
                        ops.append(lambda: k.op(dve, lambda: V.scalar_tensor_tensor(out=hh[:, ll, :], in0=hh[:, ll, :], scalar=vf[:, c_lno + ll:c_lno + ll + 1], in1=ge[:, jc, :], op0=ALU.mult, op1=ALU.mult), [vf, ge], [hh]))

                        def accm():
                            pe.wait(k.deps([ones, t1], [pacc] if ll == 0 else []))
                            ins = nc.tensor.matmul(pacc[:, :W], lhsT=ones[:, :], rhs=t1[:, :], start=(ll == 0), stop=(ll == NL - 1))
                            k.commit(pe.done(ins), [ones, t1], [pacc])
                        ops.append(accm)
                    return ops

                for blk in range(NL // 2):
                    lockstep([conv_chain(2 * blk), conv_chain(2 * blk + 1)])
                    lockstep([gate_chain(blk, 0), gate_chain(blk, 1)])
                if own:
                    k.op(dve, lambda: V.tensor_scalar(out=t2[:], in0=pacc[:, :W], scalar1=1.0 / LW, scalar2=EPS, op0=ALU.mult, op1=ALU.add), [pacc], [t2])
                    k.op(act, lambda: A.activation(out=t2[:], in_=t2[:], func=AF.Sqrt), [], [t2])
                    k.op(dve, lambda: V.reciprocal(out=t2[:], in_=t2[:]), [], [t2])
                    for l in range(NL):
                        k.op(dve, lambda: V.tensor_tensor(out=yT[:, NH + l, :W], in0=hh[:, l, :], in1=t2[:], op=ALU.mult), [hh, t2], [yT])
        rss = down(douts, yT, KC, W, wout, 1.0, None, 1) if own else None
        pm.__exit__(None, None, None)
        if own:
            residual(h1s, hrow, rss, 1, h2s, 0)
        po.__exit__(None, None, None)

    if _STOP == "setup":
        k.barrier()
        k.es.close()
        return nc
    NBLK = (FC + 1) // 2
    c_ffn1 = (WCache("c1g", NBLK, KC * 256), WCache("c1u", NBLK, KC * 256), WCache("c1d", NDB * ((FC + GP - 1) // GP), GP * DB))
    blocks = [(NM + j * SB, SB, j >= NSB - NOWN, (j - (NSB - NOWN)) * SB) for j in range(NSB)]
    _skip = _os.environ.get("SKIP_MIX") == "1"
    for bi0, (r0, W, own, o0) in enumerate(blocks):
        for c_ in c_ffn1:
            c_.parity = bi0 if bi0 < 2 else None
        if bi0 == 0:
            ffn(xs, 0, NM + SB, c_g1, w1g, w1u, w1d, 0, h1s, 0, c_ffn1)
            if not _skip:
                c_win.parity = 2
                mixer(0, NM, False, 0)
                c_win.parity = None
            hrow = NM
        else:
            ffn(xs, r0, W, c_g1, w1g, w1u, w1d, 0, h1s, 0, c_ffn1)
            hrow = 0
        if not _skip:
            mixer(r0, W, own, hrow)
        if own:
            ffn(h1s if _skip else h2s, hrow if _skip else 0, W, c_g2, w2g, w2u, w2d, 2, outd, o0)
    k.barrier()
    k.es.close()
    return nc


def make_inputs(cfg, x, meta_tokens, ffn1_pre_norm, ffn1_w_gate, ffn1_w_up, ffn1_w_down, ffn1_post_norm,
                mix_pre_norm, w_in, hgrn_lb_logits, hgrn_out_norm, lru_conv_w, lru_conv_b,
                lru_w_a, lru_b_a, lru_w_x, lru_b_x, lru_lambda, lru_out_norm, w_out, mix_post_norm,
                ffn2_pre_norm, ffn2_w_gate, ffn2_w_up, ffn2_w_down, ffn2_post_norm):
    D, NM, SEQ, NSEG, B = cfg["D"], cfg["NM"], cfg["SEQ"], cfg["NSEG"], cfg["B"]
    SEG = SEQ // NSEG
    NT = NM + SEQ
    f = lambda a: np.ascontiguousarray(np.asarray(a, dtype=np.float32))
    fm = lambda v: f(v).reshape(-1, 128).T
    cw = f(lru_conv_w)[0]
    vfm = np.concatenate([fm(ffn1_pre_norm[0]), fm(mix_pre_norm[0]), fm(ffn2_pre_norm[0]),
                          fm(hgrn_lb_logits[0]), fm(hgrn_lb_logits[1]), fm(hgrn_out_norm[0]),
                          fm(cw[0]), fm(cw[1]), fm(cw[2]), fm(cw[3]), fm(lru_conv_b[0]), fm(lru_b_a[0]), fm(lru_b_x[0]),
                          fm(lru_lambda[0]), fm(lru_out_norm[0])], axis=1)
    vfm = np.ascontiguousarray(vfm)
    grow = np.ascontiguousarray(np.stack([np.broadcast_to(f(g[0])[None, :], (128, D))
                                          for g in (ffn1_post_norm, mix_post_norm, ffn2_post_norm, ffn1_pre_norm, mix_pre_norm, ffn2_pre_norm)]))
    shared = {"w1g": f(ffn1_w_gate[0]), "w1u": f(ffn1_w_up[0]), "w1d": f(ffn1_w_down[0]),
              "w2g": f(ffn2_w_gate[0]), "w2u": f(ffn2_w_up[0]), "w2d": f(ffn2_w_down[0]),
              "win": f(w_in[0]), "wout": f(w_out[0]), "wa": f(lru_w_a[0]), "wx": f(lru_w_x[0]),
              "vfm": vfm, "grow": grow, "ident": np.eye(128, dtype=np.float32),
              "tri": np.triu(np.ones((64, 64), np.float32))}
    x = f(x)
    meta = f(meta_tokens)
    maps = []
    for c in range(B * NSEG):
        b, s = divmod(c, NSEG)
        npad = (NSEG - 1 - s) * SEG
        xs = np.zeros((NT, D), np.float32)
        xs[npad:npad + NM] = meta
        xs[npad + NM:] = x[b, :(s + 1) * SEG]
        mask = np.zeros((128, NT), np.float32)
        mask[:, npad:] = 1.0
        m = dict(shared)
        m["xs"] = xs
        m["mask"] = mask
        maps.append(m)
    return maps


def run(cfg, inputs):
    nc = build(cfg)
    maps = make_inputs(cfg, **inputs)
    n = cfg["B"] * cfg["NSEG"]
    res = run_bass_kernel_spmd(nc, maps, core_ids=list(range(n)))
    SEG = cfg["SEQ"] // cfg["NSEG"]
    out = np.zeros((cfg["B"], cfg["SEQ"], cfg["D"]), np.float32)
    for c in range(n):
        b, s = divmod(c, cfg["NSEG"])
        out[b, s * SEG:(s + 1) * SEG] = res.results[c]["out"]
    return out


def kernel(**inputs):
    return run(CFG_FULL, inputs)
```
